# Optimizing a Trainium2 kernel written in Bass

```python
import math
import jax, jax.numpy as jnp
from jax import lax
import numpy as np

D_MODEL = 1024
BATCH = 8
SEQ = 2048
DEPTH = 1

N_MEM = 256
A_HEADS = 4
A_HEAD_DIM = 64
A_V_DIM = 2 * A_HEAD_DIM
B_GROUPS = ((128, 1), (512, 4), (2048, 16))
B_HEADS_PER_GROUP = 4
B_HEAD_DIM = 128
C_HEADS = 4
C_HEAD_DIM = 128
N_BRANCHES = 3
BRANCH_WIDTH = 512
D_FF = -(-8 * D_MODEL // (3 * 256)) * 256
Q_BLOCK = 128
EPS = 1e-6
ALIBI_MAX_BIAS = 8.0

A_QK_COLS = A_HEADS * 2 * A_HEAD_DIM
A_V_COLS = A_HEADS * A_V_DIM
B_COLS = len(B_GROUPS) * B_HEADS_PER_GROUP * B_HEAD_DIM
C_Q_COLS = C_HEADS * C_HEAD_DIM
IN_SIZES = (A_QK_COLS, A_QK_COLS, A_V_COLS, B_COLS, B_COLS, B_COLS, C_Q_COLS)
D_IN = sum(IN_SIZES)

kernel_name = 'gated_hybrid_diff_dilated_memory_block'


def rms_norm(x, g):
    xf = x.astype(jnp.float32)
    y = xf * lax.rsqrt(jnp.mean(xf * xf, axis=-1, keepdims=True) + EPS)
    return (y * g.astype(jnp.float32)).astype(x.dtype)


def alibi_slopes(n):
    return jnp.exp2(-ALIBI_MAX_BIAS * jnp.arange(1, n + 1, dtype=jnp.float32) / n)


def diff_attention(q, k, v, slopes, lam):
    b, t, h, _, dh = q.shape
    nq = t // Q_BLOCK
    scale = dh ** -0.5
    pos_k = jnp.arange(t, dtype=jnp.int32)
    q_blocks = q.reshape(b, nq, Q_BLOCK, h, 2, dh).transpose(1, 0, 2, 3, 4, 5)
    starts = jnp.arange(nq, dtype=jnp.int32) * Q_BLOCK

    def one_block(args):
        qb, start = args
        s = jnp.einsum('bqhcd,bkhcd->bhcqk', qb, k).astype(jnp.float32) * scale
        pos_q = start + jnp.arange(Q_BLOCK, dtype=jnp.int32)
        dist = jnp.abs(pos_q[:, None] - pos_k[None, :]).astype(jnp.float32)
        s = s - slopes[None, :, None, None, None] * dist[None, None, None]
        p = jax.nn.softmax(s, axis=-1)
        p_diff = p[:, :, 0] - lam * p[:, :, 1]
        return jnp.einsum('bhqk,bkhe->bqhe', p_diff.astype(v.dtype), v)

    o = lax.map(one_block, (q_blocks, starts))
    return o.transpose(1, 0, 2, 3, 4).reshape(b, t, h, v.shape[-1])


def dilated_window_attention(q, k, v, slopes, window, dilation):
    b, t, h, hd = q.shape
    n_side = window // (2 * dilation)
    sub_len = t // dilation
    blk = n_side
    nb = -(-sub_len // blk)
    pad = nb * blk - sub_len

    def to_sub(a):
        return a.reshape(b, sub_len, dilation, h, hd).transpose(0, 2, 3, 1, 4)

    qs = jnp.pad(to_sub(q), ((0, 0), (0, 0), (0, 0), (0, pad), (0, 0)))
    qs = qs.reshape(b, dilation, h, nb, blk, hd)

    def windows(a):
        ap = jnp.pad(to_sub(a), ((0, 0), (0, 0), (0, 0), (blk, blk + pad), (0, 0)))
        ap = ap.reshape(b, dilation, h, nb + 2, blk, hd)
        return jnp.concatenate([ap[:, :, :, :-2], ap[:, :, :, 1:-1], ap[:, :, :, 2:]], axis=4)

    kw, vw = windows(k), windows(v)
    s = jnp.einsum('brhnqd,brhnkd->brhnqk', qs, kw).astype(jnp.float32) * hd ** -0.5
    q_idx = jnp.arange(nb)[:, None, None] * blk + jnp.arange(blk)[None, :, None]
    k_idx = (jnp.arange(nb)[:, None, None] - 1) * blk + jnp.arange(3 * blk)[None, None, :]
    rel = k_idx - q_idx
    valid = (jnp.abs(rel) <= n_side) & (k_idx >= 0) & (k_idx < sub_len)
    dist = (dilation * jnp.abs(rel)).astype(jnp.float32)
    s = s - slopes[None, None, :, None, None, None] * dist
    s = jnp.where(valid, s, -jnp.inf)
    lse = jax.nn.logsumexp(s, axis=-1)
    p = jnp.exp(s - lse[..., None])
    o = jnp.einsum('brhnqk,brhnkd->brhnqd', p.astype(v.dtype), vw)
    o = o.reshape(b, dilation, h, nb * blk, hd)[:, :, :, :sub_len]
    lse = lse.reshape(b, dilation, h, nb * blk)[:, :, :, :sub_len]
    o = o.transpose(0, 3, 1, 2, 4).reshape(b, t, h, hd)
    lse = lse.transpose(0, 3, 1, 2).reshape(b, t, h)
    return o, lse


def memory_attention(q, k, v):
    s = jnp.einsum('bthd,bnhd->bhtn', q, k).astype(jnp.float32) * q.shape[-1] ** -0.5
    p = jax.nn.softmax(s, axis=-1)
    return jnp.einsum('bhtn,bnhd->bthd', p.astype(v.dtype), v)


def setup_inputs(seed: int = 0) -> dict:
    key = jax.random.key(seed)
    ks = jax.random.split(key, 32)
    L = DEPTH

    def nrm(k, shape, fan_in):
        return jax.random.normal(k, shape, jnp.float32) * fan_in ** -0.5

    def gain(k, shape):
        return 1.0 + 0.02 * jax.random.normal(k, shape, jnp.float32)

    def small(k, shape, scale):
        return scale * jax.random.normal(k, shape, jnp.float32)

    return {
        'x': jax.random.normal(ks[0], (BATCH, SEQ, D_MODEL), jnp.float32),
        'mem': jax.random.normal(ks[1], (BATCH, N_MEM, D_MODEL), jnp.float32),
        'norm_mix': gain(ks[2], (L, D_MODEL)),
        'w_in': nrm(ks[3], (L, D_MODEL, D_IN), D_MODEL),
        'w_gate': nrm(ks[4], (L, D_MODEL, N_BRANCHES * D_MODEL), D_MODEL),
        'b_gate': small(ks[5], (L, N_BRANCHES * D_MODEL), 0.02),
        'a_q_norm': gain(ks[6], (L, A_HEAD_DIM)),
        'a_k_norm': gain(ks[7], (L, A_HEAD_DIM)),
        'a_lambda_q1': small(ks[8], (L, A_HEAD_DIM), 0.1),
        'a_lambda_k1': small(ks[9], (L, A_HEAD_DIM), 0.1),
        'a_lambda_q2': small(ks[10], (L, A_HEAD_DIM), 0.1),
        'a_lambda_k2': small(ks[11], (L, A_HEAD_DIM), 0.1),
        'a_subln': gain(ks[12], (L, A_V_DIM)),
        'b_q_norm': gain(ks[13], (L, B_HEAD_DIM)),
        'b_k_norm': gain(ks[14], (L, B_HEAD_DIM)),
        'mem_norm': gain(ks[15], (L, D_MODEL)),
        'w_mem_kv': nrm(ks[16], (L, D_MODEL, 2 * C_HEADS * C_HEAD_DIM), D_MODEL),
        'c_q_norm': gain(ks[17], (L, C_HEAD_DIM)),
        'c_k_norm': gain(ks[18], (L, C_HEAD_DIM)),
        'w_branch': nrm(ks[19], (L, N_BRANCHES, BRANCH_WIDTH, D_MODEL), BRANCH_WIDTH),
        'w_out': nrm(ks[20], (L, D_MODEL, D_MODEL), D_MODEL),
        'norm_ffn': gain(ks[21], (L, D_MODEL)),
        'w_ffn_gate': nrm(ks[22], (L, D_MODEL, D_FF), D_MODEL),
        'w_ffn_up': nrm(ks[23], (L, D_MODEL, D_FF), D_MODEL),
        'w_ffn_down': nrm(ks[24], (L, D_FF, D_MODEL), D_FF),
    }


def reference(x, mem, norm_mix, w_in, w_gate, b_gate, a_q_norm, a_k_norm, a_lambda_q1, a_lambda_k1,
              a_lambda_q2, a_lambda_k2, a_subln, b_q_norm, b_k_norm, mem_norm, w_mem_kv, c_q_norm,
              c_k_norm, w_branch, w_out, norm_ffn, w_ffn_gate, w_ffn_up, w_ffn_down):
    b, t, d = x.shape
    n_mem = mem.shape[1]
    n_groups = len(B_GROUPS)
    offsets = [int(o) for o in np.cumsum(IN_SIZES)[:-1]]
    slopes_a = alibi_slopes(A_HEADS)
    slopes_b = alibi_slopes(n_groups * B_HEADS_PER_GROUP).reshape(n_groups, B_HEADS_PER_GROUP)
    for l in range(DEPTH):
        lambda_init = 0.8 - 0.6 * math.exp(-0.3 * l)
        h = rms_norm(x, norm_mix[l])
        aq, ak, av, bq, bk, bv, cq = jnp.split(h @ w_in[l], offsets, axis=-1)

        aq = rms_norm(aq.reshape(b, t, A_HEADS, 2, A_HEAD_DIM), a_q_norm[l])
        ak = rms_norm(ak.reshape(b, t, A_HEADS, 2, A_HEAD_DIM), a_k_norm[l])
        av = av.reshape(b, t, A_HEADS, A_V_DIM)
        lam = (jnp.exp(jnp.sum(a_lambda_q1[l] * a_lambda_k1[l]).astype(jnp.float32))
               - jnp.exp(jnp.sum(a_lambda_q2[l] * a_lambda_k2[l]).astype(jnp.float32)) + lambda_init)
        out_a = diff_attention(aq, ak, av, slopes_a, lam)
        out_a = (rms_norm(out_a, a_subln[l]) * (1.0 - lambda_init)).reshape(b, t, BRANCH_WIDTH)

        bq = rms_norm(bq.reshape(b, t, n_groups, B_HEADS_PER_GROUP, B_HEAD_DIM), b_q_norm[l])
        bk = rms_norm(bk.reshape(b, t, n_groups, B_HEADS_PER_GROUP, B_HEAD_DIM), b_k_norm[l])
        bv = bv.reshape(b, t, n_groups, B_HEADS_PER_GROUP, B_HEAD_DIM)
        outs, lses = [], []
        for g, (window, dilation) in enumerate(B_GROUPS):
            o, lse = dilated_window_attention(bq[:, :, g], bk[:, :, g], bv[:, :, g], slopes_b[g], window, dilation)
            outs.append(o)
            lses.append(lse)
        wts = jax.nn.softmax(jnp.stack(lses, axis=2), axis=2)
        out_b = jnp.sum(wts[..., None].astype(bv.dtype) * jnp.stack(outs, axis=2), axis=2)
        out_b = out_b.reshape(b, t, BRANCH_WIDTH)

        m = rms_norm(mem, mem_norm[l])
        kv = (m @ w_mem_kv[l]).reshape(b, n_mem, 2, C_HEADS, C_HEAD_DIM)
        ck = rms_norm(kv[:, :, 0], c_k_norm[l])
        cv = kv[:, :, 1]
        cq = rms_norm(cq.reshape(b, t, C_HEADS, C_HEAD_DIM), c_q_norm[l])
        out_c = memory_attention(cq, ck, cv).reshape(b, t, BRANCH_WIDTH)

        branches = jnp.einsum('btgc,gcd->btgd', jnp.stack([out_a, out_b, out_c], axis=2), w_branch[l])
        gates = jax.nn.sigmoid(h @ w_gate[l] + b_gate[l]).reshape(b, t, N_BRANCHES, d)
        x = x + jnp.sum(gates * branches, axis=2) @ w_out[l]

        h2 = rms_norm(x, norm_ffn[l])
        x = x + (jax.nn.silu(h2 @ w_ffn_gate[l]) * (h2 @ w_ffn_up[l])) @ w_ffn_down[l]
    return x
```

```python
import contextlib
import numpy as np
import ml_dtypes
import concourse.bass as bass
import concourse.mybir as mybir
from concourse.bass_utils import run_bass_kernel_spmd

F32 = mybir.dt.float32
BF16 = mybir.dt.bfloat16
AF = mybir.ActivationFunctionType
ALU = mybir.AluOpType

SAME_ENGINE_SYNC = True
DEBUG = False
BASE = 16512
T = 2048
EPS = 1e-6
D_IN = 6656
LAMBDA_INIT = 0.2
BIG = 1.0e5


class Tk:
    __slots__ = ("w", "rs")

    def __init__(self):
        self.w = None
        self.rs = []


class Prog:
    ENG = ("pe", "act", "dve", "pool", "sp")

    def __init__(self, nc):
        self.nc = nc
        self.ops = {e: [] for e in self.ENG}
        self.dma_cnt = {}

    def _deps(self, r, w):
        deps = set()
        for t in r:
            if t.w is not None:
                deps.add(t.w)
        for t in w:
            if t.w is not None:
                deps.add(t.w)
            deps.update(t.rs)
        return deps

    def _commit(self, me, r, w):
        for t in r:
            t.rs.append(me)
        for t in w:
            t.w = me
            t.rs = []

    def _mark(self, deps, eng):
        for d in deps:
            if d[0] == "e":
                if d[1] == eng and (eng in ("pe", "sp") or not SAME_ENGINE_SYNC):
                    continue
                self.ops[d[1]][d[2]]["waited"] = True

    def op(self, eng, fn, r=(), w=()):
        deps = self._deps(r, w)
        me = ("e", eng, len(self.ops[eng]))
        self.ops[eng].append(dict(fn=fn, deps=deps, waited=False, dma=None))
        self._mark(deps, eng)
        self._commit(me, r, w)
        return me

    def dma(self, eng, fn, sem, r=(), w=()):
        deps = self._deps(r, w)
        n = self.dma_cnt.get(sem, 0) + 1
        self.dma_cnt[sem] = n
        me = ("d", sem, 16 * n)
        self.ops[eng].append(dict(fn=fn, deps=deps, waited=False, dma=sem))
        self._mark(deps, eng)
        self._commit(me, r, w)
        return me

    def wait_only(self, eng, deps):
        deps = set(deps)
        self.ops[eng].append(dict(fn=None, deps=deps, waited=False, dma=None))
        self._mark(deps, eng)

    def barrier(self, extra=()):
        evs = set(extra)
        for e in ("pe", "act", "dve", "pool"):
            for i in range(len(self.ops[e]) - 1, -1, -1):
                rec = self.ops[e][i]
                if rec["fn"] is not None and rec["dma"] is None:
                    evs.add(("e", e, i))
                    break
        for e in self.ENG:
            self.wait_only(e, evs)

    def emit(self):
        nc = self.nc
        for e in self.ENG:
            c = 0
            for rec in self.ops[e]:
                if rec["dma"] is None and rec["waited"] and rec["fn"] is not None:
                    c += 1
                rec["val"] = c
        with contextlib.ExitStack() as st:
            esem = {e: st.enter_context(nc.semaphore("s_" + e)) for e in self.ENG}
            dsem = {k: st.enter_context(nc.semaphore("d_%s" % (k,))) for k in self.dma_cnt}
            block = st.enter_context(nc.Block())
            ops = self.ops

            def run(e, engobj):
                waited = {}
                for rec in ops[e]:
                    need = {}
                    for d in rec["deps"]:
                        if d[0] == "e":
                            if d[1] == e and (e in ("pe", "sp") or not SAME_ENGINE_SYNC):
                                continue
                            key = ("e", d[1])
                            val = ops[d[1]][d[2]]["val"]
                            sem = esem[d[1]]
                        else:
                            key = ("d", d[1])
                            val = d[2]
                            sem = dsem[d[1]]
                        if waited.get(key, 0) >= val:
                            continue
                        if need.get(key, (None, 0))[1] < val:
                            need[key] = (sem, val)
                    for key, (sem, val) in need.items():
                        engobj.wait_ge(sem, val)
                        waited[key] = val
                    if rec["fn"] is None:
                        continue
                    ins = rec["fn"](engobj)
                    if rec["dma"] is not None:
                        ins.then_inc(dsem[rec["dma"]], 16)
                    elif rec["waited"]:
                        ins.then_inc(esem[e], 1)

            @block.tensor
            def _(eng):
                run("pe", eng)

            @block.scalar
            def _(eng):
                run("act", eng)

            @block.vector
            def _(eng):
                run("dve", eng)

            @block.gpsimd
            def _(eng):
                run("pool", eng)

            @block.sync
            def _(eng):
                run("sp", eng)


def interleave(gens):
    live = [[g, 0, max(1, n)] for g, n in gens]
    while live:
        live.sort(key=lambda x: x[1] / x[2])
        g = live[0]
        try:
            next(g[0])
            g[1] += 1
        except StopIteration:
            live.pop(0)


class Ring:
    def __init__(self, P, slots, jobs, name, tks=None, init=None):
        self.P, self.slots, self.jobs, self.name = P, slots, jobs, name
        self.tk = tks if tks is not None else [Tk() for _ in slots]
        self.issued = 0
        for _ in range(len(slots) if init is None else init):
            self.advance()

    def advance(self):
        j = self.issued
        if j >= len(self.jobs):
            return
        s = j % len(self.slots)
        for outf, in_ap in self.jobs[j]:
            out_ap = outf(self.slots[s])
            self.P.dma("pool", (lambda e, o=out_ap, i=in_ap: e.dma_start(out=o, in_=i)),
                       "%s%d" % (self.name, s), w=[self.tk[s]])
        self.issued += 1

    def get(self, j):
        assert j < self.issued, (self.name, j, self.issued)
        s = j % len(self.slots)
        return self.slots[s], self.tk[s]


def build():
    nc = bass.Bass("TRN2", target_bir_lowering=False)
    P = Prog(nc)

    def din(name, shape, dtype=F32):
        return nc.dram_tensor(name, shape, dtype, kind="ExternalInput").ap()

    xT = din("xT", [1024, T])
    memT = din("memT", [1024, 256])
    w_in = din("w_in", [1024, D_IN])
    w_gate = din("w_gate", [1024, 3072])
    w_mem = din("w_mem", [1024, 1024])
    w_br = din("w_br", [1536, 1024])
    w_out = din("w_out", [1024, 1024])
    w_fg = din("w_fg", [1024, 2816])
    w_fu = din("w_fu", [1024, 2816])
    w_fd = din("w_fd", [2816, 1024])
    cols_d = din("cols", [128, 64])
    lamv_d = din("lamv", [128, 256])
    ra_d = din("c_ra", [128, 3968])
    rb_d = din("c_rb", [128, 896])
    yT = nc.dram_tensor("yT", [1024, T], F32, kind="ExternalOutput").ap()
    if DEBUG:
        dbg_h = nc.dram_tensor("dbg_h", [128, 8, T], BF16, kind="ExternalOutput").ap()
        dbg_o = nc.dram_tensor("dbg_o", [128, 12, T], BF16, kind="ExternalOutput").ap()
        dbg_x1 = nc.dram_tensor("dbg_x1", [128, 8, T], F32, kind="ExternalOutput").ap()

    def sb(name, shape, dtype, off):
        return nc.alloc_sbuf_tensor_at(name, shape, dtype, offset=BASE + off)

    ps = [nc.alloc_psum_tensor("ps%d" % i, [128, 512], F32) for i in range(8)]
    t_ps = [Tk() for _ in range(8)]

    def mm(out, lhsT, rhs, start, stop, r, w):
        P.op("pe", lambda e: e.matmul(out, lhsT=lhsT, rhs=rhs, start=start, stop=stop), r=r, w=w)

    def act(out, in_, func, r, w, bias=None, scale=None):
        kw = {}
        if bias is not None:
            kw["bias"] = bias
        if scale is not None:
            kw["scale"] = scale
        P.op("act", lambda e: e.activation(out=out, in_=in_, func=func, **kw), r=r, w=w)

    def stt(eng, out, in0, scalar, in1, op0, op1, r, w):
        P.op(eng, lambda e: e.scalar_tensor_tensor(out=out, in0=in0, scalar=scalar, in1=in1, op0=op0, op1=op1),
             r=r, w=w)

    def tt(eng, out, in0, in1, op, r, w):
        P.op(eng, lambda e: e.tensor_tensor(out=out, in0=in0, in1=in1, op=op), r=r, w=w)

    def recip(out, in_, r, w):
        P.op("dve", lambda e: e.reciprocal(out=out, in_=in_), r=r, w=w)

    def recip_act(out, in_, r, w):
        act(out, in_, AF.Ln, r=r, w=w)
        act(out, out, AF.Exp, r=w, w=w, scale=-1.0)

    def tsmul(eng, out, in0, s, r, w):
        P.op(eng, lambda e: e.tensor_scalar_mul(out=out, in0=in0, scalar1=s), r=r, w=w)

    def dma_sp(out, in_, sem, r=(), w=()):
        return P.dma("sp", lambda e: e.dma_start(out=out, in_=in_), sem, r=r, w=w)

    ones128 = sb("ones128", [128, 128], BF16, 0)
    ones1024 = sb("ones1024", [128, 128], BF16, 256)
    onesblk = sb("onesblk", [128, 128], BF16, 512)
    ones1 = sb("ones1", [128, 128], BF16, 768)
    cols = sb("colsb", [128, 64], F32, 1024)
    lamv = sb("lamvb", [128, 256], F32, 1280)
    rb = sb("rbb", [128, 896], F32, 2304)
    ckT = sb("ckT", [128, 4, 256], BF16, 5888)
    cv = sb("cv", [128, 2, 512], BF16, 7936)
    t_const = Tk()
    t_cols = Tk()
    t_rb = Tk()
    t_ck = [Tk() for _ in range(4)]
    t_cv = [Tk() for _ in range(2)]

    P.op("pool", lambda e: e.memset(ones128[:], 1.0 / 128), w=[t_const])
    P.op("pool", lambda e: e.memset(ones1024[:], 1.0 / 1024), w=[t_const])
    P.op("pool", lambda e: e.memset(onesblk[:], 0.0), w=[t_const])
    P.op("pool", lambda e: e.memset(onesblk[0:64, 0:64], 1.0 / 64), w=[t_const])
    P.op("pool", lambda e: e.memset(onesblk[64:128, 64:128], 1.0 / 64), w=[t_const])
    P.op("pool", lambda e: e.memset(ones1[:], 1.0), w=[t_const])
    dma_sp(cols[:], cols_d[:, :], "c0", w=[t_cols])
    dma_sp(lamv[:], lamv_d[:, :], "c1", w=[t_cols])
    dma_sp(rb[:], rb_d[:, :], "c2", w=[t_rb])

    stt("dve", cols[:, 56:57], cols[:, 48:49], 64.0 ** -0.5, cols[:, 49:50], ALU.mult, ALU.mult, r=[t_cols], w=[t_cols])
    stt("dve", cols[:, 57:58], cols[:, 50:51], 128.0 ** -0.5, cols[:, 51:52], ALU.mult, ALU.mult, r=[t_cols], w=[t_cols])
    stt("dve", cols[:, 58:59], cols[:, 52:53], 128.0 ** -0.5, cols[:, 53:54], ALU.mult, ALU.mult, r=[t_cols], w=[t_cols])
    tsmul("dve", cols[:, 59:60], cols[:, 54:55], 1.0 - LAMBDA_INIT, r=[t_cols], w=[t_cols])
    tt("dve", lamv[:, 0:64], lamv[:, 0:64], lamv[:, 64:128], ALU.mult, r=[t_cols], w=[t_cols])
    tt("dve", lamv[:, 128:192], lamv[:, 128:192], lamv[:, 192:256], ALU.mult, r=[t_cols], w=[t_cols])
    P.op("dve", lambda e: e.reduce_sum(out=cols[:, 61:62], in_=lamv[:, 0:64], axis=mybir.AxisListType.X), r=[t_cols], w=[t_cols])
    P.op("dve", lambda e: e.reduce_sum(out=cols[:, 62:63], in_=lamv[:, 128:192], axis=mybir.AxisListType.X), r=[t_cols], w=[t_cols])
    act(cols[:, 61:63], cols[:, 61:63], AF.Exp, r=[t_cols], w=[t_cols])
    stt("dve", cols[:, 60:61], cols[:, 62:63], -LAMBDA_INIT, cols[:, 61:62], ALU.add, ALU.subtract, r=[t_cols], w=[t_cols])

    hT = sb("hT", [128, 8, T], BF16, 10240)
    outT = sb("outT", [128, 12, T], BF16, 43008)
    ra = sb("ra", [128, 3968], F32, 92160)
    W1_OFF = 108032
    NW1 = 10
    w1slots = [sb("w1_%d" % i, [128, 8, 128], BF16, W1_OFF + 2048 * i) for i in range(NW1)]
    U_OFF = W1_OFF + 2048 * NW1
    USZ = 13312
    uq = [sb("uq%d" % i, [128, T], BF16, U_OFF + USZ * i) for i in range(3)]
    uk = [sb("uk%d" % i, [128, T], BF16, U_OFF + USZ * i + 4096) for i in range(3)]
    uv = [sb("uv%d" % i, [128, 20, 128], BF16, U_OFF + USZ * i + 8192) for i in range(3)]
    ACC_OFF = U_OFF + 3 * USZ
    Oacc = sb("Oacc", [128, T], F32, ACC_OFF)
    Lacc = sb("Lacc", [128, T], F32, ACC_OFF + 8192)
    TMP = ACC_OFF + 16384
    sq = [sb("sq%d" % i, [128, 512], BF16, TMP + 1024 * i) for i in range(2)]
    rstd = [sb("rstd%d" % i, [128, 512], F32, TMP + 2048 + 2048 * i) for i in range(2)]
    std = sb("std", [128, 512], F32, TMP + 6144)
    St = [sb("St%d" % i, [128, 512], F32, TMP + 8192 + 2048 * i) for i in range(2)]
    Pt = [sb("Pt%d" % i, [128, 512], BF16, TMP + 12288 + 1024 * i) for i in range(3)]
    fin = [sb("fin%d" % i, [128, 512], F32, TMP + 15360 + 2048 * i) for i in range(3)]
    sqA = sb("sqA", [128, 512], BF16, TMP + 15360 + 2048)
    rawg = [sb("rawg%d" % i, [128, 512], F32, TMP + 21504 + 2048 * i) for i in range(2)]
    t_rawg = [Tk(), Tk()]
    St.append(sb("St2", [128, 512], F32, TMP + 25600))
    assert TMP + 25600 + 2048 <= 212864
    xs = [sb("xs%d" % i, [128, 8, 512], F32, U_OFF + 16384 * i) for i in range(2)]
    ms = sb("ms", [128, 8, 256], F32, U_OFF + 32768)
    mT = sb("mT", [128, 8, 256], BF16, U_OFF + 40960)

    t_h = [[Tk() for _ in range(4)] for _ in range(8)]
    t_o = [[Tk() for _ in range(4)] for _ in range(12)]
    t_ra = Tk()
    t_uq = [[Tk() for _ in range(4)] for _ in range(3)]
    t_uk = [[Tk() for _ in range(4)] for _ in range(3)]
    t_uv = [[Tk() for _ in range(5)] for _ in range(3)]
    t_Oacc, t_Lacc = Tk(), Tk()
    t_sq = [Tk(), Tk()]
    t_rstd = [Tk(), Tk()]
    t_std = Tk()
    t_St = [Tk(), Tk(), Tk()]
    t_Pt = [Tk(), Tk(), Tk()]
    t_fin = [Tk(), Tk(), Tk()]
    t_xs = [Tk(), Tk()]
    t_ms, t_mT = Tk(), [Tk() for _ in range(8)]

    rr = {"sq": 0, "rstd": 0, "St": 0, "Pt": 0, "pb": 0, "sbk": 0, "rawg": 0, "St3": 0}

    def nxt(key, n):
        v = rr[key]
        rr[key] = (v + 1) % n
        return v

    dma_sp(ra[:], ra_d[:, :], "c3", w=[t_ra])

    def wsl(w_ap, col0, ncol=128):
        src = w_ap.rearrange("(c p) n -> p c n", p=128)[:, :, col0:col0 + ncol]
        return [((lambda s: s[:]), src)]

    jobs1 = []
    for h in range(4):
        jobs1.append(wsl(w_mem, h * 128))
    for h in range(4):
        jobs1.append(wsl(w_mem, 512 + h * 128))
    units = []
    for h in range(4):
        units.append(("C", h, 0))
    for h in range(4):
        units.append(("A", h, 0))
    for h in range(4):
        for g in range(3):
            units.append(("B", h, g))
    ujob = {}
    for u in units:
        kind, h, g = u
        ujob[u] = len(jobs1)
        if kind == "C":
            jobs1.append(wsl(w_in, 6144 + h * 128))
        elif kind == "A":
            jobs1.append(wsl(w_in, 0 + h * 128))
            jobs1.append(wsl(w_in, 512 + h * 128))
            jobs1.append(wsl(w_in, 1024 + h * 128))
        else:
            jobs1.append(wsl(w_in, 1536 + g * 512 + h * 128))
            jobs1.append(wsl(w_in, 3072 + g * 512 + h * 128))
            jobs1.append(wsl(w_in, 4608 + g * 512 + h * 128))
    ring1 = Ring(P, w1slots, jobs1, "w1")

    def rms_in(src_d, stage, t_stage, sem, ncol, gcol0, dst_fn, t_dst_fn, col_lo):
        dma_sp(stage[:, :, 0:ncol], src_d.rearrange("(c p) t -> p c t", p=128)[:, :, col_lo:col_lo + ncol], sem, w=[t_stage])
        for c in range(8):
            b = nxt("sq", 2)
            act(sq[b][:, 0:ncol], stage[:, c, 0:ncol], AF.Square, r=[t_stage], w=[t_sq[b]])
            mm(ps[2][:, 0:ncol], ones1024[:], sq[b][:, 0:ncol], c == 0, c == 7, r=[t_sq[b], t_const], w=[t_ps[2]])
        act(std[:, 0:ncol], ps[2][:, 0:ncol], AF.Ln, r=[t_ps[2]], w=[t_std], bias=EPS)
        b = nxt("rstd", 2)
        act(rstd[b][:, 0:ncol], std[:, 0:ncol], AF.Exp, r=[t_std], w=[t_rstd[b]], scale=-0.5)
        for c in range(8):
            eng = "dve"
            stt(eng, dst_fn(c), stage[:, c, 0:ncol], cols[:, gcol0 + c:gcol0 + c + 1], rstd[b][:, 0:ncol],
                ALU.mult, ALU.mult, r=[t_stage, t_cols, t_rstd[b]], w=[t_dst_fn(c)])

    for qd in range(4):
        s = qd % 2
        rms_in(xT, xs[s], t_xs[s], "xs%d" % s, 512, 0,
               (lambda c, qd=qd: hT[:, c, qd * 512:(qd + 1) * 512]), (lambda c, qd=qd: t_h[c][qd]), qd * 512)
    rms_in(memT, ms, t_ms, "ms", 256, 16, (lambda c: mT[:, c, :]), (lambda c: t_mT[c]), 0)

    if DEBUG:
        dma_sp(dbg_h[:, :, :], hT[:], "dbg0", r=[t for row in t_h for t in row])

    SSB = 1
    FINAL_ENG = "pool"
    FINAL_ENG_D = "pool"

    def norm_sq(psb, ncol):
        b = nxt("sq", 2)
        act(sq[b][:, 0:ncol], ps[psb][:, 0:ncol], AF.Square, r=[t_ps[psb]], w=[t_sq[b]])
        return b

    def norm_rest(b, psb, ncol, ones_ap, gcol, out_ap, in_view, w_tk):
        mm(ps[SSB][:, 0:ncol], ones_ap, sq[b][:, 0:ncol], True, True, r=[t_sq[b], t_const], w=[t_ps[SSB]])
        act(std[:, 0:ncol], ps[SSB][:, 0:ncol], AF.Ln, r=[t_ps[SSB]], w=[t_std], bias=EPS)
        rb_ = nxt("rstd", 2)
        act(rstd[rb_][:, 0:ncol], std[:, 0:ncol], AF.Exp, r=[t_std], w=[t_rstd[rb_]], scale=-0.5)
        if gcol is None:
            tt("dve", out_ap, in_view(ps[psb][:, 0:ncol]), in_view(rstd[rb_][:, 0:ncol]), ALU.mult,
               r=[t_ps[psb], t_rstd[rb_]], w=w_tk)
        else:
            stt("dve", out_ap, in_view(ps[psb][:, 0:ncol]), gcol, in_view(rstd[rb_][:, 0:ncol]), ALU.mult, ALU.mult,
                r=[t_ps[psb], t_rstd[rb_], t_cols], w=w_tk)

    def norm_tail(psb, ncol, ones_ap, gcol, out_ap, in_view, r_extra, w_tk):
        b = norm_sq(psb, ncol)
        norm_rest(b, psb, ncol, ones_ap, gcol, out_ap, in_view, w_tk)

    def proj_qk(wj, dst, t_dst, ones_ap, gcol, Dil, rawbanks, ssb):
        wslot, wtk = ring1.get(wj)
        n = 512 // Dil

        def pv(a):
            return a if Dil == 1 else a.rearrange("p (i r) -> p r i", r=Dil)

        def cv_(a):
            return a if Dil == 1 else a.rearrange("p (r i) -> p r i", r=Dil)

        def MM(qd):
            pb = rawbanks[qd % len(rawbanks)]
            for c in range(8):
                mm(ps[pb][:], wslot[:, c, :], hT[:, c, qd * 512:(qd + 1) * 512], c == 0, c == 7,
                   r=[wtk, t_h[c][qd]], w=[t_ps[pb]])
            if qd == 3:
                ring1.advance()

        def EV(qd):
            pb = rawbanks[qd % len(rawbanks)]
            b = norm_sq(pb, 512)
            rg = nxt("rawg", 2)
            if gcol is None:
                P.op("dve", (lambda e, rg=rg: e.tensor_copy(out=cv_(rawg[rg][:]), in_=pv(ps[pb][:]))),
                     r=[t_ps[pb], t_sq[b]], w=[t_rawg[rg]])
            else:
                tsmul("dve", cv_(rawg[rg][:]), pv(ps[pb][:]), gcol, r=[t_ps[pb], t_cols, t_sq[b]], w=[t_rawg[rg]])
            return b, rg

        def REST(qd, b, rg):
            mm(ps[ssb][:], ones_ap, sq[b][:], True, True, r=[t_sq[b], t_const], w=[t_ps[ssb]])
            act(std[:], ps[ssb][:], AF.Ln, r=[t_ps[ssb]], w=[t_std], bias=EPS)
            rb_ = nxt("rstd", 2)
            act(cv_(rstd[rb_][:]), pv(std[:]), AF.Exp, r=[t_std], w=[t_rstd[rb_]], scale=-0.5)
            if Dil == 1:
                out_ap = dst[:, qd * 512:(qd + 1) * 512]
            else:
                out_ap = dst[:, :].rearrange("p (r i) -> p r i", r=Dil)[:, :, qd * n:(qd + 1) * n]
            tt(FINAL_ENG if Dil == 1 else FINAL_ENG_D, out_ap, cv_(rawg[rg][:]), cv_(rstd[rb_][:]), ALU.mult,
               r=[t_rawg[rg], t_rstd[rb_]], w=[t_dst[qd]])

        if len(rawbanks) == 1:
            for qd in range(4):
                MM(qd)
                yield
                b, rg = EV(qd)
                yield
                REST(qd, b, rg)
            yield
        else:
            MM(0)
            yield
            MM(1)
            yield
            e0 = EV(0)
            yield
            MM(2)
            yield
            e1 = EV(1)
            REST(0, *e0)
            yield
            MM(3)
            yield
            e2 = EV(2)
            REST(1, *e1)
            yield
            e3 = EV(3)
            REST(2, *e2)
            yield
            REST(3, *e3)
            yield

    def proj_v(wj, vbuf, t_vb, tiles, vbanks):
        wslot, wtk = ring1.get(wj)
        ngrp = (len(tiles) + 3) // 4

        def MMG(gi):
            g0 = gi * 4
            grp = tiles[g0:g0 + 4]
            vb = vbanks[gi % len(vbanks)]
            for j, (start, step) in enumerate(grp):
                hi = start + 127 * step
                qs = range(start // 512, hi // 512 + 1)
                for c in range(8):
                    mm(ps[vb][:, j * 128:(j + 1) * 128], hT[:, c, start:hi + 1:step], wslot[:, c, :], c == 0, c == 7,
                       r=[wtk] + [t_h[c][q] for q in qs], w=[t_ps[vb]])
            if gi == ngrp - 1:
                ring1.advance()

        def CP(gi):
            g0 = gi * 4
            n = len(tiles[g0:g0 + 4])
            vb = vbanks[gi % len(vbanks)]
            act(vbuf[:, g0:g0 + n, :], ps[vb][:, 0:n * 128].rearrange("p (a b) -> p a b", b=128), AF.Copy,
                r=[t_ps[vb]], w=[t_vb[gi]])

        if len(vbanks) == 1:
            for gi in range(ngrp):
                MMG(gi)
                yield
                CP(gi)
                yield
        else:
            MMG(0)
            yield
            for gi in range(1, ngrp):
                MMG(gi)
                CP(gi - 1)
                yield
            CP(ngrp - 1)
            yield

    def kv_unit():
        for h in range(4):
            wslot, wtk = ring1.get(h)
            for c in range(8):
                mm(ps[0][:, 0:256], wslot[:, c, :], mT[:, c, :], c == 0, c == 7, r=[wtk, t_mT[c]], w=[t_ps[0]])
            ring1.advance()
            norm_tail(0, 256, ones128[:], None, ckT[:, h, :], (lambda a: a), [], [t_ck[h]])
        for kt in range(2):
            for h in range(4):
                wslot, wtk = ring1.get(4 + h)
                for c in range(8):
                    mm(ps[3][:, h * 128:(h + 1) * 128], mT[:, c, kt * 128:(kt + 1) * 128], wslot[:, c, :], c == 0, c == 7,
                       r=[wtk, t_mT[c]], w=[t_ps[3]])
                if kt == 1:
                    ring1.advance()
            act(cv[:, kt, :], ps[3][:], AF.Copy, r=[t_ps[3]], w=[t_cv[kt]])

    kv_unit()
    P.barrier()

    SLOPE_A = [2.0 ** (-2.0 * (h + 1)) for h in range(4)]
    SLOPE_B = [[2.0 ** (-8.0 * (4 * g + h + 1) / 12.0) for h in range(4)] for g in range(3)]
    DILS = [1, 4, 16]

    def b_geom(g):
        Dil = DILS[g]
        Ls = T // Dil
        nq = Ls // 128
        ntile = nq + 1 if nq > 1 else 1

        def ws(ct):
            if nq == 1 or ct == 0:
                return 0
            if ct == nq:
                return Ls - 128
            return 128 * ct - 64
        return Dil, Ls, nq, ntile, ws

    def gen_proj(u, slot, mode):
        kind, h, g = u
        j0 = ujob[u]
        rawbanks, ssb = ([0], 1) if mode == "A" else ([0, 1], 2)
        vbanks = rawbanks
        if kind == "C":
            yield from proj_qk(j0, uq[slot], t_uq[slot], ones128[:], cols[:, 58:59], 1, rawbanks, ssb)
        elif kind == "A":
            yield from proj_qk(j0, uq[slot], t_uq[slot], onesblk[:], cols[:, 56:57], 1, rawbanks, ssb)
            yield from proj_qk(j0 + 1, uk[slot], t_uk[slot], onesblk[:], None, 1, rawbanks, ssb)
            yield from proj_v(j0 + 2, uv[slot], t_uv[slot], [(kt * 128, 1) for kt in range(16)], vbanks)
        else:
            Dil, Ls, nq, ntile, ws = b_geom(g)
            yield from proj_qk(j0, uq[slot], t_uq[slot], ones128[:], cols[:, 57:58], Dil, rawbanks, ssb)
            yield from proj_qk(j0 + 1, uk[slot], t_uk[slot], ones128[:], None, Dil, rawbanks, ssb)
            tiles = [(Dil * ws(ct) + r, Dil) for r in range(Dil) for ct in range(ntile)]
            yield from proj_v(j0 + 2, uv[slot], t_uv[slot], tiles, vbanks)

    def nproj(u):
        kind, h, g = u
        if kind == "C":
            return 9
        if kind == "A":
            return 9 + 9 + 8
        _, _, _, ntile, _ = b_geom(g)
        return 18 + 1 + ((DILS[g] * ntile + 3) // 4)

    LA = 2
    t_s3 = [Tk(), Tk()]

    def gen_attn_C(h, slot):
        q, tq = uq[slot], t_uq[slot]
        tiles = [(qd, kt) for qd in range(4) for kt in range(2)]
        deferred = []

        def S(i):
            qd, kt = tiles[i]
            sbk = 2 + i % 2
            mm(ps[sbk][:], ckT[:, h, kt * 128:(kt + 1) * 128], q[:, qd * 512:(qd + 1) * 512], True, True,
               r=[t_ck[h], tq[qd]], w=[t_ps[sbk]])

        def fin_c(qd, ob, lb):
            def f():
                recip_act(fin[0][:], ps[lb][:], r=[t_ps[lb]], w=[t_fin[0]])
                tt("dve", outT[:, 8 + h, qd * 512:(qd + 1) * 512], ps[ob][:], fin[0][:], ALU.mult,
                   r=[t_ps[ob], t_fin[0]], w=[t_o[8 + h][qd]])
            return f

        for i in range(LA):
            S(i)
        for j, (qd, kt) in enumerate(tiles):
            sbk = 2 + j % 2
            ob, lb = 4 + qd % 2, 6 + qd % 2
            pj = nxt("Pt", 3)
            act(Pt[pj][:], ps[sbk][:], AF.Exp, r=[t_ps[sbk]], w=[t_Pt[pj]])
            if j + LA < len(tiles):
                S(j + LA)
            mm(ps[ob][:], cv[:, kt, h * 128:(h + 1) * 128], Pt[pj][:], kt == 0, kt == 1, r=[t_cv[kt], t_Pt[pj]], w=[t_ps[ob]])
            mm(ps[lb][:], ones1[:], Pt[pj][:], kt == 0, kt == 1, r=[t_const, t_Pt[pj]], w=[t_ps[lb]])
            if kt == 0 and deferred:
                deferred.pop(0)()
            if kt == 1:
                deferred.append(fin_c(qd, ob, lb))
            yield
        while deferred:
            deferred.pop(0)()
        yield

    def gen_attn_A(h, slot):
        q, tq, k, tk_, v, tv = uq[slot], t_uq[slot], uk[slot], t_uk[slot], uv[slot], t_uv[slot]
        tiles = [(qd, cmp, kt) for qd in range(4) for cmp in range(2) for kt in range(16)]
        deferred = []

        def S(i):
            qd, cmp, kt = tiles[i]
            lo, hi = cmp * 64, cmp * 64 + 64
            sbk = 2 + i % 2
            mm(ps[sbk][:], k[lo:hi, kt * 128:(kt + 1) * 128], q[lo:hi, qd * 512:(qd + 1) * 512], True, True,
               r=[tk_[kt // 4], tq[qd]], w=[t_ps[sbk]])

        def fin_a(qd, cmp, ob, lb):
            st = {}

            def f1():
                act(fin[0][:], ps[lb][:], AF.Ln, r=[t_ps[lb]], w=[t_fin[0]])
                fo = 1 if cmp == 0 else 2
                P.op("dve", (lambda e: e.tensor_copy(out=fin[fo][:], in_=ps[ob][:])), r=[t_ps[ob]], w=[t_fin[fo]])
                act(fin[0][:], fin[0][:], AF.Exp, r=[t_fin[0]], w=[t_fin[0]], scale=-1.0)
                tt("pool", fin[fo][:], fin[fo][:], fin[0][:], ALU.mult, r=[t_fin[0]], w=[t_fin[fo]])

            def f2():
                stt("dve", fin[2][:], fin[2][:], cols[:, 60:61], fin[1][:], ALU.mult, ALU.add,
                    r=[t_fin[1], t_cols], w=[t_fin[2]])
                act(sqA[:], fin[2][:], AF.Square, r=[t_fin[2]], w=[t_fin[1]])

            def f3():
                mm(ps[SSB][:], ones128[:], sqA[:], True, True, r=[t_fin[1], t_const], w=[t_ps[SSB]])
                act(std[:], ps[SSB][:], AF.Ln, r=[t_ps[SSB]], w=[t_std], bias=EPS)
                act(fin[0][:], std[:], AF.Exp, r=[t_std], w=[t_fin[0]], scale=-0.5)

            def f4():
                stt("dve", outT[:, h, qd * 512:(qd + 1) * 512], fin[2][:], cols[:, 59:60], fin[0][:], ALU.mult, ALU.mult,
                    r=[t_fin[2], t_cols, t_fin[0]], w=[t_o[h][qd]])
            return [f1] if cmp == 0 else [f1, f2, f3, f4]

        for i in range(LA):
            S(i)
        for j, (qd, cmp, kt) in enumerate(tiles):
            sbk = 2 + j % 2
            gi = qd * 2 + cmp
            ob, lb = 4 + gi % 2, 6 + gi % 2
            off = qd * 512 - kt * 128 + 1920
            si = nxt("St3", 3)
            stt("dve", St[si][:], ra[:, off:off + 512], -SLOPE_A[h], ps[sbk][:], ALU.mult, ALU.add,
                r=[t_ra, t_ps[sbk]], w=[t_St[si]])
            pj = nxt("Pt", 3)
            act(Pt[pj][:], St[si][:], AF.Exp, r=[t_St[si]], w=[t_Pt[pj]])
            if j + LA < len(tiles):
                S(j + LA)
            mm(ps[ob][:], v[:, kt, :], Pt[pj][:], kt == 0, kt == 15, r=[tv[kt // 4], t_Pt[pj]], w=[t_ps[ob]])
            mm(ps[lb][:], ones1[:], Pt[pj][:], kt == 0, kt == 15, r=[t_const, t_Pt[pj]], w=[t_ps[lb]])
            if kt in (2, 6, 9, 12) and deferred:
                deferred.pop(0)()
            if kt == 15:
                deferred.extend(fin_a(qd, cmp, ob, lb))
            yield
        while deferred:
            deferred.pop(0)()
            yield

    def gen_attn_B(h, g, slot):
        sbanks = [3, 6, 7]
        LB = 3
        q, tq, k, tk_, v, tv = uq[slot], t_uq[slot], uk[slot], t_uk[slot], uv[slot], t_uv[slot]
        Dil, Ls, nq, ntile, ws = b_geom(g)
        mult = -SLOPE_B[g][h] * Dil
        qtiles = [(r, m) for r in range(Dil) for m in range(nq)]
        deferred = []

        def geom(i):
            r, m = qtiles[i]
            if nq == 1:
                return r, m, [0], 768
            return r, m, [m, m + 1], (0 if m == 0 else (512 if m == nq - 1 else 256))

        def S(i):
            r, m, subs, boff = geom(i)
            sbk = sbanks[i % LB]
            qap = q[:, r * Ls + m * 128:r * Ls + (m + 1) * 128]
            for si_, ct in enumerate(subs):
                k0 = r * Ls + ws(ct)
                mm(ps[sbk][:, si_ * 128:(si_ + 1) * 128], k[:, k0:k0 + 128], qap, True, True,
                   r=tk_ + tq, w=[t_ps[sbk]])

        def accum(i1, olb):
            r0, m0 = qtiles[i1 - 1]

            def dview(a):
                if nq == 1:
                    return a[:, :].rearrange("p (e r) -> p r e", r=16)[:, r0:r0 + 2, :]
                return a[:, :].rearrange("p (m e r) -> p r m e", r=Dil, e=128)[:, r0, m0:m0 + 2, :]

            def sview(c):
                return ps[olb][:].rearrange("p (t c e) -> p t c e", t=2, c=2)[:, :, c, :]

            def f():
                if g == 0:
                    act(dview(Oacc), sview(0), AF.Copy, r=[t_ps[olb]], w=[t_Oacc])
                    act(dview(Lacc), sview(1), AF.Copy, r=[t_ps[olb]], w=[t_Lacc])
                else:
                    tt("dve", dview(Oacc), sview(0), dview(Oacc), ALU.add, r=[t_ps[olb]], w=[t_Oacc])
                    tt("dve", dview(Lacc), sview(1), dview(Lacc), ALU.add, r=[t_ps[olb]], w=[t_Lacc])
            return f

        for i in range(LB):
            S(i)
        for i in range(16):
            r, m, subs, boff = geom(i)
            n = 128 * len(subs)
            sbk = sbanks[i % LB]
            j = i % 2
            bi = i // 2
            olb = 4 + bi % 2
            si = nxt("St3", 3)
            stt("dve", St[si][:, 0:n], rb[:, boff:boff + n], mult, ps[sbk][:, 0:n], ALU.mult, ALU.add,
                r=[t_rb, t_ps[sbk]], w=[t_St[si]])
            pj = nxt("Pt", 3)
            act(Pt[pj][:, 0:n], St[si][:, 0:n], AF.Exp, r=[t_St[si]], w=[t_Pt[pj]])
            if i + LB < 16:
                S(i + LB)
            for si_, ct in enumerate(subs):
                vt = r * ntile + ct
                mm(ps[olb][:, j * 256:j * 256 + 128], v[:, vt, :], Pt[pj][:, si_ * 128:(si_ + 1) * 128],
                   si_ == 0, si_ == len(subs) - 1, r=[tv[vt // 4], t_Pt[pj]], w=[t_ps[olb]])
            for si_, ct in enumerate(subs):
                mm(ps[olb][:, j * 256 + 128:j * 256 + 256], ones1[:], Pt[pj][:, si_ * 128:(si_ + 1) * 128],
                   si_ == 0, si_ == len(subs) - 1, r=[t_const, t_Pt[pj]], w=[t_ps[olb]])
            if j == 1:
                if deferred:
                    deferred.pop(0)()
                deferred.append(accum(i, olb))
            yield
        while deferred:
            deferred.pop(0)()
        if g == 2:
            for qd in range(4):
                recip_act(Lacc[:, qd * 512:(qd + 1) * 512], Lacc[:, qd * 512:(qd + 1) * 512], r=[t_Lacc], w=[t_Lacc])
            for qd in range(4):
                tt("pool", outT[:, 4 + h, qd * 512:(qd + 1) * 512], Oacc[:, qd * 512:(qd + 1) * 512],
                   Lacc[:, qd * 512:(qd + 1) * 512], ALU.mult, r=[t_Oacc, t_Lacc], w=[t_o[4 + h][qd]])
        yield

    def gen_attn(u, slot):
        kind, h, g = u
        if kind == "C":
            return gen_attn_C(h, slot), 9
        if kind == "A":
            return gen_attn_A(h, slot), 130
        return gen_attn_B(h, g, slot), 17

    prev = None
    last_mode = None
    for i, u in enumerate(units + [None]):
        mode = "A" if (prev is not None and prev[0][0] in ("A", "C")) else "B"
        if last_mode is not None and mode != last_mode:
            P.barrier()
        last_mode = mode
        gens = []
        if u is not None:
            gens.append((gen_proj(u, i % 3, mode), nproj(u)))
        if prev is not None:
            gens.append(gen_attn(prev[0], prev[1]))
        interleave(gens)
        prev = (u, i % 3) if u is not None else None

    if DEBUG:
        dma_sp(dbg_o[:, :, :], outT[:], "dbg1", r=[t for row in t_o for t in row])

    mixT = sb("mixT", [128, 8, T], BF16, 104448)
    W2_OFFS = [92160, 137216, 149504]
    w2g = [sb("w2g%d" % i, [128, 8, 512], BF16, W2_OFFS[i]) for i in range(3)]
    w2b = [sb("w2b%d" % i, [128, 4, 512], BF16, W2_OFFS[i] + 8192) for i in range(3)]
    gacc = sb("gacc", [128, 16, 512], F32, 161792)
    sg = [sb("sg%d" % i, [128, 512], F32, 194560 + 2048 * i) for i in range(2)]
    gtmp = [sb("gtmp%d" % i, [128, 512], F32, 198656 + 2048 * i) for i in range(2)]
    t_sg, t_gtmp = [Tk(), Tk()], [Tk(), Tk()]
    t_gacc = [Tk() for _ in range(16)]
    t_mix = [[Tk() for _ in range(4)] for _ in range(8)]

    jobs2 = []
    for cb in range(2):
        for g in range(3):
            jobs2.append([
                ((lambda s: s[0][:]),
                 w_gate.rearrange("(c p) n -> p c n", p=128)[:, :, g * 1024 + cb * 512:g * 1024 + (cb + 1) * 512]),
                ((lambda s: s[1][:]),
                 w_br[g * 512:(g + 1) * 512, :].rearrange("(c p) n -> p c n", p=128)[:, :, cb * 512:(cb + 1) * 512]),
            ])
    for cb in range(2):
        jobs2.append([((lambda s: s[0][:]),
                       w_out.rearrange("(c p) n -> p c n", p=128)[:, :, cb * 512:(cb + 1) * 512])])
    ring2 = Ring(P, list(zip(w2g, w2b)), jobs2, "w2", tks=[t_ra, Tk(), Tk()], init=1)
    P.barrier()
    ring2.advance()
    ring2.advance()

    jn = 0
    for cb in range(2):
        for g in range(3):
            (wg, wb_), wtk = ring2.get(jn)
            jn += 1
            for dci in range(4):
                dc = cb * 4 + dci
                for qd in range(4):
                    qs = slice(qd * 512, (qd + 1) * 512)
                    k = dci * 4 + qd
                    pg = nxt("pb", 2)
                    pbk = 2 + pg
                    for c in range(8):
                        mm(ps[pg][:], wg[:, c, dci * 128:(dci + 1) * 128], hT[:, c, qs], c == 0, c == 7,
                           r=[wtk, t_h[c][qd]], w=[t_ps[pg]])
                    for hh in range(4):
                        mm(ps[pbk][:], wb_[:, hh, dci * 128:(dci + 1) * 128], outT[:, g * 4 + hh, qs], hh == 0, hh == 3,
                           r=[wtk, t_o[g * 4 + hh][qd]], w=[t_ps[pbk]])
                    s_ = nxt("St", 2)
                    act(sg[s_][:], ps[pg][:], AF.Sigmoid, r=[t_ps[pg], t_cols], w=[t_sg[s_]],
                        bias=cols[:, 24 + g * 8 + dc:24 + g * 8 + dc + 1])
                    if g == 0:
                        tt("dve", gacc[:, k, :], ps[pbk][:], sg[s_][:], ALU.mult, r=[t_ps[pbk], t_sg[s_]], w=[t_gacc[k]])
                    else:
                        a = nxt("sq", 2)
                        tt("dve", gtmp[a][:], ps[pbk][:], sg[s_][:], ALU.mult, r=[t_ps[pbk], t_sg[s_]], w=[t_gtmp[a]])
                        if g == 1:
                            tt("dve", gacc[:, k, :], gacc[:, k, :], gtmp[a][:], ALU.add, r=[t_gtmp[a]], w=[t_gacc[k]])
                        else:
                            tt("dve", mixT[:, dc, qs], gacc[:, k, :], gtmp[a][:], ALU.add, r=[t_gtmp[a], t_gacc[k]],
                               w=[t_mix[dc][qd]])
            ring2.advance()

    P.barrier()
    x1T = sb("x1T", [128, 8, T], F32, 10240)
    t_x1 = [[Tk() for _ in range(4)] for _ in range(8)]
    for dc in range(8):
        dma_sp(x1T[:, dc, :], xT[dc * 128:(dc + 1) * 128, :], "x1_%d" % dc, w=t_x1[dc])
    W3_OFF = 153600
    NW3 = 5
    w3 = [sb("w3_%d" % i, [128, 4096], BF16, W3_OFF + 8192 * i) for i in range(NW3)]
    FB = [(0, 4), (4, 4), (8, 4), (12, 4), (16, 4), (20, 2)]
    jobs3 = []
    for hf in range(2):
        for (f0, nf) in FB:
            for wsrc in (w_fg, w_fu):
                jobs3.append([((lambda s, nf=nf: s[:, 0:8 * nf * 128].rearrange("p (c n) -> p c n", c=8)),
                               wsrc.rearrange("(c p) n -> p c n", p=128)[:, :, f0 * 128:(f0 + nf) * 128])])
        for dc in range(8):
            jobs3.append([((lambda s: s[:, 0:22 * 128].rearrange("p (c n) -> p c n", c=22)),
                           w_fd.rearrange("(c p) n -> p c n", p=128)[:, :, dc * 128:(dc + 1) * 128])])
    ring3 = Ring(P, w3, jobs3, "w3", tks=[ring2.tk[2], Tk(), Tk(), Tk(), Tk()])

    for cb in range(2):
        (wg, wb_), wtk = ring2.get(6 + cb)
        for dci in range(4):
            dc = cb * 4 + dci
            for qd in range(4):
                qs = slice(qd * 512, (qd + 1) * 512)
                pb = 4 + nxt("pb", 2)
                for c in range(8):
                    mm(ps[pb][:], wg[:, c, dci * 128:(dci + 1) * 128], mixT[:, c, qs], c == 0, c == 7,
                       r=[wtk, t_mix[c][qd]], w=[t_ps[pb]])
                tt("dve", x1T[:, dc, qs], ps[pb][:], x1T[:, dc, qs], ALU.add, r=[t_ps[pb]], w=[t_x1[dc][qd]])
        ring2.advance()

    if DEBUG:
        dma_sp(dbg_x1[:, :, :], x1T[:], "dbg2", r=[t for row in t_x1 for t in row])
    P.barrier()

    h2T = sb("h2T", [128, 8, T], BF16, 75776)
    actT = sb("actT", [128, 22, 1024], BF16, 108544)
    F_TMP = W3_OFF + 8192 * NW3
    sq3 = [sb("sq3_%d" % i, [128, 512], BF16, F_TMP + 1024 * i) for i in range(2)]
    std3 = sb("std3", [128, 512], F32, F_TMP + 2048)
    rstd3 = [sb("rstd3_%d" % i, [128, 512], F32, F_TMP + 4096 + 2048 * i) for i in range(2)]
    sil = [sb("sil%d" % i, [128, 512], F32, F_TMP + 8192 + 2048 * i) for i in range(2)]
    assert F_TMP + 12288 <= 212864
    t_sq3, t_std3, t_rstd3, t_sil = [Tk(), Tk()], Tk(), [Tk(), Tk()], [Tk(), Tk()]
    t_h2 = [[Tk() for _ in range(4)] for _ in range(8)]
    t_act = [[Tk() for _ in range(2)] for _ in range(22)]

    for qd in range(4):
        qs = slice(qd * 512, (qd + 1) * 512)
        for c in range(8):
            b = nxt("sq", 2)
            act(sq3[b][:], x1T[:, c, qs], AF.Square, r=[t_x1[c][qd]], w=[t_sq3[b]])
            mm(ps[6][:], ones1024[:], sq3[b][:], c == 0, c == 7, r=[t_sq3[b], t_const], w=[t_ps[6]])
        act(std3[:], ps[6][:], AF.Ln, r=[t_ps[6]], w=[t_std3], bias=EPS)
        b = nxt("rstd", 2)
        act(rstd3[b][:], std3[:], AF.Exp, r=[t_std3], w=[t_rstd3[b]], scale=-0.5)
        for c in range(8):
            eng = "dve"
            stt(eng, h2T[:, c, qs], x1T[:, c, qs], cols[:, 8 + c:9 + c], rstd3[b][:], ALU.mult, ALU.mult,
                r=[t_x1[c][qd], t_cols, t_rstd3[b]], w=[t_h2[c][qd]])

    t_out = Tk()
    out_evs = []
    jn = 0
    for hf in range(2):
        for (f0, nf) in FB:
            wgs, wgt = ring3.get(jn)
            wus, wut = ring3.get(jn + 1)
            jn += 2
            wgv = wgs[:, 0:8 * nf * 128].rearrange("p (c n) -> p c n", c=8)
            wuv = wus[:, 0:8 * nf * 128].rearrange("p (c n) -> p c n", c=8)
            for fi in range(nf):
                fc = f0 + fi
                for q2 in range(2):
                    qd = hf * 2 + q2
                    qs = slice(qd * 512, (qd + 1) * 512)
                    pg = nxt("pb", 2)
                    pu = 2 + pg
                    for c in range(8):
                        mm(ps[pg][:], wgv[:, c, fi * 128:(fi + 1) * 128], h2T[:, c, qs], c == 0, c == 7,
                           r=[wgt, t_h2[c][qd]], w=[t_ps[pg]])
                    for c in range(8):
                        mm(ps[pu][:], wuv[:, c, fi * 128:(fi + 1) * 128], h2T[:, c, qs], c == 0, c == 7,
                           r=[wut, t_h2[c][qd]], w=[t_ps[pu]])
                    s_ = nxt("St", 2)
                    act(sil[s_][:], ps[pg][:], AF.Silu, r=[t_ps[pg]], w=[t_sil[s_]])
                    tt("dve", actT[:, fc, q2 * 512:(q2 + 1) * 512], ps[pu][:], sil[s_][:], ALU.mult,
                       r=[t_ps[pu], t_sil[s_]], w=[t_act[fc][q2]])
            ring3.advance()
            ring3.advance()
        for dc in range(8):
            wds, wdt = ring3.get(jn)
            jn += 1
            wdv = wds[:, 0:22 * 128].rearrange("p (c n) -> p c n", c=22)
            for q2 in range(2):
                qd = hf * 2 + q2
                qs = slice(qd * 512, (qd + 1) * 512)
                pb = 4 + nxt("sbk", 2)
                for fc in range(22):
                    mm(ps[pb][:], wdv[:, fc, :], actT[:, fc, q2 * 512:(q2 + 1) * 512], fc == 0, fc == 21,
                       r=[wdt, t_act[fc][q2]], w=[t_ps[pb]])
                tt("dve", x1T[:, dc, qs], ps[pb][:], x1T[:, dc, qs], ALU.add, r=[t_ps[pb]], w=[t_x1[dc][qd]])
            ring3.advance()
            ev = dma_sp(yT[dc * 128:(dc + 1) * 128, hf * 1024:(hf + 1) * 1024], x1T[:, dc, hf * 1024:(hf + 1) * 1024],
                        "y%d" % (dc % 4), r=[t_x1[dc][hf * 2], t_x1[dc][hf * 2 + 1]])
            out_evs.append(ev)
    P.wait_only("sp", out_evs)
    P.emit()
    return nc


def _consts():
    p = np.arange(128, dtype=np.float32)[:, None]
    u = np.arange(3968, dtype=np.float32)[None, :]
    ra = np.abs(u - 1920.0 - p).astype(np.float32)
    j = np.arange(128, dtype=np.float32)[None, :]

    def tile(rel, ok):
        a = np.abs(rel)
        return np.where(ok & (a <= 64), a, BIG).astype(np.float32)

    allp = np.ones((128, 128), bool)
    Ft = tile(j - p, (p < 64) & allp)
    At = tile(j - p + 64, allp)
    Bt = tile(j - p - 64, allp)
    Lt = tile(j - p, (p >= 64) & allp)
    G2 = tile(j - p, allp)
    rb = np.concatenate([Ft, Bt, At, Bt, At, Lt, G2], axis=1)
    return np.ascontiguousarray(ra), np.ascontiguousarray(rb)


_NC_CACHE = {}


def kernel(x, mem, norm_mix, w_in, w_gate, b_gate, a_q_norm, a_k_norm, a_lambda_q1, a_lambda_k1,
           a_lambda_q2, a_lambda_k2, a_subln, b_q_norm, b_k_norm, mem_norm, w_mem_kv, c_q_norm,
           c_k_norm, w_branch, w_out, norm_ffn, w_ffn_gate, w_ffn_up, w_ffn_down):
    f = lambda a: np.ascontiguousarray(np.asarray(a, dtype=np.float32))
    x = np.asarray(x, dtype=np.float32)
    mem = np.asarray(mem, dtype=np.float32)
    ncore = x.shape[0]

    def c8(v):
        return np.asarray(v, np.float32).reshape(8, 128).T

    cols = np.zeros((128, 64), np.float32)
    cols[:, 0:8] = c8(norm_mix[0])
    cols[:, 8:16] = c8(norm_ffn[0])
    cols[:, 16:24] = c8(mem_norm[0])
    cols[:, 24:48] = np.asarray(b_gate[0], np.float32).reshape(24, 128).T
    cols[:, 48] = np.tile(np.asarray(a_q_norm[0], np.float32), 2)
    cols[:, 49] = np.tile(np.asarray(a_k_norm[0], np.float32), 2)
    cols[:, 50] = b_q_norm[0]
    cols[:, 51] = b_k_norm[0]
    cols[:, 52] = c_q_norm[0]
    cols[:, 53] = c_k_norm[0]
    cols[:, 54] = a_subln[0]
    lamv = np.broadcast_to(np.concatenate([np.asarray(v[0], np.float32) for v in
                                           (a_lambda_q1, a_lambda_k1, a_lambda_q2, a_lambda_k2)])[None, :], (128, 256))
    lamv = np.ascontiguousarray(lamv)
    ra, rb = _consts()
    shared = {
        "w_in": f(w_in[0]), "w_gate": f(w_gate[0]), "w_mem": f(w_mem_kv[0]),
        "w_br": f(np.asarray(w_branch[0]).reshape(1536, 1024)), "w_out": f(w_out[0]),
        "w_fg": f(w_ffn_gate[0]), "w_fu": f(w_ffn_up[0]), "w_fd": f(w_ffn_down[0]),
        "cols": cols, "lamv": lamv, "c_ra": ra, "c_rb": rb,
    }
    in_maps = []
    for b in range(ncore):
        m = dict(shared)
        m["xT"] = np.ascontiguousarray(x[b].T)
        m["memT"] = np.ascontiguousarray(mem[b].T)
        in_maps.append(m)
    if "nc" not in _NC_CACHE:
        _NC_CACHE["nc"] = build()
    res = run_bass_kernel_spmd(_NC_CACHE["nc"], in_maps, core_ids=list(range(ncore)))
    if DEBUG:
        _NC_CACHE["res"] = res
    out = np.stack([np.asarray(r["yT"]).T for r in res.results], axis=0)
    return np.ascontiguousarray(out.astype(np.float32))
```

```python
import contextlib
import numpy as np
import ml_dtypes
import concourse.bass as bass
import concourse.mybir as mybir
from concourse.bass_utils import run_bass_kernel_spmd

F32 = mybir.dt.float32
BF16 = mybir.dt.bfloat16
AF = mybir.ActivationFunctionType
ALU = mybir.AluOpType

SAME_ENGINE_SYNC = True
DEBUG = False
BASE = 16512
T = 2048
EPS = 1e-6
D_IN = 6656
LAMBDA_INIT = 0.2
BIG = 1.0e5
ALIBI_SKIP = 64.0


class Tk:
    __slots__ = ("w", "rs")

    def __init__(self):
        self.w = None
        self.rs = []


class Prog:
    ENG = ("pe", "act", "dve", "pool", "sp")

    def __init__(self, nc):
        self.nc = nc
        self.ops = {e: [] for e in self.ENG}
        self.dma_cnt = {}

    def _deps(self, r, w):
        deps = set()
        for t in r:
            if t.w is not None:
                deps.add(t.w)
        for t in w:
            if t.w is not None:
                deps.add(t.w)
            deps.update(t.rs)
        return deps

    def _commit(self, me, r, w):
        for t in r:
            t.rs.append(me)
        for t in w:
            t.w = me
            t.rs = []

    def _mark(self, deps, eng):
        for d in deps:
            if d[0] == "e":
                if d[1] == eng and (eng in ("pe", "sp") or not SAME_ENGINE_SYNC):
                    continue
                self.ops[d[1]][d[2]]["waited"] = True

    def op(self, eng, fn, r=(), w=()):
        deps = self._deps(r, w)
        me = ("e", eng, len(self.ops[eng]))
        self.ops[eng].append(dict(fn=fn, deps=deps, waited=False, dma=None))
        self._mark(deps, eng)
        self._commit(me, r, w)
        return me

    def dma(self, eng, fn, sem, r=(), w=()):
        deps = self._deps(r, w)
        n = self.dma_cnt.get(sem, 0) + 1
        self.dma_cnt[sem] = n
        me = ("d", sem, 16 * n)
        self.ops[eng].append(dict(fn=fn, deps=deps, waited=False, dma=sem))
        self._mark(deps, eng)
        self._commit(me, r, w)
        return me

    def wait_only(self, eng, deps):
        deps = set(deps)
        self.ops[eng].append(dict(fn=None, deps=deps, waited=False, dma=None))
        self._mark(deps, eng)

    def barrier(self, extra=()):
        evs = set(extra)
        for e in ("pe", "act", "dve", "pool"):
            for i in range(len(self.ops[e]) - 1, -1, -1):
                rec = self.ops[e][i]
                if rec["fn"] is not None and rec["dma"] is None:
                    evs.add(("e", e, i))
                    break
        for e in self.ENG:
            self.wait_only(e, evs)

    def emit(self):
        nc = self.nc
        for e in self.ENG:
            c = 0
            for rec in self.ops[e]:
                if rec["dma"] is None and rec["waited"] and rec["fn"] is not None:
                    c += 1
                rec["val"] = c
        with contextlib.ExitStack() as st:
            esem = {e: st.enter_context(nc.semaphore("s_" + e)) for e in self.ENG}
            dsem = {k: st.enter_context(nc.semaphore("d_%s" % (k,))) for k in self.dma_cnt}
            block = st.enter_context(nc.Block())
            ops = self.ops

            def run(e, engobj):
                waited = {}
                for rec in ops[e]:
                    need = {}
                    for d in rec["deps"]:
                        if d[0] == "e":
                            if d[1] == e and (e in ("pe", "sp") or not SAME_ENGINE_SYNC):
                                continue
                            key = ("e", d[1])
                            val = ops[d[1]][d[2]]["val"]
                            sem = esem[d[1]]
                        else:
                            key = ("d", d[1])
                            val = d[2]
                            sem = dsem[d[1]]
                        if waited.get(key, 0) >= val:
                            continue
                        if need.get(key, (None, 0))[1] < val:
                            need[key] = (sem, val)
                    for key, (sem, val) in need.items():
                        engobj.wait_ge(sem, val)
                        waited[key] = val
                    if rec["fn"] is None:
                        continue
                    ins = rec["fn"](engobj)
                    if rec["dma"] is not None:
                        ins.then_inc(dsem[rec["dma"]], 16)
                    elif rec["waited"]:
                        ins.then_inc(esem[e], 1)

            @block.tensor
            def _(eng):
                run("pe", eng)

            @block.scalar
            def _(eng):
                run("act", eng)

            @block.vector
            def _(eng):
                run("dve", eng)

            @block.gpsimd
            def _(eng):
                run("pool", eng)

            @block.sync
            def _(eng):
                run("sp", eng)


def interleave(gens):
    live = [[g, 0, max(1, n)] for g, n in gens]
    while live:
        live.sort(key=lambda x: x[1] / x[2])
        g = live[0]
        try:
            next(g[0])
            g[1] += 1
        except StopIteration:
            live.pop(0)


class Ring:
    def __init__(self, P, slots, jobs, name, tks=None, init=None):
        self.P, self.slots, self.jobs, self.name = P, slots, jobs, name
        self.tk = tks if tks is not None else [Tk() for _ in slots]
        self.issued = 0
        for _ in range(len(slots) if init is None else init):
            self.advance()

    def advance(self):
        j = self.issued
        if j >= len(self.jobs):
            return
        s = j % len(self.slots)
        for outf, in_ap in self.jobs[j]:
            out_ap = outf(self.slots[s])
            self.P.dma("pool", (lambda e, o=out_ap, i=in_ap: e.dma_start(out=o, in_=i)),
                       "%s%d" % (self.name, s), w=[self.tk[s]])
        self.issued += 1

    def get(self, j):
        assert j < self.issued, (self.name, j, self.issued)
        s = j % len(self.slots)
        return self.slots[s], self.tk[s]


def build():
    nc = bass.Bass("TRN2", target_bir_lowering=False)
    P = Prog(nc)

    def din(name, shape, dtype=F32):
        return nc.dram_tensor(name, shape, dtype, kind="ExternalInput").ap()

    xT = din("xT", [1024, T])
    memT = din("memT", [1024, 256])
    w_in = din("w_in", [1024, D_IN])
    w_gate = din("w_gate", [1024, 3072])
    w_mem = din("w_mem", [1024, 1024])
    w_br = din("w_br", [1536, 1024])
    w_out = din("w_out", [1024, 1024])
    w_fg = din("w_fg", [1024, 2816])
    w_fu = din("w_fu", [1024, 2816])
    w_fd = din("w_fd", [2816, 1024])
    cols_d = din("cols", [128, 64])
    lamv_d = din("lamv", [128, 256])
    ra_d = din("c_ra", [128, 3968])
    rb_d = din("c_rb", [128, 896])
    yT = nc.dram_tensor("yT", [1024, T], F32, kind="ExternalOutput").ap()
    if DEBUG:
        dbg_h = nc.dram_tensor("dbg_h", [128, 8, T], BF16, kind="ExternalOutput").ap()
        dbg_o = nc.dram_tensor("dbg_o", [128, 12, T], BF16, kind="ExternalOutput").ap()
        dbg_x1 = nc.dram_tensor("dbg_x1", [128, 8, T], F32, kind="ExternalOutput").ap()

    def sb(name, shape, dtype, off):
        return nc.alloc_sbuf_tensor_at(name, shape, dtype, offset=BASE + off)

    ps = [nc.alloc_psum_tensor("ps%d" % i, [128, 512], F32) for i in range(8)]
    t_ps = [Tk() for _ in range(8)]

    def mm(out, lhsT, rhs, start, stop, r, w):
        P.op("pe", lambda e: e.matmul(out, lhsT=lhsT, rhs=rhs, start=start, stop=stop), r=r, w=w)

    def act(out, in_, func, r, w, bias=None, scale=None):
        kw = {}
        if bias is not None:
            kw["bias"] = bias
        if scale is not None:
            kw["scale"] = scale
        P.op("act", lambda e: e.activation(out=out, in_=in_, func=func, **kw), r=r, w=w)

    def stt(eng, out, in0, scalar, in1, op0, op1, r, w):
        P.op(eng, lambda e: e.scalar_tensor_tensor(out=out, in0=in0, scalar=scalar, in1=in1, op0=op0, op1=op1),
             r=r, w=w)

    def tt(eng, out, in0, in1, op, r, w):
        P.op(eng, lambda e: e.tensor_tensor(out=out, in0=in0, in1=in1, op=op), r=r, w=w)

    def recip(out, in_, r, w):
        P.op("dve", lambda e: e.reciprocal(out=out, in_=in_), r=r, w=w)

    def recip_act(out, in_, r, w):
        act(out, in_, AF.Ln, r=r, w=w)
        act(out, out, AF.Exp, r=w, w=w, scale=-1.0)

    def tsmul(eng, out, in0, s, r, w):
        P.op(eng, lambda e: e.tensor_scalar_mul(out=out, in0=in0, scalar1=s), r=r, w=w)

    def dma_sp(out, in_, sem, r=(), w=()):
        return P.dma("sp", lambda e: e.dma_start(out=out, in_=in_), sem, r=r, w=w)

    ones128 = sb("ones128", [128, 128], BF16, 0)
    ones1024 = sb("ones1024", [128, 128], BF16, 256)
    onesblk = sb("onesblk", [128, 128], BF16, 512)
    ones1 = sb("ones1", [128, 128], BF16, 768)
    cols = sb("colsb", [128, 64], F32, 1024)
    lamv = sb("lamvb", [128, 256], F32, 1280)
    rb = sb("rbb", [128, 896], F32, 2304)
    ckT = sb("ckT", [128, 4, 256], BF16, 5888)
    cv = sb("cv", [128, 2, 512], BF16, 7936)
    t_const = Tk()
    t_cols = Tk()
    t_rb = Tk()
    t_ck = [Tk() for _ in range(4)]
    t_cv = [Tk() for _ in range(2)]

    P.op("pool", lambda e: e.memset(ones128[:], 1.0 / 128), w=[t_const])
    P.op("pool", lambda e: e.memset(ones1024[:], 1.0 / 1024), w=[t_const])
    P.op("pool", lambda e: e.memset(onesblk[:], 0.0), w=[t_const])
    P.op("pool", lambda e: e.memset(onesblk[0:64, 0:64], 1.0 / 64), w=[t_const])
    P.op("pool", lambda e: e.memset(onesblk[64:128, 64:128], 1.0 / 64), w=[t_const])
    P.op("pool", lambda e: e.memset(ones1[:], 1.0), w=[t_const])
    dma_sp(cols[:], cols_d[:, :], "c0", w=[t_cols])
    dma_sp(lamv[:], lamv_d[:, :], "c1", w=[t_cols])
    dma_sp(rb[:], rb_d[:, :], "c2", w=[t_rb])

    stt("dve", cols[:, 56:57], cols[:, 48:49], 64.0 ** -0.5, cols[:, 49:50], ALU.mult, ALU.mult, r=[t_cols], w=[t_cols])
    stt("dve", cols[:, 57:58], cols[:, 50:51], 128.0 ** -0.5, cols[:, 51:52], ALU.mult, ALU.mult, r=[t_cols], w=[t_cols])
    stt("dve", cols[:, 58:59], cols[:, 52:53], 128.0 ** -0.5, cols[:, 53:54], ALU.mult, ALU.mult, r=[t_cols], w=[t_cols])
    tsmul("dve", cols[:, 59:60], cols[:, 54:55], 1.0 - LAMBDA_INIT, r=[t_cols], w=[t_cols])
    tt("dve", lamv[:, 0:64], lamv[:, 0:64], lamv[:, 64:128], ALU.mult, r=[t_cols], w=[t_cols])
    tt("dve", lamv[:, 128:192], lamv[:, 128:192], lamv[:, 192:256], ALU.mult, r=[t_cols], w=[t_cols])
    P.op("dve", lambda e: e.reduce_sum(out=cols[:, 61:62], in_=lamv[:, 0:64], axis=mybir.AxisListType.X), r=[t_cols], w=[t_cols])
    P.op("dve", lambda e: e.reduce_sum(out=cols[:, 62:63], in_=lamv[:, 128:192], axis=mybir.AxisListType.X), r=[t_cols], w=[t_cols])
    act(cols[:, 61:63], cols[:, 61:63], AF.Exp, r=[t_cols], w=[t_cols])
    stt("dve", cols[:, 60:61], cols[:, 62:63], -LAMBDA_INIT, cols[:, 61:62], ALU.add, ALU.subtract, r=[t_cols], w=[t_cols])

    hT = sb("hT", [128, 8, T], BF16, 10240)
    outT = sb("outT", [128, 12, T], BF16, 43008)
    ra = sb("ra", [128, 3968], F32, 92160)
    W1_OFF = 108032
    NW1 = 10
    w1slots = [sb("w1_%d" % i, [128, 8, 128], BF16, W1_OFF + 2048 * i) for i in range(NW1)]
    U_OFF = W1_OFF + 2048 * NW1
    USZ = 13312
    uq = [sb("uq%d" % i, [128, T], BF16, U_OFF + USZ * i) for i in range(3)]
    uk = [sb("uk%d" % i, [128, T], BF16, U_OFF + USZ * i + 4096) for i in range(3)]
    uv = [sb("uv%d" % i, [128, 20, 128], BF16, U_OFF + USZ * i + 8192) for i in range(3)]
    ACC_OFF = U_OFF + 3 * USZ
    Oacc = sb("Oacc", [128, T], F32, ACC_OFF)
    Lacc = sb("Lacc", [128, T], F32, ACC_OFF + 8192)
    TMP = ACC_OFF + 16384
    sq = [sb("sq%d" % i, [128, 512], BF16, TMP + 1024 * i) for i in range(2)]
    rstd = [sb("rstd%d" % i, [128, 512], F32, TMP + 2048 + 2048 * i) for i in range(2)]
    std = sb("std", [128, 512], F32, TMP + 6144)
    St = [sb("St%d" % i, [128, 512], F32, TMP + 8192 + 2048 * i) for i in range(2)]
    Pt = [sb("Pt%d" % i, [128, 512], BF16, TMP + 12288 + 1024 * i) for i in range(3)]
    fin = [sb("fin%d" % i, [128, 512], F32, TMP + 15360 + 2048 * i) for i in range(3)]
    sqA = sb("sqA", [128, 512], BF16, TMP + 15360 + 2048)
    rawg = [sb("rawg%d" % i, [128, 512], F32, TMP + 21504 + 2048 * i) for i in range(2)]
    t_rawg = [Tk(), Tk()]
    St.append(sb("St2", [128, 512], F32, TMP + 25600))
    assert TMP + 25600 + 2048 <= 212864
    xs = [sb("xs%d" % i, [128, 8, 512], F32, U_OFF + 16384 * i) for i in range(2)]
    ms = sb("ms", [128, 8, 256], F32, U_OFF + 32768)
    mT = sb("mT", [128, 8, 256], BF16, U_OFF + 40960)

    t_h = [[Tk() for _ in range(4)] for _ in range(8)]
    t_o = [[Tk() for _ in range(4)] for _ in range(12)]
    t_ra = Tk()
    t_uq = [[Tk() for _ in range(4)] for _ in range(3)]
    t_uk = [[Tk() for _ in range(4)] for _ in range(3)]
    t_uv = [[Tk() for _ in range(5)] for _ in range(3)]
    t_Oacc, t_Lacc = Tk(), Tk()
    t_sq = [Tk(), Tk()]
    t_rstd = [Tk(), Tk()]
    t_std = Tk()
    t_St = [Tk(), Tk(), Tk()]
    t_Pt = [Tk(), Tk(), Tk()]
    t_fin = [Tk(), Tk(), Tk()]
    t_xs = [Tk(), Tk()]
    t_ms, t_mT = Tk(), [Tk() for _ in range(8)]

    rr = {"sq": 0, "rstd": 0, "St": 0, "Pt": 0, "pb": 0, "sbk": 0, "rawg": 0, "St3": 0}

    def nxt(key, n):
        v = rr[key]
        rr[key] = (v + 1) % n
        return v

    dma_sp(ra[:], ra_d[:, :], "c3", w=[t_ra])

    def wsl(w_ap, col0, ncol=128):
        src = w_ap.rearrange("(c p) n -> p c n", p=128)[:, :, col0:col0 + ncol]
        return [((lambda s: s[:]), src)]

    jobs1 = []
    for h in range(4):
        jobs1.append(wsl(w_mem, h * 128))
    for h in range(4):
        jobs1.append(wsl(w_mem, 512 + h * 128))
    units = []
    for h in range(4):
        units.append(("C", h, 0))
    for h in range(4):
        units.append(("A", h, 0))
    for h in range(4):
        for g in range(3):
            units.append(("B", h, g))
    ujob = {}
    for u in units:
        kind, h, g = u
        ujob[u] = len(jobs1)
        if kind == "C":
            jobs1.append(wsl(w_in, 6144 + h * 128))
        elif kind == "A":
            jobs1.append(wsl(w_in, 0 + h * 128))
            jobs1.append(wsl(w_in, 512 + h * 128))
            jobs1.append(wsl(w_in, 1024 + h * 128))
        else:
            jobs1.append(wsl(w_in, 1536 + g * 512 + h * 128))
            jobs1.append(wsl(w_in, 3072 + g * 512 + h * 128))
            jobs1.append(wsl(w_in, 4608 + g * 512 + h * 128))
    ring1 = Ring(P, w1slots, jobs1, "w1")

    def rms_in(src_d, stage, t_stage, sem, ncol, gcol0, dst_fn, t_dst_fn, col_lo):
        dma_sp(stage[:, :, 0:ncol], src_d.rearrange("(c p) t -> p c t", p=128)[:, :, col_lo:col_lo + ncol], sem, w=[t_stage])
        for c in range(8):
            b = nxt("sq", 2)
            act(sq[b][:, 0:ncol], stage[:, c, 0:ncol], AF.Square, r=[t_stage], w=[t_sq[b]])
            mm(ps[2][:, 0:ncol], ones1024[:], sq[b][:, 0:ncol], c == 0, c == 7, r=[t_sq[b], t_const], w=[t_ps[2]])
        act(std[:, 0:ncol], ps[2][:, 0:ncol], AF.Ln, r=[t_ps[2]], w=[t_std], bias=EPS)
        b = nxt("rstd", 2)
        act(rstd[b][:, 0:ncol], std[:, 0:ncol], AF.Exp, r=[t_std], w=[t_rstd[b]], scale=-0.5)
        for c in range(8):
            eng = "dve"
            stt(eng, dst_fn(c), stage[:, c, 0:ncol], cols[:, gcol0 + c:gcol0 + c + 1], rstd[b][:, 0:ncol],
                ALU.mult, ALU.mult, r=[t_stage, t_cols, t_rstd[b]], w=[t_dst_fn(c)])

    for qd in range(4):
        s = qd % 2
        rms_in(xT, xs[s], t_xs[s], "xs%d" % s, 512, 0,
               (lambda c, qd=qd: hT[:, c, qd * 512:(qd + 1) * 512]), (lambda c, qd=qd: t_h[c][qd]), qd * 512)
    rms_in(memT, ms, t_ms, "ms", 256, 16, (lambda c: mT[:, c, :]), (lambda c: t_mT[c]), 0)

    if DEBUG:
        dma_sp(dbg_h[:, :, :], hT[:], "dbg0", r=[t for row in t_h for t in row])

    SSB = 1
    FINAL_ENG = "pool"
    FINAL_ENG_D = "pool"

    def norm_sq(psb, ncol):
        b = nxt("sq", 2)
        act(sq[b][:, 0:ncol], ps[psb][:, 0:ncol], AF.Square, r=[t_ps[psb]], w=[t_sq[b]])
        return b

    def norm_rest(b, psb, ncol, ones_ap, gcol, out_ap, in_view, w_tk):
        mm(ps[SSB][:, 0:ncol], ones_ap, sq[b][:, 0:ncol], True, True, r=[t_sq[b], t_const], w=[t_ps[SSB]])
        act(std[:, 0:ncol], ps[SSB][:, 0:ncol], AF.Ln, r=[t_ps[SSB]], w=[t_std], bias=EPS)
        rb_ = nxt("rstd", 2)
        act(rstd[rb_][:, 0:ncol], std[:, 0:ncol], AF.Exp, r=[t_std], w=[t_rstd[rb_]], scale=-0.5)
        if gcol is None:
            tt("dve", out_ap, in_view(ps[psb][:, 0:ncol]), in_view(rstd[rb_][:, 0:ncol]), ALU.mult,
               r=[t_ps[psb], t_rstd[rb_]], w=w_tk)
        else:
            stt("dve", out_ap, in_view(ps[psb][:, 0:ncol]), gcol, in_view(rstd[rb_][:, 0:ncol]), ALU.mult, ALU.mult,
                r=[t_ps[psb], t_rstd[rb_], t_cols], w=w_tk)

    def norm_tail(psb, ncol, ones_ap, gcol, out_ap, in_view, r_extra, w_tk):
        b = norm_sq(psb, ncol)
        norm_rest(b, psb, ncol, ones_ap, gcol, out_ap, in_view, w_tk)

    def proj_qk(wj, dst, t_dst, ones_ap, gcol, Dil, rawbanks, ssb):
        wslot, wtk = ring1.get(wj)
        n = 512 // Dil

        def pv(a):
            return a if Dil == 1 else a.rearrange("p (i r) -> p r i", r=Dil)

        def cv_(a):
            return a if Dil == 1 else a.rearrange("p (r i) -> p r i", r=Dil)

        def MM(qd):
            pb = rawbanks[qd % len(rawbanks)]
            for c in range(8):
                mm(ps[pb][:], wslot[:, c, :], hT[:, c, qd * 512:(qd + 1) * 512], c == 0, c == 7,
                   r=[wtk, t_h[c][qd]], w=[t_ps[pb]])
            if qd == 3:
                ring1.advance()

        def EV(qd):
            pb = rawbanks[qd % len(rawbanks)]
            b = norm_sq(pb, 512)
            rg = nxt("rawg", 2)
            if gcol is None:
                P.op("dve", (lambda e, rg=rg: e.tensor_copy(out=cv_(rawg[rg][:]), in_=pv(ps[pb][:]))),
                     r=[t_ps[pb], t_sq[b]], w=[t_rawg[rg]])
            else:
                tsmul("dve", cv_(rawg[rg][:]), pv(ps[pb][:]), gcol, r=[t_ps[pb], t_cols, t_sq[b]], w=[t_rawg[rg]])
            return b, rg

        def REST(qd, b, rg):
            mm(ps[ssb][:], ones_ap, sq[b][:], True, True, r=[t_sq[b], t_const], w=[t_ps[ssb]])
            act(std[:], ps[ssb][:], AF.Ln, r=[t_ps[ssb]], w=[t_std], bias=EPS)
            rb_ = nxt("rstd", 2)
            act(cv_(rstd[rb_][:]), pv(std[:]), AF.Exp, r=[t_std], w=[t_rstd[rb_]], scale=-0.5)
            if Dil == 1:
                out_ap = dst[:, qd * 512:(qd + 1) * 512]
            else:
                out_ap = dst[:, :].rearrange("p (r i) -> p r i", r=Dil)[:, :, qd * n:(qd + 1) * n]
            tt(FINAL_ENG if Dil == 1 else FINAL_ENG_D, out_ap, cv_(rawg[rg][:]), cv_(rstd[rb_][:]), ALU.mult,
               r=[t_rawg[rg], t_rstd[rb_]], w=[t_dst[qd]])

        if len(rawbanks) == 1:
            for qd in range(4):
                MM(qd)
                yield
                b, rg = EV(qd)
                yield
                REST(qd, b, rg)
            yield
        else:
            MM(0)
            yield
            MM(1)
            yield
            e0 = EV(0)
            yield
            MM(2)
            yield
            e1 = EV(1)
            REST(0, *e0)
            yield
            MM(3)
            yield
            e2 = EV(2)
            REST(1, *e1)
            yield
            e3 = EV(3)
            REST(2, *e2)
            yield
            REST(3, *e3)
            yield

    def proj_v(wj, vbuf, t_vb, tiles, vbanks):
        wslot, wtk = ring1.get(wj)
        ngrp = (len(tiles) + 3) // 4

        def MMG(gi):
            g0 = gi * 4
            grp = tiles[g0:g0 + 4]
            vb = vbanks[gi % len(vbanks)]
            for j, (start, step) in enumerate(grp):
                hi = start + 127 * step
                qs = range(start // 512, hi // 512 + 1)
                for c in range(8):
                    mm(ps[vb][:, j * 128:(j + 1) * 128], hT[:, c, start:hi + 1:step], wslot[:, c, :], c == 0, c == 7,
                       r=[wtk] + [t_h[c][q] for q in qs], w=[t_ps[vb]])
            if gi == ngrp - 1:
                ring1.advance()

        def CP(gi):
            g0 = gi * 4
            n = len(tiles[g0:g0 + 4])
            vb = vbanks[gi % len(vbanks)]
            act(vbuf[:, g0:g0 + n, :], ps[vb][:, 0:n * 128].rearrange("p (a b) -> p a b", b=128), AF.Copy,
                r=[t_ps[vb]], w=[t_vb[gi]])

        if len(vbanks) == 1:
            for gi in range(ngrp):
                MMG(gi)
                yield
                CP(gi)
                yield
        else:
            MMG(0)
            yield
            for gi in range(1, ngrp):
                MMG(gi)
                CP(gi - 1)
                yield
            CP(ngrp - 1)
            yield

    def kv_unit():
        for h in range(4):
            wslot, wtk = ring1.get(h)
            for c in range(8):
                mm(ps[0][:, 0:256], wslot[:, c, :], mT[:, c, :], c == 0, c == 7, r=[wtk, t_mT[c]], w=[t_ps[0]])
            ring1.advance()
            norm_tail(0, 256, ones128[:], None, ckT[:, h, :], (lambda a: a), [], [t_ck[h]])
        for kt in range(2):
            for h in range(4):
                wslot, wtk = ring1.get(4 + h)
                for c in range(8):
                    mm(ps[3][:, h * 128:(h + 1) * 128], mT[:, c, kt * 128:(kt + 1) * 128], wslot[:, c, :], c == 0, c == 7,
                       r=[wtk, t_mT[c]], w=[t_ps[3]])
                if kt == 1:
                    ring1.advance()
            act(cv[:, kt, :], ps[3][:], AF.Copy, r=[t_ps[3]], w=[t_cv[kt]])

    kv_unit()
    P.barrier()

    SLOPE_A = [2.0 ** (-2.0 * (h + 1)) for h in range(4)]
    SLOPE_B = [[2.0 ** (-8.0 * (4 * g + h + 1) / 12.0) for h in range(4)] for g in range(3)]
    DILS = [1, 4, 16]

    def b_geom(g):
        Dil = DILS[g]
        Ls = T // Dil
        nq = Ls // 128
        ntile = nq + 1 if nq > 1 else 1

        def ws(ct):
            if nq == 1 or ct == 0:
                return 0
            if ct == nq:
                return Ls - 128
            return 128 * ct - 64
        return Dil, Ls, nq, ntile, ws

    def gen_proj(u, slot, mode):
        kind, h, g = u
        j0 = ujob[u]
        rawbanks, ssb = ([0], 1) if mode == "A" else ([0, 1], 2)
        vbanks = rawbanks
        if kind == "C":
            yield from proj_qk(j0, uq[slot], t_uq[slot], ones128[:], cols[:, 58:59], 1, rawbanks, ssb)
        elif kind == "A":
            yield from proj_qk(j0, uq[slot], t_uq[slot], onesblk[:], cols[:, 56:57], 1, rawbanks, ssb)
            yield from proj_qk(j0 + 1, uk[slot], t_uk[slot], onesblk[:], None, 1, rawbanks, ssb)
            yield from proj_v(j0 + 2, uv[slot], t_uv[slot], [(kt * 128, 1) for kt in range(16)], vbanks)
        else:
            Dil, Ls, nq, ntile, ws = b_geom(g)
            yield from proj_qk(j0, uq[slot], t_uq[slot], ones128[:], cols[:, 57:58], Dil, rawbanks, ssb)
            yield from proj_qk(j0 + 1, uk[slot], t_uk[slot], ones128[:], None, Dil, rawbanks, ssb)
            tiles = [(Dil * ws(ct) + r, Dil) for r in range(Dil) for ct in range(ntile)]
            yield from proj_v(j0 + 2, uv[slot], t_uv[slot], tiles, vbanks)

    def nproj(u):
        kind, h, g = u
        if kind == "C":
            return 9
        if kind == "A":
            return 9 + 9 + 8
        _, _, _, ntile, _ = b_geom(g)
        return 18 + 1 + ((DILS[g] * ntile + 3) // 4)

    LA = 2
    t_s3 = [Tk(), Tk()]

    def gen_attn_C(h, slot):
        q, tq = uq[slot], t_uq[slot]
        tiles = [(qd, kt) for qd in range(4) for kt in range(2)]
        deferred = []

        def S(i):
            qd, kt = tiles[i]
            sbk = 2 + i % 2
            mm(ps[sbk][:], ckT[:, h, kt * 128:(kt + 1) * 128], q[:, qd * 512:(qd + 1) * 512], True, True,
               r=[t_ck[h], tq[qd]], w=[t_ps[sbk]])

        def fin_c(qd, ob, lb):
            def f():
                recip_act(fin[0][:], ps[lb][:], r=[t_ps[lb]], w=[t_fin[0]])
                tt("dve", outT[:, 8 + h, qd * 512:(qd + 1) * 512], ps[ob][:], fin[0][:], ALU.mult,
                   r=[t_ps[ob], t_fin[0]], w=[t_o[8 + h][qd]])
            return f

        for i in range(LA):
            S(i)
        for j, (qd, kt) in enumerate(tiles):
            sbk = 2 + j % 2
            ob, lb = 4 + qd % 2, 6 + qd % 2
            pj = nxt("Pt", 3)
            act(Pt[pj][:], ps[sbk][:], AF.Exp, r=[t_ps[sbk]], w=[t_Pt[pj]])
            if j + LA < len(tiles):
                S(j + LA)
            mm(ps[ob][:], cv[:, kt, h * 128:(h + 1) * 128], Pt[pj][:], kt == 0, kt == 1, r=[t_cv[kt], t_Pt[pj]], w=[t_ps[ob]])
            mm(ps[lb][:], ones1[:], Pt[pj][:], kt == 0, kt == 1, r=[t_const, t_Pt[pj]], w=[t_ps[lb]])
            if kt == 0 and deferred:
                deferred.pop(0)()
            if kt == 1:
                deferred.append(fin_c(qd, ob, lb))
            yield
        while deferred:
            deferred.pop(0)()
        yield

    def gen_attn_A(h, slot):
        q, tq, k, tk_, v, tv = uq[slot], t_uq[slot], uk[slot], t_uk[slot], uv[slot], t_uv[slot]
        def kts_for(qd):
            out = []
            for kt in range(16):
                gap = max(qd * 512 - (kt * 128 + 127), kt * 128 - (qd * 512 + 511), 0)
                if SLOPE_A[h] * gap < ALIBI_SKIP:
                    out.append(kt)
            return out

        tiles = []
        for qd in range(4):
            for cmp in range(2):
                ks = kts_for(qd)
                for idx, kt in enumerate(ks):
                    tiles.append((qd, cmp, kt, idx, len(ks)))
        deferred = []

        def S(i):
            qd, cmp, kt = tiles[i][:3]
            lo, hi = cmp * 64, cmp * 64 + 64
            sbk = 2 + i % 2
            mm(ps[sbk][:], k[lo:hi, kt * 128:(kt + 1) * 128], q[lo:hi, qd * 512:(qd + 1) * 512], True, True,
               r=[tk_[kt // 4], tq[qd]], w=[t_ps[sbk]])

        def fin_a(qd, cmp, ob, lb):
            st = {}

            def f1():
                act(fin[0][:], ps[lb][:], AF.Ln, r=[t_ps[lb]], w=[t_fin[0]])
                fo = 1 if cmp == 0 else 2
                P.op("dve", (lambda e: e.tensor_copy(out=fin[fo][:], in_=ps[ob][:])), r=[t_ps[ob]], w=[t_fin[fo]])
                act(fin[0][:], fin[0][:], AF.Exp, r=[t_fin[0]], w=[t_fin[0]], scale=-1.0)
                tt("pool", fin[fo][:], fin[fo][:], fin[0][:], ALU.mult, r=[t_fin[0]], w=[t_fin[fo]])

            def f2():
                stt("dve", fin[2][:], fin[2][:], cols[:, 60:61], fin[1][:], ALU.mult, ALU.add,
                    r=[t_fin[1], t_cols], w=[t_fin[2]])
                act(sqA[:], fin[2][:], AF.Square, r=[t_fin[2]], w=[t_fin[1]])

            def f3():
                mm(ps[SSB][:], ones128[:], sqA[:], True, True, r=[t_fin[1], t_const], w=[t_ps[SSB]])
                act(std[:], ps[SSB][:], AF.Ln, r=[t_ps[SSB]], w=[t_std], bias=EPS)
                act(fin[0][:], std[:], AF.Exp, r=[t_std], w=[t_fin[0]], scale=-0.5)

            def f4():
                stt("dve", outT[:, h, qd * 512:(qd + 1) * 512], fin[2][:], cols[:, 59:60], fin[0][:], ALU.mult, ALU.mult,
                    r=[t_fin[2], t_cols, t_fin[0]], w=[t_o[h][qd]])
            return [f1] if cmp == 0 else [f1, f2, f3, f4]

        for i in range(LA):
            S(i)
        for j, (qd, cmp, kt, idx, ng) in enumerate(tiles):
            sbk = 2 + j % 2
            gi = qd * 2 + cmp
            ob, lb = 4 + gi % 2, 6 + gi % 2
            off = qd * 512 - kt * 128 + 1920
            si = nxt("St3", 3)
            stt("dve", St[si][:], ra[:, off:off + 512], -SLOPE_A[h], ps[sbk][:], ALU.mult, ALU.add,
                r=[t_ra, t_ps[sbk]], w=[t_St[si]])
            pj = nxt("Pt", 3)
            act(Pt[pj][:], St[si][:], AF.Exp, r=[t_St[si]], w=[t_Pt[pj]])
            if j + LA < len(tiles):
                S(j + LA)
            mm(ps[ob][:], v[:, kt, :], Pt[pj][:], idx == 0, idx == ng - 1, r=[tv[kt // 4], t_Pt[pj]], w=[t_ps[ob]])
            mm(ps[lb][:], ones1[:], Pt[pj][:], idx == 0, idx == ng - 1, r=[t_const, t_Pt[pj]], w=[t_ps[lb]])
            pops = (2, 6, 9, 12) if ng >= 14 else ((2, 5, 8, 10) if ng >= 12 else (1, 2, 4, 5))
            if idx in pops and deferred:
                deferred.pop(0)()
            if idx == ng - 1:
                assert len(deferred) == 0, "previous finalize not fully emitted"
                deferred.extend(fin_a(qd, cmp, ob, lb))
            yield
        while deferred:
            deferred.pop(0)()
            yield

    def gen_attn_B(h, g, slot):
        sbanks = [3, 6, 7]
        LB = 3
        q, tq, k, tk_, v, tv = uq[slot], t_uq[slot], uk[slot], t_uk[slot], uv[slot], t_uv[slot]
        Dil, Ls, nq, ntile, ws = b_geom(g)
        mult = -SLOPE_B[g][h] * Dil
        qtiles = [(r, m) for r in range(Dil) for m in range(nq)]
        deferred = []

        def geom(i):
            r, m = qtiles[i]
            if nq == 1:
                return r, m, [0], 768
            return r, m, [m, m + 1], (0 if m == 0 else (512 if m == nq - 1 else 256))

        def S(i):
            r, m, subs, boff = geom(i)
            sbk = sbanks[i % LB]
            qap = q[:, r * Ls + m * 128:r * Ls + (m + 1) * 128]
            for si_, ct in enumerate(subs):
                k0 = r * Ls + ws(ct)
                mm(ps[sbk][:, si_ * 128:(si_ + 1) * 128], k[:, k0:k0 + 128], qap, True, True,
                   r=tk_ + tq, w=[t_ps[sbk]])

        def accum(i1, olb):
            r0, m0 = qtiles[i1 - 1]

            def dview(a):
                if nq == 1:
                    return a[:, :].rearrange("p (e r) -> p r e", r=16)[:, r0:r0 + 2, :]
                return a[:, :].rearrange("p (m e r) -> p r m e", r=Dil, e=128)[:, r0, m0:m0 + 2, :]

            def sview(c):
                return ps[olb][:].rearrange("p (t c e) -> p t c e", t=2, c=2)[:, :, c, :]

            def f():
                if g == 0:
                    act(dview(Oacc), sview(0), AF.Copy, r=[t_ps[olb]], w=[t_Oacc])
                    act(dview(Lacc), sview(1), AF.Copy, r=[t_ps[olb]], w=[t_Lacc])
                else:
                    tt("dve", dview(Oacc), sview(0), dview(Oacc), ALU.add, r=[t_ps[olb]], w=[t_Oacc])
                    tt("dve", dview(Lacc), sview(1), dview(Lacc), ALU.add, r=[t_ps[olb]], w=[t_Lacc])
            return f

        for i in range(LB):
            S(i)
        for i in range(16):
            r, m, subs, boff = geom(i)
            n = 128 * len(subs)
            sbk = sbanks[i % LB]
            j = i % 2
            bi = i // 2
            olb = 4 + bi % 2
            si = nxt("St3", 3)
            stt("dve", St[si][:, 0:n], rb[:, boff:boff + n], mult, ps[sbk][:, 0:n], ALU.mult, ALU.add,
                r=[t_rb, t_ps[sbk]], w=[t_St[si]])
            pj = nxt("Pt", 3)
            act(Pt[pj][:, 0:n], St[si][:, 0:n], AF.Exp, r=[t_St[si]], w=[t_Pt[pj]])
            if i + LB < 16:
                S(i + LB)
            for si_, ct in enumerate(subs):
                vt = r * ntile + ct
                mm(ps[olb][:, j * 256:j * 256 + 128], v[:, vt, :], Pt[pj][:, si_ * 128:(si_ + 1) * 128],
                   si_ == 0, si_ == len(subs) - 1, r=[tv[vt // 4], t_Pt[pj]], w=[t_ps[olb]])
            for si_, ct in enumerate(subs):
                mm(ps[olb][:, j * 256 + 128:j * 256 + 256], ones1[:], Pt[pj][:, si_ * 128:(si_ + 1) * 128],
                   si_ == 0, si_ == len(subs) - 1, r=[t_const, t_Pt[pj]], w=[t_ps[olb]])
            if j == 1:
                if deferred:
                    deferred.pop(0)()
                deferred.append(accum(i, olb))
            yield
        while deferred:
            deferred.pop(0)()
        if g == 2:
            for qd in range(4):
                recip_act(Lacc[:, qd * 512:(qd + 1) * 512], Lacc[:, qd * 512:(qd + 1) * 512], r=[t_Lacc], w=[t_Lacc])
            for qd in range(4):
                tt("pool", outT[:, 4 + h, qd * 512:(qd + 1) * 512], Oacc[:, qd * 512:(qd + 1) * 512],
                   Lacc[:, qd * 512:(qd + 1) * 512], ALU.mult, r=[t_Oacc, t_Lacc], w=[t_o[4 + h][qd]])
        yield

    def gen_attn(u, slot):
        kind, h, g = u
        if kind == "C":
            return gen_attn_C(h, slot), 9
        if kind == "A":
            return gen_attn_A(h, slot), 130 if h >= 2 else (110 if h == 1 else 60)
        return gen_attn_B(h, g, slot), 17

    prev = None
    last_mode = None
    for i, u in enumerate(units + [None]):
        mode = "A" if (prev is not None and prev[0][0] in ("A", "C")) else "B"
        if last_mode is not None and mode != last_mode:
            P.barrier()
        last_mode = mode
        gens = []
        if u is not None:
            gens.append((gen_proj(u, i % 3, mode), nproj(u)))
        if prev is not None:
            gens.append(gen_attn(prev[0], prev[1]))
        interleave(gens)
        prev = (u, i % 3) if u is not None else None

    if DEBUG:
        dma_sp(dbg_o[:, :, :], outT[:], "dbg1", r=[t for row in t_o for t in row])

    mixT = sb("mixT", [128, 8, T], BF16, 104448)
    W2_OFFS = [92160, 137216, 149504]
    w2g = [sb("w2g%d" % i, [128, 8, 512], BF16, W2_OFFS[i]) for i in range(3)]
    w2b = [sb("w2b%d" % i, [128, 4, 512], BF16, W2_OFFS[i] + 8192) for i in range(3)]
    gacc = sb("gacc", [128, 16, 512], F32, 161792)
    sg = [sb("sg%d" % i, [128, 512], F32, 194560 + 2048 * i) for i in range(2)]
    gtmp = [sb("gtmp%d" % i, [128, 512], F32, 198656 + 2048 * i) for i in range(2)]
    t_sg, t_gtmp = [Tk(), Tk()], [Tk(), Tk()]
    t_gacc = [Tk() for _ in range(16)]
    t_mix = [[Tk() for _ in range(4)] for _ in range(8)]

    jobs2 = []
    for cb in range(2):
        for g in range(3):
            jobs2.append([
                ((lambda s: s[0][:]),
                 w_gate.rearrange("(c p) n -> p c n", p=128)[:, :, g * 1024 + cb * 512:g * 1024 + (cb + 1) * 512]),
                ((lambda s: s[1][:]),
                 w_br[g * 512:(g + 1) * 512, :].rearrange("(c p) n -> p c n", p=128)[:, :, cb * 512:(cb + 1) * 512]),
            ])
    for cb in range(2):
        jobs2.append([((lambda s: s[0][:]),
                       w_out.rearrange("(c p) n -> p c n", p=128)[:, :, cb * 512:(cb + 1) * 512])])
    ring2 = Ring(P, list(zip(w2g, w2b)), jobs2, "w2", tks=[t_ra, Tk(), Tk()], init=1)
    P.barrier()
    ring2.advance()
    ring2.advance()

    jn = 0
    for cb in range(2):
        for g in range(3):
            (wg, wb_), wtk = ring2.get(jn)
            jn += 1
            for dci in range(4):
                dc = cb * 4 + dci
                for qd in range(4):
                    qs = slice(qd * 512, (qd + 1) * 512)
                    k = dci * 4 + qd
                    pg = nxt("pb", 2)
                    pbk = 2 + pg
                    for c in range(8):
                        mm(ps[pg][:], wg[:, c, dci * 128:(dci + 1) * 128], hT[:, c, qs], c == 0, c == 7,
                           r=[wtk, t_h[c][qd]], w=[t_ps[pg]])
                    for hh in range(4):
                        mm(ps[pbk][:], wb_[:, hh, dci * 128:(dci + 1) * 128], outT[:, g * 4 + hh, qs], hh == 0, hh == 3,
                           r=[wtk, t_o[g * 4 + hh][qd]], w=[t_ps[pbk]])
                    s_ = nxt("St", 2)
                    act(sg[s_][:], ps[pg][:], AF.Sigmoid, r=[t_ps[pg], t_cols], w=[t_sg[s_]],
                        bias=cols[:, 24 + g * 8 + dc:24 + g * 8 + dc + 1])
                    if g == 0:
                        tt("dve", gacc[:, k, :], ps[pbk][:], sg[s_][:], ALU.mult, r=[t_ps[pbk], t_sg[s_]], w=[t_gacc[k]])
                    else:
                        a = nxt("sq", 2)
                        tt("dve", gtmp[a][:], ps[pbk][:], sg[s_][:], ALU.mult, r=[t_ps[pbk], t_sg[s_]], w=[t_gtmp[a]])
                        if g == 1:
                            tt("dve", gacc[:, k, :], gacc[:, k, :], gtmp[a][:], ALU.add, r=[t_gtmp[a]], w=[t_gacc[k]])
                        else:
                            tt("dve", mixT[:, dc, qs], gacc[:, k, :], gtmp[a][:], ALU.add, r=[t_gtmp[a], t_gacc[k]],
                               w=[t_mix[dc][qd]])
            ring2.advance()

    P.barrier()
    x1T = sb("x1T", [128, 8, T], F32, 10240)
    t_x1 = [[Tk() for _ in range(4)] for _ in range(8)]
    for dc in range(8):
        dma_sp(x1T[:, dc, :], xT[dc * 128:(dc + 1) * 128, :], "x1_%d" % dc, w=t_x1[dc])
    W3_OFF = 153600
    NW3 = 5
    w3 = [sb("w3_%d" % i, [128, 4096], BF16, W3_OFF + 8192 * i) for i in range(NW3)]
    FB = [(0, 4), (4, 4), (8, 4), (12, 4), (16, 4), (20, 2)]
    jobs3 = []
    for hf in range(2):
        for (f0, nf) in FB:
            for wsrc in (w_fg, w_fu):
                jobs3.append([((lambda s, nf=nf: s[:, 0:8 * nf * 128].rearrange("p (c n) -> p c n", c=8)),
                               wsrc.rearrange("(c p) n -> p c n", p=128)[:, :, f0 * 128:(f0 + nf) * 128])])
        for dc in range(8):
            jobs3.append([((lambda s: s[:, 0:22 * 128].rearrange("p (c n) -> p c n", c=22)),
                           w_fd.rearrange("(c p) n -> p c n", p=128)[:, :, dc * 128:(dc + 1) * 128])])
    ring3 = Ring(P, w3, jobs3, "w3", tks=[ring2.tk[2], Tk(), Tk(), Tk(), Tk()])

    for cb in range(2):
        (wg, wb_), wtk = ring2.get(6 + cb)
        for dci in range(4):
            dc = cb * 4 + dci
            for qd in range(4):
                qs = slice(qd * 512, (qd + 1) * 512)
                pb = 4 + nxt("pb", 2)
                for c in range(8):
                    mm(ps[pb][:], wg[:, c, dci * 128:(dci + 1) * 128], mixT[:, c, qs], c == 0, c == 7,
                       r=[wtk, t_mix[c][qd]], w=[t_ps[pb]])
                tt("dve", x1T[:, dc, qs], ps[pb][:], x1T[:, dc, qs], ALU.add, r=[t_ps[pb]], w=[t_x1[dc][qd]])
        ring2.advance()

    if DEBUG:
        dma_sp(dbg_x1[:, :, :], x1T[:], "dbg2", r=[t for row in t_x1 for t in row])
    P.barrier()

    h2T = sb("h2T", [128, 8, T], BF16, 75776)
    actT = sb("actT", [128, 22, 1024], BF16, 108544)
    F_TMP = W3_OFF + 8192 * NW3
    sq3 = [sb("sq3_%d" % i, [128, 512], BF16, F_TMP + 1024 * i) for i in range(2)]
    std3 = sb("std3", [128, 512], F32, F_TMP + 2048)
    rstd3 = [sb("rstd3_%d" % i, [128, 512], F32, F_TMP + 4096 + 2048 * i) for i in range(2)]
    sil = [sb("sil%d" % i, [128, 512], F32, F_TMP + 8192 + 2048 * i) for i in range(2)]
    assert F_TMP + 12288 <= 212864
    t_sq3, t_std3, t_rstd3, t_sil = [Tk(), Tk()], Tk(), [Tk(), Tk()], [Tk(), Tk()]
    t_h2 = [[Tk() for _ in range(4)] for _ in range(8)]
    t_act = [[Tk() for _ in range(2)] for _ in range(22)]

    for qd in range(4):
        qs = slice(qd * 512, (qd + 1) * 512)
        for c in range(8):
            b = nxt("sq", 2)
            act(sq3[b][:], x1T[:, c, qs], AF.Square, r=[t_x1[c][qd]], w=[t_sq3[b]])
            mm(ps[6][:], ones1024[:], sq3[b][:], c == 0, c == 7, r=[t_sq3[b], t_const], w=[t_ps[6]])
        act(std3[:], ps[6][:], AF.Ln, r=[t_ps[6]], w=[t_std3], bias=EPS)
        b = nxt("rstd", 2)
        act(rstd3[b][:], std3[:], AF.Exp, r=[t_std3], w=[t_rstd3[b]], scale=-0.5)
        for c in range(8):
            eng = "dve"
            stt(eng, h2T[:, c, qs], x1T[:, c, qs], cols[:, 8 + c:9 + c], rstd3[b][:], ALU.mult, ALU.mult,
                r=[t_x1[c][qd], t_cols, t_rstd3[b]], w=[t_h2[c][qd]])

    t_out = Tk()
    out_evs = []
    jn = 0
    for hf in range(2):
        for (f0, nf) in FB:
            wgs, wgt = ring3.get(jn)
            wus, wut = ring3.get(jn + 1)
            jn += 2
            wgv = wgs[:, 0:8 * nf * 128].rearrange("p (c n) -> p c n", c=8)
            wuv = wus[:, 0:8 * nf * 128].rearrange("p (c n) -> p c n", c=8)
            for fi in range(nf):
                fc = f0 + fi
                for q2 in range(2):
                    qd = hf * 2 + q2
                    qs = slice(qd * 512, (qd + 1) * 512)
                    pg = nxt("pb", 2)
                    pu = 2 + pg
                    for c in range(8):
                        mm(ps[pg][:], wgv[:, c, fi * 128:(fi + 1) * 128], h2T[:, c, qs], c == 0, c == 7,
                           r=[wgt, t_h2[c][qd]], w=[t_ps[pg]])
                    for c in range(8):
                        mm(ps[pu][:], wuv[:, c, fi * 128:(fi + 1) * 128], h2T[:, c, qs], c == 0, c == 7,
                           r=[wut, t_h2[c][qd]], w=[t_ps[pu]])
                    s_ = nxt("St", 2)
                    act(sil[s_][:], ps[pg][:], AF.Silu, r=[t_ps[pg]], w=[t_sil[s_]])
                    tt("dve", actT[:, fc, q2 * 512:(q2 + 1) * 512], ps[pu][:], sil[s_][:], ALU.mult,
                       r=[t_ps[pu], t_sil[s_]], w=[t_act[fc][q2]])
            ring3.advance()
            ring3.advance()
        for dc in range(8):
            wds, wdt = ring3.get(jn)
            jn += 1
            wdv = wds[:, 0:22 * 128].rearrange("p (c n) -> p c n", c=22)
            for q2 in range(2):
                qd = hf * 2 + q2
                qs = slice(qd * 512, (qd + 1) * 512)
                pb = 4 + nxt("sbk", 2)
                for fc in range(22):
                    mm(ps[pb][:], wdv[:, fc, :], actT[:, fc, q2 * 512:(q2 + 1) * 512], fc == 0, fc == 21,
                       r=[wdt, t_act[fc][q2]], w=[t_ps[pb]])
                tt("dve", x1T[:, dc, qs], ps[pb][:], x1T[:, dc, qs], ALU.add, r=[t_ps[pb]], w=[t_x1[dc][qd]])
            ring3.advance()
            ev = dma_sp(yT[dc * 128:(dc + 1) * 128, hf * 1024:(hf + 1) * 1024], x1T[:, dc, hf * 1024:(hf + 1) * 1024],
                        "y%d" % (dc % 4), r=[t_x1[dc][hf * 2], t_x1[dc][hf * 2 + 1]])
            out_evs.append(ev)
    P.wait_only("sp", out_evs)
    P.emit()
    return nc


def _consts():
    p = np.arange(128, dtype=np.float32)[:, None]
    u = np.arange(3968, dtype=np.float32)[None, :]
    ra = np.abs(u - 1920.0 - p).astype(np.float32)
    j = np.arange(128, dtype=np.float32)[None, :]

    def tile(rel, ok):
        a = np.abs(rel)
        return np.where(ok & (a <= 64), a, BIG).astype(np.float32)

    allp = np.ones((128, 128), bool)
    Ft = tile(j - p, (p < 64) & allp)
    At = tile(j - p + 64, allp)
    Bt = tile(j - p - 64, allp)
    Lt = tile(j - p, (p >= 64) & allp)
    G2 = tile(j - p, allp)
    rb = np.concatenate([Ft, Bt, At, Bt, At, Lt, G2], axis=1)
    return np.ascontiguousarray(ra), np.ascontiguousarray(rb)


_NC_CACHE = {}


def kernel(x, mem, norm_mix, w_in, w_gate, b_gate, a_q_norm, a_k_norm, a_lambda_q1, a_lambda_k1,
           a_lambda_q2, a_lambda_k2, a_subln, b_q_norm, b_k_norm, mem_norm, w_mem_kv, c_q_norm,
           c_k_norm, w_branch, w_out, norm_ffn, w_ffn_gate, w_ffn_up, w_ffn_down):
    f = lambda a: np.ascontiguousarray(np.asarray(a, dtype=np.float32))
    x = np.asarray(x, dtype=np.float32)
    mem = np.asarray(mem, dtype=np.float32)
    ncore = x.shape[0]

    def c8(v):
        return np.asarray(v, np.float32).reshape(8, 128).T

    cols = np.zeros((128, 64), np.float32)
    cols[:, 0:8] = c8(norm_mix[0])
    cols[:, 8:16] = c8(norm_ffn[0])
    cols[:, 16:24] = c8(mem_norm[0])
    cols[:, 24:48] = np.asarray(b_gate[0], np.float32).reshape(24, 128).T
    cols[:, 48] = np.tile(np.asarray(a_q_norm[0], np.float32), 2)
    cols[:, 49] = np.tile(np.asarray(a_k_norm[0], np.float32), 2)
    cols[:, 50] = b_q_norm[0]
    cols[:, 51] = b_k_norm[0]
    cols[:, 52] = c_q_norm[0]
    cols[:, 53] = c_k_norm[0]
    cols[:, 54] = a_subln[0]
    lamv = np.broadcast_to(np.concatenate([np.asarray(v[0], np.float32) for v in
                                           (a_lambda_q1, a_lambda_k1, a_lambda_q2, a_lambda_k2)])[None, :], (128, 256))
    lamv = np.ascontiguousarray(lamv)
    ra, rb = _consts()
    shared = {
        "w_in": f(w_in[0]), "w_gate": f(w_gate[0]), "w_mem": f(w_mem_kv[0]),
        "w_br": f(np.asarray(w_branch[0]).reshape(1536, 1024)), "w_out": f(w_out[0]),
        "w_fg": f(w_ffn_gate[0]), "w_fu": f(w_ffn_up[0]), "w_fd": f(w_ffn_down[0]),
        "cols": cols, "lamv": lamv, "c_ra": ra, "c_rb": rb,
    }
    in_maps = []
    for b in range(ncore):
        m = dict(shared)
        m["xT"] = np.ascontiguousarray(x[b].T)
        m["memT"] = np.ascontiguousarray(mem[b].T)
        in_maps.append(m)
    if "nc" not in _NC_CACHE:
        _NC_CACHE["nc"] = build()
    res = run_bass_kernel_spmd(_NC_CACHE["nc"], in_maps, core_ids=list(range(ncore)))
    if DEBUG:
        _NC_CACHE["res"] = res
    out = np.stack([np.asarray(r["yT"]).T for r in res.results], axis=0)
    return np.ascontiguousarray(out.astype(np.float32))
```

```python
import contextlib
import numpy as np
import ml_dtypes
import concourse.bass as bass
import concourse.mybir as mybir
from concourse.bass_utils import run_bass_kernel_spmd

F32 = mybir.dt.float32
BF16 = mybir.dt.bfloat16
AF = mybir.ActivationFunctionType
ALU = mybir.AluOpType

SAME_ENGINE_SYNC = True
DEBUG = False
BASE = 16512
T = 2048
EPS = 1e-6
D_IN = 6656
LAMBDA_INIT = 0.2
BIG = 1.0e5
ALIBI_SKIP = 64.0


class Tk:
    __slots__ = ("w", "rs")

    def __init__(self):
        self.w = None
        self.rs = []


class Prog:
    ENG = ("pe", "act", "dve", "pool", "sp")

    def __init__(self, nc):
        self.nc = nc
        self.ops = {e: [] for e in self.ENG}
        self.dma_cnt = {}

    def _deps(self, r, w):
        deps = set()
        for t in r:
            if t.w is not None:
                deps.add(t.w)
        for t in w:
            if t.w is not None:
                deps.add(t.w)
            deps.update(t.rs)
        return deps

    def _commit(self, me, r, w):
        for t in r:
            t.rs.append(me)
        for t in w:
            t.w = me
            t.rs = []

    def _mark(self, deps, eng):
        for d in deps:
            if d[0] == "e":
                if d[1] == eng and (eng in ("pe", "sp") or not SAME_ENGINE_SYNC):
                    continue
                self.ops[d[1]][d[2]]["waited"] = True

    def op(self, eng, fn, r=(), w=()):
        deps = self._deps(r, w)
        me = ("e", eng, len(self.ops[eng]))
        self.ops[eng].append(dict(fn=fn, deps=deps, waited=False, dma=None))
        self._mark(deps, eng)
        self._commit(me, r, w)
        return me

    def dma(self, eng, fn, sem, r=(), w=()):
        deps = self._deps(r, w)
        n = self.dma_cnt.get(sem, 0) + 1
        self.dma_cnt[sem] = n
        me = ("d", sem, 16 * n)
        self.ops[eng].append(dict(fn=fn, deps=deps, waited=False, dma=sem))
        self._mark(deps, eng)
        self._commit(me, r, w)
        return me

    def wait_only(self, eng, deps):
        deps = set(deps)
        self.ops[eng].append(dict(fn=None, deps=deps, waited=False, dma=None))
        self._mark(deps, eng)

    def barrier(self, extra=()):
        evs = set(extra)
        for e in ("pe", "act", "dve", "pool"):
            for i in range(len(self.ops[e]) - 1, -1, -1):
                rec = self.ops[e][i]
                if rec["fn"] is not None and rec["dma"] is None:
                    evs.add(("e", e, i))
                    break
        for e in self.ENG:
            self.wait_only(e, evs)

    def emit(self):
        nc = self.nc
        for e in self.ENG:
            c = 0
            for rec in self.ops[e]:
                if rec["dma"] is None and rec["waited"] and rec["fn"] is not None:
                    c += 1
                rec["val"] = c
        with contextlib.ExitStack() as st:
            esem = {e: st.enter_context(nc.semaphore("s_" + e)) for e in self.ENG}
            dsem = {k: st.enter_context(nc.semaphore("d_%s" % (k,))) for k in self.dma_cnt}
            block = st.enter_context(nc.Block())
            ops = self.ops

            def run(e, engobj):
                waited = {}
                for rec in ops[e]:
                    need = {}
                    for d in rec["deps"]:
                        if d[0] == "e":
                            if d[1] == e and (e in ("pe", "sp") or not SAME_ENGINE_SYNC):
                                continue
                            key = ("e", d[1])
                            val = ops[d[1]][d[2]]["val"]
                            sem = esem[d[1]]
                        else:
                            key = ("d", d[1])
                            val = d[2]
                            sem = dsem[d[1]]
                        if waited.get(key, 0) >= val:
                            continue
                        if need.get(key, (None, 0))[1] < val:
                            need[key] = (sem, val)
                    for key, (sem, val) in need.items():
                        engobj.wait_ge(sem, val)
                        waited[key] = val
                    if rec["fn"] is None:
                        continue
                    ins = rec["fn"](engobj)
                    if rec["dma"] is not None:
                        ins.then_inc(dsem[rec["dma"]], 16)
                    elif rec["waited"]:
                        ins.then_inc(esem[e], 1)

            @block.tensor
            def _(eng):
                run("pe", eng)

            @block.scalar
            def _(eng):
                run("act", eng)

            @block.vector
            def _(eng):
                run("dve", eng)

            @block.gpsimd
            def _(eng):
                run("pool", eng)

            @block.sync
            def _(eng):
                run("sp", eng)


def interleave(gens):
    live = [[g, 0, max(1, n)] for g, n in gens]
    while live:
        live.sort(key=lambda x: x[1] / x[2])
        g = live[0]
        try:
            next(g[0])
            g[1] += 1
        except StopIteration:
            live.pop(0)


class Ring:
    def __init__(self, P, slots, jobs, name, tks=None, init=None):
        self.P, self.slots, self.jobs, self.name = P, slots, jobs, name
        self.tk = tks if tks is not None else [Tk() for _ in slots]
        self.issued = 0
        for _ in range(len(slots) if init is None else init):
            self.advance()

    def advance(self):
        j = self.issued
        if j >= len(self.jobs):
            return
        s = j % len(self.slots)
        for outf, in_ap in self.jobs[j]:
            out_ap = outf(self.slots[s])
            self.P.dma("pool", (lambda e, o=out_ap, i=in_ap: e.dma_start(out=o, in_=i)),
                       "%s%d" % (self.name, s), w=[self.tk[s]])
        self.issued += 1

    def get(self, j):
        assert j < self.issued, (self.name, j, self.issued)
        s = j % len(self.slots)
        return self.slots[s], self.tk[s]


def build():
    nc = bass.Bass("TRN2", target_bir_lowering=False)
    P = Prog(nc)

    def din(name, shape, dtype=F32):
        return nc.dram_tensor(name, shape, dtype, kind="ExternalInput").ap()

    xT = din("xT", [1024, T])
    memT = din("memT", [1024, 256])
    w_in = din("w_in", [1024, D_IN])
    w_gate = din("w_gate", [1024, 3072])
    w_mem = din("w_mem", [1024, 1024])
    w_br = din("w_br", [1536, 1024])
    w_out = din("w_out", [1024, 1024])
    w_fg = din("w_fg", [1024, 2816])
    w_fu = din("w_fu", [1024, 2816])
    w_fd = din("w_fd", [2816, 1024])
    cols_d = din("cols", [128, 64])
    lamv_d = din("lamv", [128, 256])
    ra_d = din("c_ra", [128, 3968])
    rb_d = din("c_rb", [128, 896])
    yT = nc.dram_tensor("yT", [1024, T], F32, kind="ExternalOutput").ap()
    if DEBUG:
        dbg_h = nc.dram_tensor("dbg_h", [128, 8, T], BF16, kind="ExternalOutput").ap()
        dbg_o = nc.dram_tensor("dbg_o", [128, 12, T], BF16, kind="ExternalOutput").ap()
        dbg_x1 = nc.dram_tensor("dbg_x1", [128, 8, T], F32, kind="ExternalOutput").ap()

    def sb(name, shape, dtype, off):
        return nc.alloc_sbuf_tensor_at(name, shape, dtype, offset=BASE + off)

    ps = [nc.alloc_psum_tensor("ps%d" % i, [128, 512], F32) for i in range(8)]
    t_ps = [Tk() for _ in range(8)]

    def mm(out, lhsT, rhs, start, stop, r, w):
        P.op("pe", lambda e: e.matmul(out, lhsT=lhsT, rhs=rhs, start=start, stop=stop), r=r, w=w)

    def act(out, in_, func, r, w, bias=None, scale=None):
        kw = {}
        if bias is not None:
            kw["bias"] = bias
        if scale is not None:
            kw["scale"] = scale
        P.op("act", lambda e: e.activation(out=out, in_=in_, func=func, **kw), r=r, w=w)

    def stt(eng, out, in0, scalar, in1, op0, op1, r, w):
        P.op(eng, lambda e: e.scalar_tensor_tensor(out=out, in0=in0, scalar=scalar, in1=in1, op0=op0, op1=op1),
             r=r, w=w)

    def tt(eng, out, in0, in1, op, r, w):
        P.op(eng, lambda e: e.tensor_tensor(out=out, in0=in0, in1=in1, op=op), r=r, w=w)

    def recip(out, in_, r, w):
        P.op("dve", lambda e: e.reciprocal(out=out, in_=in_), r=r, w=w)

    def recip_act(out, in_, r, w):
        act(out, in_, AF.Ln, r=r, w=w)
        act(out, out, AF.Exp, r=w, w=w, scale=-1.0)

    def tsmul(eng, out, in0, s, r, w):
        P.op(eng, lambda e: e.tensor_scalar_mul(out=out, in0=in0, scalar1=s), r=r, w=w)

    def dma_sp(out, in_, sem, r=(), w=()):
        return P.dma("sp", lambda e: e.dma_start(out=out, in_=in_), sem, r=r, w=w)

    ones128 = sb("ones128", [128, 128], BF16, 0)
    ones1024 = sb("ones1024", [128, 128], BF16, 256)
    onesblk = sb("onesblk", [128, 128], BF16, 512)
    ones1 = sb("ones1", [128, 128], BF16, 768)
    cols = sb("colsb", [128, 64], F32, 1024)
    lamv = sb("lamvb", [128, 256], F32, 1280)
    rb = sb("rbb", [128, 896], F32, 2304)
    ckT = sb("ckT", [128, 4, 256], BF16, 5888)
    cv = sb("cv", [128, 2, 512], BF16, 7936)
    t_const = Tk()
    t_cols = Tk()
    t_rb = Tk()
    t_ck = [Tk() for _ in range(4)]
    t_cv = [Tk() for _ in range(2)]

    P.op("pool", lambda e: e.memset(ones128[:], 1.0 / 128), w=[t_const])
    P.op("pool", lambda e: e.memset(ones1024[:], 1.0 / 1024), w=[t_const])
    P.op("pool", lambda e: e.memset(onesblk[:], 0.0), w=[t_const])
    P.op("pool", lambda e: e.memset(onesblk[0:64, 0:64], 1.0 / 64), w=[t_const])
    P.op("pool", lambda e: e.memset(onesblk[64:128, 64:128], 1.0 / 64), w=[t_const])
    P.op("pool", lambda e: e.memset(ones1[:], 1.0), w=[t_const])
    dma_sp(cols[:], cols_d[:, :], "c0", w=[t_cols])
    dma_sp(lamv[:], lamv_d[:, :], "c1", w=[t_cols])
    dma_sp(rb[:], rb_d[:, :], "c2", w=[t_rb])

    stt("dve", cols[:, 56:57], cols[:, 48:49], 64.0 ** -0.5, cols[:, 49:50], ALU.mult, ALU.mult, r=[t_cols], w=[t_cols])
    stt("dve", cols[:, 57:58], cols[:, 50:51], 128.0 ** -0.5, cols[:, 51:52], ALU.mult, ALU.mult, r=[t_cols], w=[t_cols])
    stt("dve", cols[:, 58:59], cols[:, 52:53], 128.0 ** -0.5, cols[:, 53:54], ALU.mult, ALU.mult, r=[t_cols], w=[t_cols])
    tsmul("dve", cols[:, 59:60], cols[:, 54:55], 1.0 - LAMBDA_INIT, r=[t_cols], w=[t_cols])
    tt("dve", lamv[:, 0:64], lamv[:, 0:64], lamv[:, 64:128], ALU.mult, r=[t_cols], w=[t_cols])
    tt("dve", lamv[:, 128:192], lamv[:, 128:192], lamv[:, 192:256], ALU.mult, r=[t_cols], w=[t_cols])
    P.op("dve", lambda e: e.reduce_sum(out=cols[:, 61:62], in_=lamv[:, 0:64], axis=mybir.AxisListType.X), r=[t_cols], w=[t_cols])
    P.op("dve", lambda e: e.reduce_sum(out=cols[:, 62:63], in_=lamv[:, 128:192], axis=mybir.AxisListType.X), r=[t_cols], w=[t_cols])
    act(cols[:, 61:63], cols[:, 61:63], AF.Exp, r=[t_cols], w=[t_cols])
    stt("dve", cols[:, 60:61], cols[:, 62:63], -LAMBDA_INIT, cols[:, 61:62], ALU.add, ALU.subtract, r=[t_cols], w=[t_cols])

    hT = sb("hT", [128, 8, T], BF16, 10240)
    outT = sb("outT", [128, 12, T], BF16, 43008)
    ra = sb("ra", [128, 3968], F32, 92160)
    W1_OFF = 108032
    NW1 = 10
    w1slots = [sb("w1_%d" % i, [128, 8, 128], BF16, W1_OFF + 2048 * i) for i in range(NW1)]
    U_OFF = W1_OFF + 2048 * NW1
    USZ = 13312
    uq = [sb("uq%d" % i, [128, T], BF16, U_OFF + USZ * i) for i in range(3)]
    uk = [sb("uk%d" % i, [128, T], BF16, U_OFF + USZ * i + 4096) for i in range(3)]
    uv = [sb("uv%d" % i, [128, 20, 128], BF16, U_OFF + USZ * i + 8192) for i in range(3)]
    ACC_OFF = U_OFF + 3 * USZ
    Oacc = sb("Oacc", [128, T], F32, ACC_OFF)
    Lacc = sb("Lacc", [128, T], F32, ACC_OFF + 8192)
    TMP = ACC_OFF + 16384
    sq = [sb("sq%d" % i, [128, 512], BF16, TMP + 1024 * i) for i in range(2)]
    rstd = [sb("rstd%d" % i, [128, 512], F32, TMP + 2048 + 2048 * i) for i in range(2)]
    std = sb("std", [128, 512], F32, TMP + 6144)
    St = [sb("St%d" % i, [128, 512], F32, TMP + 8192 + 2048 * i) for i in range(2)]
    Pt = [sb("Pt%d" % i, [128, 512], BF16, TMP + 12288 + 1024 * i) for i in range(3)]
    fin = [sb("fin%d" % i, [128, 512], F32, TMP + 15360 + 2048 * i) for i in range(3)]
    sqA = sb("sqA", [128, 512], BF16, TMP + 15360 + 2048)
    rawg = [sb("rawg%d" % i, [128, 512], F32, TMP + 21504 + 2048 * i) for i in range(2)]
    t_rawg = [Tk(), Tk()]
    St.append(sb("St2", [128, 512], F32, TMP + 25600))
    assert TMP + 25600 + 2048 <= 212864
    xs = [sb("xs%d" % i, [128, 8, 512], F32, U_OFF + 16384 * i) for i in range(2)]
    ms = sb("ms", [128, 8, 256], F32, U_OFF + 32768)
    mT = sb("mT", [128, 8, 256], BF16, U_OFF + 40960)

    t_h = [[Tk() for _ in range(4)] for _ in range(8)]
    t_o = [[Tk() for _ in range(4)] for _ in range(12)]
    t_ra = Tk()
    t_uq = [[Tk() for _ in range(4)] for _ in range(3)]
    t_uk = [[Tk() for _ in range(4)] for _ in range(3)]
    t_uv = [[Tk() for _ in range(5)] for _ in range(3)]
    t_Oacc, t_Lacc = Tk(), Tk()
    t_sq = [Tk(), Tk()]
    t_rstd = [Tk(), Tk()]
    t_std = Tk()
    t_St = [Tk(), Tk(), Tk()]
    t_Pt = [Tk(), Tk(), Tk()]
    t_fin = [Tk(), Tk(), Tk()]
    t_xs = [Tk(), Tk()]
    t_ms, t_mT = Tk(), [Tk() for _ in range(8)]

    rr = {"sq": 0, "rstd": 0, "St": 0, "Pt": 0, "pb": 0, "sbk": 0, "rawg": 0, "St3": 0}

    def nxt(key, n):
        v = rr[key]
        rr[key] = (v + 1) % n
        return v

    dma_sp(ra[:], ra_d[:, :], "c3", w=[t_ra])

    def wsl(w_ap, col0, ncol=128):
        src = w_ap.rearrange("(c p) n -> p c n", p=128)[:, :, col0:col0 + ncol]
        return [((lambda s: s[:]), src)]

    jobs1 = []
    for h in range(4):
        jobs1.append(wsl(w_mem, h * 128))
    for h in range(4):
        jobs1.append(wsl(w_mem, 512 + h * 128))
    units = []
    for h in range(4):
        units.append(("C", h, 0))
    for h in range(4):
        units.append(("A", h, 0))
    for h in range(4):
        for g in range(3):
            units.append(("B", h, g))
    ujob = {}
    for u in units:
        kind, h, g = u
        ujob[u] = len(jobs1)
        if kind == "C":
            jobs1.append(wsl(w_in, 6144 + h * 128))
        elif kind == "A":
            jobs1.append(wsl(w_in, 0 + h * 128))
            jobs1.append(wsl(w_in, 512 + h * 128))
            jobs1.append(wsl(w_in, 1024 + h * 128))
        else:
            jobs1.append(wsl(w_in, 1536 + g * 512 + h * 128))
            jobs1.append(wsl(w_in, 3072 + g * 512 + h * 128))
            jobs1.append(wsl(w_in, 4608 + g * 512 + h * 128))
    ring1 = Ring(P, w1slots, jobs1, "w1")

    def rms_in(src_d, stage, t_stage, sem, ncol, gcol0, dst_fn, t_dst_fn, col_lo):
        dma_sp(stage[:, :, 0:ncol], src_d.rearrange("(c p) t -> p c t", p=128)[:, :, col_lo:col_lo + ncol], sem, w=[t_stage])
        for c in range(8):
            b = nxt("sq", 2)
            act(sq[b][:, 0:ncol], stage[:, c, 0:ncol], AF.Square, r=[t_stage], w=[t_sq[b]])
            mm(ps[2][:, 0:ncol], ones1024[:], sq[b][:, 0:ncol], c == 0, c == 7, r=[t_sq[b], t_const], w=[t_ps[2]])
        act(std[:, 0:ncol], ps[2][:, 0:ncol], AF.Ln, r=[t_ps[2]], w=[t_std], bias=EPS)
        b = nxt("rstd", 2)
        act(rstd[b][:, 0:ncol], std[:, 0:ncol], AF.Exp, r=[t_std], w=[t_rstd[b]], scale=-0.5)
        for c in range(8):
            eng = "dve"
            stt(eng, dst_fn(c), stage[:, c, 0:ncol], cols[:, gcol0 + c:gcol0 + c + 1], rstd[b][:, 0:ncol],
                ALU.mult, ALU.mult, r=[t_stage, t_cols, t_rstd[b]], w=[t_dst_fn(c)])

    def x_quad(qd):
        s = qd % 2
        rms_in(xT, xs[s], t_xs[s], "xs%d" % s, 512, 0,
               (lambda c, qd=qd: hT[:, c, qd * 512:(qd + 1) * 512]), (lambda c, qd=qd: t_h[c][qd]), qd * 512)

    rms_in(memT, ms, t_ms, "ms", 256, 16, (lambda c: mT[:, c, :]), (lambda c: t_mT[c]), 0)
    x_quad(0)
    x_quad(1)

    SSB = 1
    FINAL_ENG = "pool"
    FINAL_ENG_D = "pool"

    def norm_sq(psb, ncol):
        b = nxt("sq", 2)
        act(sq[b][:, 0:ncol], ps[psb][:, 0:ncol], AF.Square, r=[t_ps[psb]], w=[t_sq[b]])
        return b

    def norm_rest(b, psb, ncol, ones_ap, gcol, out_ap, in_view, w_tk):
        mm(ps[SSB][:, 0:ncol], ones_ap, sq[b][:, 0:ncol], True, True, r=[t_sq[b], t_const], w=[t_ps[SSB]])
        act(std[:, 0:ncol], ps[SSB][:, 0:ncol], AF.Ln, r=[t_ps[SSB]], w=[t_std], bias=EPS)
        rb_ = nxt("rstd", 2)
        act(rstd[rb_][:, 0:ncol], std[:, 0:ncol], AF.Exp, r=[t_std], w=[t_rstd[rb_]], scale=-0.5)
        if gcol is None:
            tt("dve", out_ap, in_view(ps[psb][:, 0:ncol]), in_view(rstd[rb_][:, 0:ncol]), ALU.mult,
               r=[t_ps[psb], t_rstd[rb_]], w=w_tk)
        else:
            stt("dve", out_ap, in_view(ps[psb][:, 0:ncol]), gcol, in_view(rstd[rb_][:, 0:ncol]), ALU.mult, ALU.mult,
                r=[t_ps[psb], t_rstd[rb_], t_cols], w=w_tk)

    def norm_tail(psb, ncol, ones_ap, gcol, out_ap, in_view, r_extra, w_tk):
        b = norm_sq(psb, ncol)
        norm_rest(b, psb, ncol, ones_ap, gcol, out_ap, in_view, w_tk)

    def proj_qk(wj, dst, t_dst, ones_ap, gcol, Dil, rawbanks, ssb):
        wslot, wtk = ring1.get(wj)
        n = 512 // Dil

        def pv(a):
            return a if Dil == 1 else a.rearrange("p (i r) -> p r i", r=Dil)

        def cv_(a):
            return a if Dil == 1 else a.rearrange("p (r i) -> p r i", r=Dil)

        def MM(qd):
            pb = rawbanks[qd % len(rawbanks)]
            for c in range(8):
                mm(ps[pb][:], wslot[:, c, :], hT[:, c, qd * 512:(qd + 1) * 512], c == 0, c == 7,
                   r=[wtk, t_h[c][qd]], w=[t_ps[pb]])
            if qd == 3:
                ring1.advance()

        def EV(qd):
            pb = rawbanks[qd % len(rawbanks)]
            b = norm_sq(pb, 512)
            rg = nxt("rawg", 2)
            if gcol is None:
                P.op("dve", (lambda e, rg=rg: e.tensor_copy(out=cv_(rawg[rg][:]), in_=pv(ps[pb][:]))),
                     r=[t_ps[pb], t_sq[b]], w=[t_rawg[rg]])
            else:
                tsmul("dve", cv_(rawg[rg][:]), pv(ps[pb][:]), gcol, r=[t_ps[pb], t_cols, t_sq[b]], w=[t_rawg[rg]])
            return b, rg

        def REST(qd, b, rg):
            mm(ps[ssb][:], ones_ap, sq[b][:], True, True, r=[t_sq[b], t_const], w=[t_ps[ssb]])
            act(std[:], ps[ssb][:], AF.Ln, r=[t_ps[ssb]], w=[t_std], bias=EPS)
            rb_ = nxt("rstd", 2)
            act(cv_(rstd[rb_][:]), pv(std[:]), AF.Exp, r=[t_std], w=[t_rstd[rb_]], scale=-0.5)
            if Dil == 1:
                out_ap = dst[:, qd * 512:(qd + 1) * 512]
            else:
                out_ap = dst[:, :].rearrange("p (r i) -> p r i", r=Dil)[:, :, qd * n:(qd + 1) * n]
            tt(FINAL_ENG if Dil == 1 else FINAL_ENG_D, out_ap, cv_(rawg[rg][:]), cv_(rstd[rb_][:]), ALU.mult,
               r=[t_rawg[rg], t_rstd[rb_]], w=[t_dst[qd]])

        if len(rawbanks) == 1:
            for qd in range(4):
                MM(qd)
                yield
                b, rg = EV(qd)
                yield
                REST(qd, b, rg)
            yield
        else:
            MM(0)
            yield
            MM(1)
            yield
            e0 = EV(0)
            yield
            MM(2)
            yield
            e1 = EV(1)
            REST(0, *e0)
            yield
            MM(3)
            yield
            e2 = EV(2)
            REST(1, *e1)
            yield
            e3 = EV(3)
            REST(2, *e2)
            yield
            REST(3, *e3)
            yield

    def proj_v(wj, vbuf, t_vb, tiles, vbanks):
        wslot, wtk = ring1.get(wj)
        ngrp = (len(tiles) + 3) // 4

        def MMG(gi):
            g0 = gi * 4
            grp = tiles[g0:g0 + 4]
            vb = vbanks[gi % len(vbanks)]
            for j, (start, step) in enumerate(grp):
                hi = start + 127 * step
                qs = range(start // 512, hi // 512 + 1)
                for c in range(8):
                    mm(ps[vb][:, j * 128:(j + 1) * 128], hT[:, c, start:hi + 1:step], wslot[:, c, :], c == 0, c == 7,
                       r=[wtk] + [t_h[c][q] for q in qs], w=[t_ps[vb]])
            if gi == ngrp - 1:
                ring1.advance()

        def CP(gi):
            g0 = gi * 4
            n = len(tiles[g0:g0 + 4])
            vb = vbanks[gi % len(vbanks)]
            act(vbuf[:, g0:g0 + n, :], ps[vb][:, 0:n * 128].rearrange("p (a b) -> p a b", b=128), AF.Copy,
                r=[t_ps[vb]], w=[t_vb[gi]])

        if len(vbanks) == 1:
            for gi in range(ngrp):
                MMG(gi)
                yield
                CP(gi)
                yield
        else:
            MMG(0)
            yield
            for gi in range(1, ngrp):
                MMG(gi)
                CP(gi - 1)
                yield
            CP(ngrp - 1)
            yield

    def kv_unit():
        for h in range(4):
            wslot, wtk = ring1.get(h)
            for c in range(8):
                mm(ps[0][:, 0:256], wslot[:, c, :], mT[:, c, :], c == 0, c == 7, r=[wtk, t_mT[c]], w=[t_ps[0]])
            ring1.advance()
            norm_tail(0, 256, ones128[:], None, ckT[:, h, :], (lambda a: a), [], [t_ck[h]])
        for kt in range(2):
            for h in range(4):
                wslot, wtk = ring1.get(4 + h)
                for c in range(8):
                    mm(ps[3][:, h * 128:(h + 1) * 128], mT[:, c, kt * 128:(kt + 1) * 128], wslot[:, c, :], c == 0, c == 7,
                       r=[wtk, t_mT[c]], w=[t_ps[3]])
                if kt == 1:
                    ring1.advance()
            act(cv[:, kt, :], ps[3][:], AF.Copy, r=[t_ps[3]], w=[t_cv[kt]])

    kv_unit()
    x_quad(2)
    x_quad(3)
    if DEBUG:
        dma_sp(dbg_h[:, :, :], hT[:], "dbg0", r=[t for row in t_h for t in row])
    P.barrier()

    SLOPE_A = [2.0 ** (-2.0 * (h + 1)) for h in range(4)]
    SLOPE_B = [[2.0 ** (-8.0 * (4 * g + h + 1) / 12.0) for h in range(4)] for g in range(3)]
    DILS = [1, 4, 16]

    def b_geom(g):
        Dil = DILS[g]
        Ls = T // Dil
        nq = Ls // 128
        ntile = nq + 1 if nq > 1 else 1

        def ws(ct):
            if nq == 1 or ct == 0:
                return 0
            if ct == nq:
                return Ls - 128
            return 128 * ct - 64
        return Dil, Ls, nq, ntile, ws

    def gen_proj(u, slot, mode):
        kind, h, g = u
        j0 = ujob[u]
        rawbanks, ssb = ([0], 1) if mode == "A" else ([0, 1], 2)
        vbanks = rawbanks
        if kind == "C":
            yield from proj_qk(j0, uq[slot], t_uq[slot], ones128[:], cols[:, 58:59], 1, rawbanks, ssb)
        elif kind == "A":
            yield from proj_qk(j0, uq[slot], t_uq[slot], onesblk[:], cols[:, 56:57], 1, rawbanks, ssb)
            yield from proj_qk(j0 + 1, uk[slot], t_uk[slot], onesblk[:], None, 1, rawbanks, ssb)
            yield from proj_v(j0 + 2, uv[slot], t_uv[slot], [(kt * 128, 1) for kt in range(16)], vbanks)
        else:
            Dil, Ls, nq, ntile, ws = b_geom(g)
            yield from proj_qk(j0, uq[slot], t_uq[slot], ones128[:], cols[:, 57:58], Dil, rawbanks, ssb)
            yield from proj_qk(j0 + 1, uk[slot], t_uk[slot], ones128[:], None, Dil, rawbanks, ssb)
            tiles = [(Dil * ws(ct) + r, Dil) for r in range(Dil) for ct in range(ntile)]
            yield from proj_v(j0 + 2, uv[slot], t_uv[slot], tiles, vbanks)

    def nproj(u):
        kind, h, g = u
        if kind == "C":
            return 9
        if kind == "A":
            return 9 + 9 + 8
        _, _, _, ntile, _ = b_geom(g)
        return 18 + 1 + ((DILS[g] * ntile + 3) // 4)

    LA = 2
    t_s3 = [Tk(), Tk()]

    def gen_attn_C(h, slot):
        q, tq = uq[slot], t_uq[slot]
        tiles = [(qd, kt) for qd in range(4) for kt in range(2)]
        deferred = []

        def S(i):
            qd, kt = tiles[i]
            sbk = 2 + i % 2
            mm(ps[sbk][:], ckT[:, h, kt * 128:(kt + 1) * 128], q[:, qd * 512:(qd + 1) * 512], True, True,
               r=[t_ck[h], tq[qd]], w=[t_ps[sbk]])

        def fin_c(qd, ob, lb):
            def f():
                recip_act(fin[0][:], ps[lb][:], r=[t_ps[lb]], w=[t_fin[0]])
                tt("dve", outT[:, 8 + h, qd * 512:(qd + 1) * 512], ps[ob][:], fin[0][:], ALU.mult,
                   r=[t_ps[ob], t_fin[0]], w=[t_o[8 + h][qd]])
            return f

        for i in range(LA):
            S(i)
        for j, (qd, kt) in enumerate(tiles):
            sbk = 2 + j % 2
            ob, lb = 4 + qd % 2, 6 + qd % 2
            pj = nxt("Pt", 3)
            act(Pt[pj][:], ps[sbk][:], AF.Exp, r=[t_ps[sbk]], w=[t_Pt[pj]])
            if j + LA < len(tiles):
                S(j + LA)
            mm(ps[ob][:], cv[:, kt, h * 128:(h + 1) * 128], Pt[pj][:], kt == 0, kt == 1, r=[t_cv[kt], t_Pt[pj]], w=[t_ps[ob]])
            mm(ps[lb][:], ones1[:], Pt[pj][:], kt == 0, kt == 1, r=[t_const, t_Pt[pj]], w=[t_ps[lb]])
            if kt == 0 and deferred:
                deferred.pop(0)()
            if kt == 1:
                deferred.append(fin_c(qd, ob, lb))
            yield
        while deferred:
            deferred.pop(0)()
        yield

    def gen_attn_A(h, slot):
        q, tq, k, tk_, v, tv = uq[slot], t_uq[slot], uk[slot], t_uk[slot], uv[slot], t_uv[slot]
        def kts_for(qd):
            out = []
            for kt in range(16):
                gap = max(qd * 512 - (kt * 128 + 127), kt * 128 - (qd * 512 + 511), 0)
                if SLOPE_A[h] * gap < ALIBI_SKIP:
                    out.append(kt)
            return out

        tiles = []
        for qd in range(4):
            for cmp in range(2):
                ks = kts_for(qd)
                for idx, kt in enumerate(ks):
                    tiles.append((qd, cmp, kt, idx, len(ks)))
        deferred = []

        def S(i):
            qd, cmp, kt = tiles[i][:3]
            lo, hi = cmp * 64, cmp * 64 + 64
            sbk = 2 + i % 2
            mm(ps[sbk][:], k[lo:hi, kt * 128:(kt + 1) * 128], q[lo:hi, qd * 512:(qd + 1) * 512], True, True,
               r=[tk_[kt // 4], tq[qd]], w=[t_ps[sbk]])

        def fin_a(qd, cmp, ob, lb):
            st = {}

            def f1():
                act(fin[0][:], ps[lb][:], AF.Ln, r=[t_ps[lb]], w=[t_fin[0]])
                fo = 1 if cmp == 0 else 2
                P.op("dve", (lambda e: e.tensor_copy(out=fin[fo][:], in_=ps[ob][:])), r=[t_ps[ob]], w=[t_fin[fo]])
                act(fin[0][:], fin[0][:], AF.Exp, r=[t_fin[0]], w=[t_fin[0]], scale=-1.0)
                tt("pool", fin[fo][:], fin[fo][:], fin[0][:], ALU.mult, r=[t_fin[0]], w=[t_fin[fo]])

            def f2():
                stt("dve", fin[2][:], fin[2][:], cols[:, 60:61], fin[1][:], ALU.mult, ALU.add,
                    r=[t_fin[1], t_cols], w=[t_fin[2]])
                act(sqA[:], fin[2][:], AF.Square, r=[t_fin[2]], w=[t_fin[1]])

            def f3():
                mm(ps[SSB][:], ones128[:], sqA[:], True, True, r=[t_fin[1], t_const], w=[t_ps[SSB]])
                act(std[:], ps[SSB][:], AF.Ln, r=[t_ps[SSB]], w=[t_std], bias=EPS)
                act(fin[0][:], std[:], AF.Exp, r=[t_std], w=[t_fin[0]], scale=-0.5)

            def f4():
                stt("dve", outT[:, h, qd * 512:(qd + 1) * 512], fin[2][:], cols[:, 59:60], fin[0][:], ALU.mult, ALU.mult,
                    r=[t_fin[2], t_cols, t_fin[0]], w=[t_o[h][qd]])
            return [f1] if cmp == 0 else [f1, f2, f3, f4]

        for i in range(LA):
            S(i)
        for j, (qd, cmp, kt, idx, ng) in enumerate(tiles):
            sbk = 2 + j % 2
            gi = qd * 2 + cmp
            ob, lb = 4 + gi % 2, 6 + gi % 2
            off = qd * 512 - kt * 128 + 1920
            si = nxt("St3", 3)
            stt("dve", St[si][:], ra[:, off:off + 512], -SLOPE_A[h], ps[sbk][:], ALU.mult, ALU.add,
                r=[t_ra, t_ps[sbk]], w=[t_St[si]])
            pj = nxt("Pt", 3)
            act(Pt[pj][:], St[si][:], AF.Exp, r=[t_St[si]], w=[t_Pt[pj]])
            if j + LA < len(tiles):
                S(j + LA)
            mm(ps[ob][:], v[:, kt, :], Pt[pj][:], idx == 0, idx == ng - 1, r=[tv[kt // 4], t_Pt[pj]], w=[t_ps[ob]])
            mm(ps[lb][:], ones1[:], Pt[pj][:], idx == 0, idx == ng - 1, r=[t_const, t_Pt[pj]], w=[t_ps[lb]])
            pops = (2, 6, 9, 12) if ng >= 14 else ((2, 5, 8, 10) if ng >= 12 else (1, 2, 4, 5))
            if idx in pops and deferred:
                deferred.pop(0)()
            if idx == ng - 1:
                assert len(deferred) == 0, "previous finalize not fully emitted"
                deferred.extend(fin_a(qd, cmp, ob, lb))
            yield
        while deferred:
            deferred.pop(0)()
            yield

    def gen_attn_B(h, g, slot):
        sbanks = [3, 6, 7]
        LB = 3
        q, tq, k, tk_, v, tv = uq[slot], t_uq[slot], uk[slot], t_uk[slot], uv[slot], t_uv[slot]
        Dil, Ls, nq, ntile, ws = b_geom(g)
        mult = -SLOPE_B[g][h] * Dil
        qtiles = [(r, m) for r in range(Dil) for m in range(nq)]
        deferred = []

        def geom(i):
            r, m = qtiles[i]
            if nq == 1:
                return r, m, [0], 768
            return r, m, [m, m + 1], (0 if m == 0 else (512 if m == nq - 1 else 256))

        def S(i):
            r, m, subs, boff = geom(i)
            sbk = sbanks[i % LB]
            qap = q[:, r * Ls + m * 128:r * Ls + (m + 1) * 128]
            for si_, ct in enumerate(subs):
                k0 = r * Ls + ws(ct)
                mm(ps[sbk][:, si_ * 128:(si_ + 1) * 128], k[:, k0:k0 + 128], qap, True, True,
                   r=tk_ + tq, w=[t_ps[sbk]])

        def accum(i1, olb):
            r0, m0 = qtiles[i1 - 1]

            def dview(a):
                if nq == 1:
                    return a[:, :].rearrange("p (e r) -> p r e", r=16)[:, r0:r0 + 2, :]
                return a[:, :].rearrange("p (m e r) -> p r m e", r=Dil, e=128)[:, r0, m0:m0 + 2, :]

            def sview(c):
                return ps[olb][:].rearrange("p (t c e) -> p t c e", t=2, c=2)[:, :, c, :]

            def f():
                if g == 0:
                    act(dview(Oacc), sview(0), AF.Copy, r=[t_ps[olb]], w=[t_Oacc])
                    act(dview(Lacc), sview(1), AF.Copy, r=[t_ps[olb]], w=[t_Lacc])
                else:
                    tt("dve", dview(Oacc), sview(0), dview(Oacc), ALU.add, r=[t_ps[olb]], w=[t_Oacc])
                    tt("dve", dview(Lacc), sview(1), dview(Lacc), ALU.add, r=[t_ps[olb]], w=[t_Lacc])
            return f

        for i in range(LB):
            S(i)
        for i in range(16):
            r, m, subs, boff = geom(i)
            n = 128 * len(subs)
            sbk = sbanks[i % LB]
            j = i % 2
            bi = i // 2
            olb = 4 + bi % 2
            si = nxt("St3", 3)
            stt("dve", St[si][:, 0:n], rb[:, boff:boff + n], mult, ps[sbk][:, 0:n], ALU.mult, ALU.add,
                r=[t_rb, t_ps[sbk]], w=[t_St[si]])
            pj = nxt("Pt", 3)
            act(Pt[pj][:, 0:n], St[si][:, 0:n], AF.Exp, r=[t_St[si]], w=[t_Pt[pj]])
            if i + LB < 16:
                S(i + LB)
            for si_, ct in enumerate(subs):
                vt = r * ntile + ct
                mm(ps[olb][:, j * 256:j * 256 + 128], v[:, vt, :], Pt[pj][:, si_ * 128:(si_ + 1) * 128],
                   si_ == 0, si_ == len(subs) - 1, r=[tv[vt // 4], t_Pt[pj]], w=[t_ps[olb]])
            for si_, ct in enumerate(subs):
                mm(ps[olb][:, j * 256 + 128:j * 256 + 256], ones1[:], Pt[pj][:, si_ * 128:(si_ + 1) * 128],
                   si_ == 0, si_ == len(subs) - 1, r=[t_const, t_Pt[pj]], w=[t_ps[olb]])
            if j == 1:
                if deferred:
                    deferred.pop(0)()
                deferred.append(accum(i, olb))
            yield
        while deferred:
            deferred.pop(0)()
        if g == 2:
            for qd in range(4):
                recip_act(Lacc[:, qd * 512:(qd + 1) * 512], Lacc[:, qd * 512:(qd + 1) * 512], r=[t_Lacc], w=[t_Lacc])
            for qd in range(4):
                tt("pool", outT[:, 4 + h, qd * 512:(qd + 1) * 512], Oacc[:, qd * 512:(qd + 1) * 512],
                   Lacc[:, qd * 512:(qd + 1) * 512], ALU.mult, r=[t_Oacc, t_Lacc], w=[t_o[4 + h][qd]])
        yield

    def gen_attn(u, slot):
        kind, h, g = u
        if kind == "C":
            return gen_attn_C(h, slot), 9
        if kind == "A":
            return gen_attn_A(h, slot), 130 if h >= 2 else (110 if h == 1 else 60)
        return gen_attn_B(h, g, slot), 17

    prev = None
    last_mode = None
    for i, u in enumerate(units + [None]):
        mode = "A" if (prev is not None and prev[0][0] in ("A", "C")) else "B"
        if last_mode is not None and mode != last_mode:
            P.barrier()
        last_mode = mode
        gens = []
        if u is not None:
            gens.append((gen_proj(u, i % 3, mode), nproj(u)))
        if prev is not None:
            gens.append(gen_attn(prev[0], prev[1]))
        interleave(gens)
        prev = (u, i % 3) if u is not None else None

    if DEBUG:
        dma_sp(dbg_o[:, :, :], outT[:], "dbg1", r=[t for row in t_o for t in row])

    mixT = sb("mixT", [128, 8, T], BF16, 104448)
    W2_OFFS = [92160, 137216, 149504]
    w2g = [sb("w2g%d" % i, [128, 8, 512], BF16, W2_OFFS[i]) for i in range(3)]
    w2b = [sb("w2b%d" % i, [128, 4, 512], BF16, W2_OFFS[i] + 8192) for i in range(3)]
    gacc = sb("gacc", [128, 16, 512], F32, 161792)
    sg = [sb("sg%d" % i, [128, 512], F32, 194560 + 2048 * i) for i in range(2)]
    gtmp = [sb("gtmp%d" % i, [128, 512], F32, 198656 + 2048 * i) for i in range(2)]
    t_sg, t_gtmp = [Tk(), Tk()], [Tk(), Tk()]
    t_gacc = [Tk() for _ in range(16)]
    t_mix = [[Tk() for _ in range(4)] for _ in range(8)]

    jobs2 = []
    for cb in range(2):
        for g in range(3):
            jobs2.append([
                ((lambda s: s[0][:]),
                 w_gate.rearrange("(c p) n -> p c n", p=128)[:, :, g * 1024 + cb * 512:g * 1024 + (cb + 1) * 512]),
                ((lambda s: s[1][:]),
                 w_br[g * 512:(g + 1) * 512, :].rearrange("(c p) n -> p c n", p=128)[:, :, cb * 512:(cb + 1) * 512]),
            ])
    for cb in range(2):
        jobs2.append([((lambda s: s[0][:]),
                       w_out.rearrange("(c p) n -> p c n", p=128)[:, :, cb * 512:(cb + 1) * 512])])
    ring2 = Ring(P, list(zip(w2g, w2b)), jobs2, "w2", tks=[t_ra, Tk(), Tk()], init=1)
    P.barrier()
    ring2.advance()
    ring2.advance()

    jn = 0
    for cb in range(2):
        for g in range(3):
            (wg, wb_), wtk = ring2.get(jn)
            jn += 1
            for dci in range(4):
                dc = cb * 4 + dci
                for qd in range(4):
                    qs = slice(qd * 512, (qd + 1) * 512)
                    k = dci * 4 + qd
                    pg = nxt("pb", 2)
                    pbk = 2 + pg
                    for c in range(8):
                        mm(ps[pg][:], wg[:, c, dci * 128:(dci + 1) * 128], hT[:, c, qs], c == 0, c == 7,
                           r=[wtk, t_h[c][qd]], w=[t_ps[pg]])
                    for hh in range(4):
                        mm(ps[pbk][:], wb_[:, hh, dci * 128:(dci + 1) * 128], outT[:, g * 4 + hh, qs], hh == 0, hh == 3,
                           r=[wtk, t_o[g * 4 + hh][qd]], w=[t_ps[pbk]])
                    s_ = nxt("St", 2)
                    act(sg[s_][:], ps[pg][:], AF.Sigmoid, r=[t_ps[pg], t_cols], w=[t_sg[s_]],
                        bias=cols[:, 24 + g * 8 + dc:24 + g * 8 + dc + 1])
                    if g == 0:
                        tt("dve", gacc[:, k, :], ps[pbk][:], sg[s_][:], ALU.mult, r=[t_ps[pbk], t_sg[s_]], w=[t_gacc[k]])
                    else:
                        a = nxt("sq", 2)
                        tt("dve", gtmp[a][:], ps[pbk][:], sg[s_][:], ALU.mult, r=[t_ps[pbk], t_sg[s_]], w=[t_gtmp[a]])
                        if g == 1:
                            tt("dve", gacc[:, k, :], gacc[:, k, :], gtmp[a][:], ALU.add, r=[t_gtmp[a]], w=[t_gacc[k]])
                        else:
                            tt("dve", mixT[:, dc, qs], gacc[:, k, :], gtmp[a][:], ALU.add, r=[t_gtmp[a], t_gacc[k]],
                               w=[t_mix[dc][qd]])
            ring2.advance()

    P.barrier()
    x1T = sb("x1T", [128, 8, T], F32, 10240)
    t_x1 = [[Tk() for _ in range(4)] for _ in range(8)]
    for dc in range(8):
        dma_sp(x1T[:, dc, :], xT[dc * 128:(dc + 1) * 128, :], "x1_%d" % dc, w=t_x1[dc])
    W3_OFF = 153600
    NW3 = 5
    w3 = [sb("w3_%d" % i, [128, 4096], BF16, W3_OFF + 8192 * i) for i in range(NW3)]
    FB = [(0, 4), (4, 4), (8, 4), (12, 4), (16, 4), (20, 2)]
    jobs3 = []
    for hf in range(2):
        for (f0, nf) in FB:
            for wsrc in (w_fg, w_fu):
                jobs3.append([((lambda s, nf=nf: s[:, 0:8 * nf * 128].rearrange("p (c n) -> p c n", c=8)),
                               wsrc.rearrange("(c p) n -> p c n", p=128)[:, :, f0 * 128:(f0 + nf) * 128])])
        for dc in range(8):
            jobs3.append([((lambda s: s[:, 0:22 * 128].rearrange("p (c n) -> p c n", c=22)),
                           w_fd.rearrange("(c p) n -> p c n", p=128)[:, :, dc * 128:(dc + 1) * 128])])
    ring3 = Ring(P, w3, jobs3, "w3", tks=[ring2.tk[2], Tk(), Tk(), Tk(), Tk()])

    for cb in range(2):
        (wg, wb_), wtk = ring2.get(6 + cb)
        for dci in range(4):
            dc = cb * 4 + dci
            for qd in range(4):
                qs = slice(qd * 512, (qd + 1) * 512)
                pb = 4 + nxt("pb", 2)
                for c in range(8):
                    mm(ps[pb][:], wg[:, c, dci * 128:(dci + 1) * 128], mixT[:, c, qs], c == 0, c == 7,
                       r=[wtk, t_mix[c][qd]], w=[t_ps[pb]])
                tt("dve", x1T[:, dc, qs], ps[pb][:], x1T[:, dc, qs], ALU.add, r=[t_ps[pb]], w=[t_x1[dc][qd]])
        ring2.advance()

    if DEBUG:
        dma_sp(dbg_x1[:, :, :], x1T[:], "dbg2", r=[t for row in t_x1 for t in row])
    P.barrier()

    h2T = sb("h2T", [128, 8, T], BF16, 75776)
    actT = sb("actT", [128, 22, 1024], BF16, 108544)
    F_TMP = W3_OFF + 8192 * NW3
    sq3 = [sb("sq3_%d" % i, [128, 512], BF16, F_TMP + 1024 * i) for i in range(2)]
    std3 = sb("std3", [128, 512], F32, F_TMP + 2048)
    rstd3 = [sb("rstd3_%d" % i, [128, 512], F32, F_TMP + 4096 + 2048 * i) for i in range(2)]
    sil = [sb("sil%d" % i, [128, 512], F32, F_TMP + 8192 + 2048 * i) for i in range(2)]
    assert F_TMP + 12288 <= 212864
    t_sq3, t_std3, t_rstd3, t_sil = [Tk(), Tk()], Tk(), [Tk(), Tk()], [Tk(), Tk()]
    t_h2 = [[Tk() for _ in range(4)] for _ in range(8)]
    t_act = [[Tk() for _ in range(2)] for _ in range(22)]

    for qd in range(4):
        qs = slice(qd * 512, (qd + 1) * 512)
        for c in range(8):
            b = nxt("sq", 2)
            act(sq3[b][:], x1T[:, c, qs], AF.Square, r=[t_x1[c][qd]], w=[t_sq3[b]])
            mm(ps[6][:], ones1024[:], sq3[b][:], c == 0, c == 7, r=[t_sq3[b], t_const], w=[t_ps[6]])
        act(std3[:], ps[6][:], AF.Ln, r=[t_ps[6]], w=[t_std3], bias=EPS)
        b = nxt("rstd", 2)
        act(rstd3[b][:], std3[:], AF.Exp, r=[t_std3], w=[t_rstd3[b]], scale=-0.5)
        for c in range(8):
            eng = "dve"
            stt(eng, h2T[:, c, qs], x1T[:, c, qs], cols[:, 8 + c:9 + c], rstd3[b][:], ALU.mult, ALU.mult,
                r=[t_x1[c][qd], t_cols, t_rstd3[b]], w=[t_h2[c][qd]])

    t_out = Tk()
    out_evs = []
    jn = 0
    for hf in range(2):
        for (f0, nf) in FB:
            wgs, wgt = ring3.get(jn)
            wus, wut = ring3.get(jn + 1)
            jn += 2
            wgv = wgs[:, 0:8 * nf * 128].rearrange("p (c n) -> p c n", c=8)
            wuv = wus[:, 0:8 * nf * 128].rearrange("p (c n) -> p c n", c=8)
            for fi in range(nf):
                fc = f0 + fi
                for q2 in range(2):
                    qd = hf * 2 + q2
                    qs = slice(qd * 512, (qd + 1) * 512)
                    pg = nxt("pb", 2)
                    pu = 2 + pg
                    for c in range(8):
                        mm(ps[pg][:], wgv[:, c, fi * 128:(fi + 1) * 128], h2T[:, c, qs], c == 0, c == 7,
                           r=[wgt, t_h2[c][qd]], w=[t_ps[pg]])
                    for c in range(8):
                        mm(ps[pu][:], wuv[:, c, fi * 128:(fi + 1) * 128], h2T[:, c, qs], c == 0, c == 7,
                           r=[wut, t_h2[c][qd]], w=[t_ps[pu]])
                    s_ = nxt("St", 2)
                    act(sil[s_][:], ps[pg][:], AF.Silu, r=[t_ps[pg]], w=[t_sil[s_]])
                    tt("dve", actT[:, fc, q2 * 512:(q2 + 1) * 512], ps[pu][:], sil[s_][:], ALU.mult,
                       r=[t_ps[pu], t_sil[s_]], w=[t_act[fc][q2]])
            ring3.advance()
            ring3.advance()
        for dc in range(8):
            wds, wdt = ring3.get(jn)
            jn += 1
            wdv = wds[:, 0:22 * 128].rearrange("p (c n) -> p c n", c=22)
            for q2 in range(2):
                qd = hf * 2 + q2
                qs = slice(qd * 512, (qd + 1) * 512)
                pb = 4 + nxt("sbk", 2)
                for fc in range(22):
                    mm(ps[pb][:], wdv[:, fc, :], actT[:, fc, q2 * 512:(q2 + 1) * 512], fc == 0, fc == 21,
                       r=[wdt, t_act[fc][q2]], w=[t_ps[pb]])
                tt("dve", x1T[:, dc, qs], ps[pb][:], x1T[:, dc, qs], ALU.add, r=[t_ps[pb]], w=[t_x1[dc][qd]])
            ring3.advance()
            ev = dma_sp(yT[dc * 128:(dc + 1) * 128, hf * 1024:(hf + 1) * 1024], x1T[:, dc, hf * 1024:(hf + 1) * 1024],
                        "y%d" % (dc % 4), r=[t_x1[dc][hf * 2], t_x1[dc][hf * 2 + 1]])
            out_evs.append(ev)
    P.wait_only("sp", out_evs)
    P.emit()
    return nc


def _consts():
    p = np.arange(128, dtype=np.float32)[:, None]
    u = np.arange(3968, dtype=np.float32)[None, :]
    ra = np.abs(u - 1920.0 - p).astype(np.float32)
    j = np.arange(128, dtype=np.float32)[None, :]

    def tile(rel, ok):
        a = np.abs(rel)
        return np.where(ok & (a <= 64), a, BIG).astype(np.float32)

    allp = np.ones((128, 128), bool)
    Ft = tile(j - p, (p < 64) & allp)
    At = tile(j - p + 64, allp)
    Bt = tile(j - p - 64, allp)
    Lt = tile(j - p, (p >= 64) & allp)
    G2 = tile(j - p, allp)
    rb = np.concatenate([Ft, Bt, At, Bt, At, Lt, G2], axis=1)
    return np.ascontiguousarray(ra), np.ascontiguousarray(rb)


_NC_CACHE = {}


def kernel(x, mem, norm_mix, w_in, w_gate, b_gate, a_q_norm, a_k_norm, a_lambda_q1, a_lambda_k1,
           a_lambda_q2, a_lambda_k2, a_subln, b_q_norm, b_k_norm, mem_norm, w_mem_kv, c_q_norm,
           c_k_norm, w_branch, w_out, norm_ffn, w_ffn_gate, w_ffn_up, w_ffn_down):
    f = lambda a: np.ascontiguousarray(np.asarray(a, dtype=np.float32))
    x = np.asarray(x, dtype=np.float32)
    mem = np.asarray(mem, dtype=np.float32)
    ncore = x.shape[0]

    def c8(v):
        return np.asarray(v, np.float32).reshape(8, 128).T

    cols = np.zeros((128, 64), np.float32)
    cols[:, 0:8] = c8(norm_mix[0])
    cols[:, 8:16] = c8(norm_ffn[0])
    cols[:, 16:24] = c8(mem_norm[0])
    cols[:, 24:48] = np.asarray(b_gate[0], np.float32).reshape(24, 128).T
    cols[:, 48] = np.tile(np.asarray(a_q_norm[0], np.float32), 2)
    cols[:, 49] = np.tile(np.asarray(a_k_norm[0], np.float32), 2)
    cols[:, 50] = b_q_norm[0]
    cols[:, 51] = b_k_norm[0]
    cols[:, 52] = c_q_norm[0]
    cols[:, 53] = c_k_norm[0]
    cols[:, 54] = a_subln[0]
    lamv = np.broadcast_to(np.concatenate([np.asarray(v[0], np.float32) for v in
                                           (a_lambda_q1, a_lambda_k1, a_lambda_q2, a_lambda_k2)])[None, :], (128, 256))
    lamv = np.ascontiguousarray(lamv)
    ra, rb = _consts()
    shared = {
        "w_in": f(w_in[0]), "w_gate": f(w_gate[0]), "w_mem": f(w_mem_kv[0]),
        "w_br": f(np.asarray(w_branch[0]).reshape(1536, 1024)), "w_out": f(w_out[0]),
        "w_fg": f(w_ffn_gate[0]), "w_fu": f(w_ffn_up[0]), "w_fd": f(w_ffn_down[0]),
        "cols": cols, "lamv": lamv, "c_ra": ra, "c_rb": rb,
    }
    in_maps = []
    for b in range(ncore):
        m = dict(shared)
        m["xT"] = np.ascontiguousarray(x[b].T)
        m["memT"] = np.ascontiguousarray(mem[b].T)
        in_maps.append(m)
    if "nc" not in _NC_CACHE:
        _NC_CACHE["nc"] = build()
    res = run_bass_kernel_spmd(_NC_CACHE["nc"], in_maps, core_ids=list(range(ncore)))
    if DEBUG:
        _NC_CACHE["res"] = res
    out = np.stack([np.asarray(r["yT"]).T for r in res.results], axis=0)
    return np.ascontiguousarray(out.astype(np.float32))
```

```python
import contextlib
import numpy as np
import ml_dtypes
import concourse.bass as bass
import concourse.mybir as mybir
from concourse.bass_utils import run_bass_kernel_spmd

F32 = mybir.dt.float32
BF16 = mybir.dt.bfloat16
AF = mybir.ActivationFunctionType
ALU = mybir.AluOpType

SAME_ENGINE_SYNC = True
DEBUG = False
BASE = 16512
T = 2048
EPS = 1e-6
D_IN = 6656
LAMBDA_INIT = 0.2
BIG = 1.0e5
ALIBI_SKIP = 64.0


class Tk:
    __slots__ = ("w", "rs")

    def __init__(self):
        self.w = None
        self.rs = []


class Prog:
    ENG = ("pe", "act", "dve", "pool", "sp")

    def __init__(self, nc):
        self.nc = nc
        self.ops = {e: [] for e in self.ENG}
        self.dma_cnt = {}

    def _deps(self, r, w):
        deps = set()
        for t in r:
            if t.w is not None:
                deps.add(t.w)
        for t in w:
            if t.w is not None:
                deps.add(t.w)
            deps.update(t.rs)
        return deps

    def _commit(self, me, r, w):
        for t in r:
            t.rs.append(me)
        for t in w:
            t.w = me
            t.rs = []

    def _mark(self, deps, eng):
        for d in deps:
            if d[0] == "e":
                if d[1] == eng and (eng in ("pe", "sp") or not SAME_ENGINE_SYNC):
                    continue
                self.ops[d[1]][d[2]]["waited"] = True

    def op(self, eng, fn, r=(), w=()):
        deps = self._deps(r, w)
        me = ("e", eng, len(self.ops[eng]))
        self.ops[eng].append(dict(fn=fn, deps=deps, waited=False, dma=None))
        self._mark(deps, eng)
        self._commit(me, r, w)
        return me

    def dma(self, eng, fn, sem, r=(), w=()):
        deps = self._deps(r, w)
        n = self.dma_cnt.get(sem, 0) + 1
        self.dma_cnt[sem] = n
        me = ("d", sem, 16 * n)
        self.ops[eng].append(dict(fn=fn, deps=deps, waited=False, dma=sem))
        self._mark(deps, eng)
        self._commit(me, r, w)
        return me

    def wait_only(self, eng, deps):
        deps = set(deps)
        self.ops[eng].append(dict(fn=None, deps=deps, waited=False, dma=None))
        self._mark(deps, eng)

    def barrier(self, extra=()):
        evs = set(extra)
        for e in ("pe", "act", "dve", "pool"):
            for i in range(len(self.ops[e]) - 1, -1, -1):
                rec = self.ops[e][i]
                if rec["fn"] is not None and rec["dma"] is None:
                    evs.add(("e", e, i))
                    break
        for e in self.ENG:
            self.wait_only(e, evs)

    def emit(self):
        nc = self.nc
        for e in self.ENG:
            c = 0
            for rec in self.ops[e]:
                if rec["dma"] is None and rec["waited"] and rec["fn"] is not None:
                    c += 1
                rec["val"] = c
        with contextlib.ExitStack() as st:
            esem = {e: st.enter_context(nc.semaphore("s_" + e)) for e in self.ENG}
            dsem = {k: st.enter_context(nc.semaphore("d_%s" % (k,))) for k in self.dma_cnt}
            block = st.enter_context(nc.Block())
            ops = self.ops

            def run(e, engobj):
                waited = {}
                for rec in ops[e]:
                    need = {}
                    for d in rec["deps"]:
                        if d[0] == "e":
                            if d[1] == e and (e in ("pe", "sp") or not SAME_ENGINE_SYNC):
                                continue
                            key = ("e", d[1])
                            val = ops[d[1]][d[2]]["val"]
                            sem = esem[d[1]]
                        else:
                            key = ("d", d[1])
                            val = d[2]
                            sem = dsem[d[1]]
                        if waited.get(key, 0) >= val:
                            continue
                        if need.get(key, (None, 0))[1] < val:
                            need[key] = (sem, val)
                    for key, (sem, val) in need.items():
                        engobj.wait_ge(sem, val)
                        waited[key] = val
                    if rec["fn"] is None:
                        continue
                    ins = rec["fn"](engobj)
                    if rec["dma"] is not None:
                        ins.then_inc(dsem[rec["dma"]], 16)
                    elif rec["waited"]:
                        ins.then_inc(esem[e], 1)

            @block.tensor
            def _(eng):
                run("pe", eng)

            @block.scalar
            def _(eng):
                run("act", eng)

            @block.vector
            def _(eng):
                run("dve", eng)

            @block.gpsimd
            def _(eng):
                run("pool", eng)

            @block.sync
            def _(eng):
                run("sp", eng)


def interleave(gens):
    live = [[g, 0, max(1, n)] for g, n in gens]
    while live:
        live.sort(key=lambda x: x[1] / x[2])
        g = live[0]
        try:
            next(g[0])
            g[1] += 1
        except StopIteration:
            live.pop(0)


class Ring:
    def __init__(self, P, slots, jobs, name, tks=None, init=None):
        self.P, self.slots, self.jobs, self.name = P, slots, jobs, name
        self.tk = tks if tks is not None else [Tk() for _ in slots]
        self.issued = 0
        for _ in range(len(slots) if init is None else init):
            self.advance()

    def advance(self):
        j = self.issued
        if j >= len(self.jobs):
            return
        s = j % len(self.slots)
        for outf, in_ap in self.jobs[j]:
            out_ap = outf(self.slots[s])
            self.P.dma("pool", (lambda e, o=out_ap, i=in_ap: e.dma_start(out=o, in_=i)),
                       "%s%d" % (self.name, s), w=[self.tk[s]])
        self.issued += 1

    def get(self, j):
        assert j < self.issued, (self.name, j, self.issued)
        s = j % len(self.slots)
        return self.slots[s], self.tk[s]


def build():
    nc = bass.Bass("TRN2", target_bir_lowering=False)
    P = Prog(nc)

    def din(name, shape, dtype=F32):
        return nc.dram_tensor(name, shape, dtype, kind="ExternalInput").ap()

    xT = din("xT", [1024, T])
    memT = din("memT", [1024, 256])
    w_in = din("w_in", [1024, D_IN])
    w_gate = din("w_gate", [1024, 3072])
    w_mem = din("w_mem", [1024, 1024])
    w_br = din("w_br", [1536, 1024])
    w_out = din("w_out", [1024, 1024])
    w_fg = din("w_fg", [1024, 2816])
    w_fu = din("w_fu", [1024, 2816])
    w_fd = din("w_fd", [2816, 1024])
    cols_d = din("cols", [128, 64])
    lamv_d = din("lamv", [128, 256])
    ra_d = din("c_ra", [128, 3968])
    rb_d = din("c_rb", [128, 896])
    yT = nc.dram_tensor("yT", [1024, T], F32, kind="ExternalOutput").ap()
    if DEBUG:
        dbg_h = nc.dram_tensor("dbg_h", [128, 8, T], BF16, kind="ExternalOutput").ap()
        dbg_o = nc.dram_tensor("dbg_o", [128, 12, T], BF16, kind="ExternalOutput").ap()
        dbg_x1 = nc.dram_tensor("dbg_x1", [128, 8, T], F32, kind="ExternalOutput").ap()

    def sb(name, shape, dtype, off):
        return nc.alloc_sbuf_tensor_at(name, shape, dtype, offset=BASE + off)

    ps = [nc.alloc_psum_tensor("ps%d" % i, [128, 512], F32) for i in range(8)]
    t_ps = [Tk() for _ in range(8)]

    def mm(out, lhsT, rhs, start, stop, r, w):
        P.op("pe", lambda e: e.matmul(out, lhsT=lhsT, rhs=rhs, start=start, stop=stop), r=r, w=w)

    def act(out, in_, func, r, w, bias=None, scale=None):
        kw = {}
        if bias is not None:
            kw["bias"] = bias
        if scale is not None:
            kw["scale"] = scale
        P.op("act", lambda e: e.activation(out=out, in_=in_, func=func, **kw), r=r, w=w)

    def stt(eng, out, in0, scalar, in1, op0, op1, r, w):
        P.op(eng, lambda e: e.scalar_tensor_tensor(out=out, in0=in0, scalar=scalar, in1=in1, op0=op0, op1=op1),
             r=r, w=w)

    def tt(eng, out, in0, in1, op, r, w):
        P.op(eng, lambda e: e.tensor_tensor(out=out, in0=in0, in1=in1, op=op), r=r, w=w)

    def recip(out, in_, r, w):
        P.op("dve", lambda e: e.reciprocal(out=out, in_=in_), r=r, w=w)

    def recip_act(out, in_, r, w):
        act(out, in_, AF.Ln, r=r, w=w)
        act(out, out, AF.Exp, r=w, w=w, scale=-1.0)

    def tsmul(eng, out, in0, s, r, w):
        P.op(eng, lambda e: e.tensor_scalar_mul(out=out, in0=in0, scalar1=s), r=r, w=w)

    def dma_sp(out, in_, sem, r=(), w=()):
        return P.dma("sp", lambda e: e.dma_start(out=out, in_=in_), sem, r=r, w=w)

    ones128 = sb("ones128", [128, 128], BF16, 0)
    ones1024 = sb("ones1024", [128, 128], BF16, 256)
    onesblk = sb("onesblk", [128, 128], BF16, 512)
    ones1 = sb("ones1", [128, 128], BF16, 768)
    cols = sb("colsb", [128, 64], F32, 1024)
    lamv = sb("lamvb", [128, 256], F32, 1280)
    rb = sb("rbb", [128, 896], F32, 2304)
    ckT = sb("ckT", [128, 4, 256], BF16, 5888)
    cv = sb("cv", [128, 2, 512], BF16, 7936)
    t_const = Tk()
    t_cols = Tk()
    t_rb = Tk()
    t_ck = [Tk() for _ in range(4)]
    t_cv = [Tk() for _ in range(2)]

    P.op("pool", lambda e: e.memset(ones128[:], 1.0 / 128), w=[t_const])
    P.op("pool", lambda e: e.memset(ones1024[:], 1.0 / 1024), w=[t_const])
    P.op("pool", lambda e: e.memset(onesblk[:], 0.0), w=[t_const])
    P.op("pool", lambda e: e.memset(onesblk[0:64, 0:64], 1.0 / 64), w=[t_const])
    P.op("pool", lambda e: e.memset(onesblk[64:128, 64:128], 1.0 / 64), w=[t_const])
    P.op("pool", lambda e: e.memset(ones1[:], 1.0), w=[t_const])
    dma_sp(cols[:], cols_d[:, :], "c0", w=[t_cols])
    dma_sp(lamv[:], lamv_d[:, :], "c1", w=[t_cols])
    dma_sp(rb[:], rb_d[:, :], "c2", w=[t_rb])

    stt("dve", cols[:, 56:57], cols[:, 48:49], 64.0 ** -0.5, cols[:, 49:50], ALU.mult, ALU.mult, r=[t_cols], w=[t_cols])
    stt("dve", cols[:, 57:58], cols[:, 50:51], 128.0 ** -0.5, cols[:, 51:52], ALU.mult, ALU.mult, r=[t_cols], w=[t_cols])
    stt("dve", cols[:, 58:59], cols[:, 52:53], 128.0 ** -0.5, cols[:, 53:54], ALU.mult, ALU.mult, r=[t_cols], w=[t_cols])
    tsmul("dve", cols[:, 59:60], cols[:, 54:55], 1.0 - LAMBDA_INIT, r=[t_cols], w=[t_cols])
    tt("dve", lamv[:, 0:64], lamv[:, 0:64], lamv[:, 64:128], ALU.mult, r=[t_cols], w=[t_cols])
    tt("dve", lamv[:, 128:192], lamv[:, 128:192], lamv[:, 192:256], ALU.mult, r=[t_cols], w=[t_cols])
    P.op("dve", lambda e: e.reduce_sum(out=cols[:, 61:62], in_=lamv[:, 0:64], axis=mybir.AxisListType.X), r=[t_cols], w=[t_cols])
    P.op("dve", lambda e: e.reduce_sum(out=cols[:, 62:63], in_=lamv[:, 128:192], axis=mybir.AxisListType.X), r=[t_cols], w=[t_cols])
    act(cols[:, 61:63], cols[:, 61:63], AF.Exp, r=[t_cols], w=[t_cols])
    stt("dve", cols[:, 60:61], cols[:, 62:63], -LAMBDA_INIT, cols[:, 61:62], ALU.add, ALU.subtract, r=[t_cols], w=[t_cols])

    hT = sb("hT", [128, 8, T], BF16, 10240)
    outT = sb("outT", [128, 12, T], BF16, 43008)
    ra = sb("ra", [128, 3968], F32, 92160)
    W1_OFF = 108032
    NW1 = 10
    w1slots = [sb("w1_%d" % i, [128, 8, 128], BF16, W1_OFF + 2048 * i) for i in range(NW1)]
    U_OFF = W1_OFF + 2048 * NW1
    USZ = 13312
    uq = [sb("uq%d" % i, [128, T], BF16, U_OFF + USZ * i) for i in range(3)]
    uk = [sb("uk%d" % i, [128, T], BF16, U_OFF + USZ * i + 4096) for i in range(3)]
    uv = [sb("uv%d" % i, [128, 20, 128], BF16, U_OFF + USZ * i + 8192) for i in range(3)]
    ACC_OFF = U_OFF + 3 * USZ
    Oacc = sb("Oacc", [128, T], F32, ACC_OFF)
    Lacc = sb("Lacc", [128, T], F32, ACC_OFF + 8192)
    TMP = ACC_OFF + 16384
    sq = [sb("sq%d" % i, [128, 512], BF16, TMP + 1024 * i) for i in range(2)]
    rstd = [sb("rstd%d" % i, [128, 512], F32, TMP + 2048 + 2048 * i) for i in range(2)]
    std = sb("std", [128, 512], F32, TMP + 6144)
    St = [sb("St%d" % i, [128, 512], F32, TMP + 8192 + 2048 * i) for i in range(2)]
    Pt = [sb("Pt%d" % i, [128, 512], BF16, TMP + 12288 + 1024 * i) for i in range(3)]
    fin = [sb("fin%d" % i, [128, 512], F32, TMP + 15360 + 2048 * i) for i in range(3)]
    sqA = sb("sqA", [128, 512], BF16, TMP + 15360 + 2048)
    rawg = [sb("rawg%d" % i, [128, 512], F32, TMP + 21504 + 2048 * i) for i in range(2)]
    t_rawg = [Tk(), Tk()]
    St.append(sb("St2", [128, 512], F32, TMP + 25600))
    assert TMP + 25600 + 2048 <= 212864
    xs = [sb("xs%d" % i, [128, 8, 512], F32, U_OFF + 16384 * i) for i in range(2)]
    ms = sb("ms", [128, 8, 256], F32, U_OFF + 32768)
    mT = sb("mT", [128, 8, 256], BF16, U_OFF + 40960)

    t_h = [[Tk() for _ in range(4)] for _ in range(8)]
    t_o = [[Tk() for _ in range(4)] for _ in range(12)]
    t_ra = Tk()
    t_uq = [[Tk() for _ in range(4)] for _ in range(3)]
    t_uk = [[Tk() for _ in range(4)] for _ in range(3)]
    t_uv = [[Tk() for _ in range(5)] for _ in range(3)]
    t_Oacc, t_Lacc = Tk(), Tk()
    t_sq = [Tk(), Tk()]
    t_rstd = [Tk(), Tk()]
    t_std = Tk()
    t_St = [Tk(), Tk(), Tk()]
    t_Pt = [Tk(), Tk(), Tk()]
    t_fin = [Tk(), Tk(), Tk()]
    t_xs = [Tk(), Tk()]
    t_ms, t_mT = Tk(), [Tk() for _ in range(8)]

    rr = {"sq": 0, "rstd": 0, "St": 0, "Pt": 0, "pb": 0, "sbk": 0, "rawg": 0, "St3": 0}

    def nxt(key, n):
        v = rr[key]
        rr[key] = (v + 1) % n
        return v

    dma_sp(ra[:], ra_d[:, :], "c3", w=[t_ra])

    def wsl(w_ap, col0, ncol=128):
        src = w_ap.rearrange("(c p) n -> p c n", p=128)[:, :, col0:col0 + ncol]
        return [((lambda s: s[:]), src)]

    jobs1 = []
    for h in range(4):
        jobs1.append(wsl(w_mem, h * 128))
    for h in range(4):
        jobs1.append(wsl(w_mem, 512 + h * 128))
    units = []
    for h in range(4):
        units.append(("C", h, 0))
    for h in range(4):
        units.append(("A", h, 0))
    for h in range(4):
        for g in range(3):
            units.append(("B", h, g))
    ujob = {}
    for u in units:
        kind, h, g = u
        ujob[u] = len(jobs1)
        if kind == "C":
            jobs1.append(wsl(w_in, 6144 + h * 128))
        elif kind == "A":
            jobs1.append(wsl(w_in, 0 + h * 128))
            jobs1.append(wsl(w_in, 512 + h * 128))
            jobs1.append(wsl(w_in, 1024 + h * 128))
        else:
            jobs1.append(wsl(w_in, 1536 + g * 512 + h * 128))
            jobs1.append(wsl(w_in, 3072 + g * 512 + h * 128))
            jobs1.append(wsl(w_in, 4608 + g * 512 + h * 128))
    ring1 = Ring(P, w1slots, jobs1, "w1")

    def rms_in(src_d, stage, t_stage, sem, ncol, gcol0, dst_fn, t_dst_fn, col_lo):
        dma_sp(stage[:, :, 0:ncol], src_d.rearrange("(c p) t -> p c t", p=128)[:, :, col_lo:col_lo + ncol], sem, w=[t_stage])
        for c in range(8):
            b = nxt("sq", 2)
            act(sq[b][:, 0:ncol], stage[:, c, 0:ncol], AF.Square, r=[t_stage], w=[t_sq[b]])
            mm(ps[2][:, 0:ncol], ones1024[:], sq[b][:, 0:ncol], c == 0, c == 7, r=[t_sq[b], t_const], w=[t_ps[2]])
        act(std[:, 0:ncol], ps[2][:, 0:ncol], AF.Ln, r=[t_ps[2]], w=[t_std], bias=EPS)
        b = nxt("rstd", 2)
        act(rstd[b][:, 0:ncol], std[:, 0:ncol], AF.Exp, r=[t_std], w=[t_rstd[b]], scale=-0.5)
        for c in range(8):
            eng = "dve"
            stt(eng, dst_fn(c), stage[:, c, 0:ncol], cols[:, gcol0 + c:gcol0 + c + 1], rstd[b][:, 0:ncol],
                ALU.mult, ALU.mult, r=[t_stage, t_cols, t_rstd[b]], w=[t_dst_fn(c)])

    for qd in range(4):
        s = qd % 2
        rms_in(xT, xs[s], t_xs[s], "xs%d" % s, 512, 0,
               (lambda c, qd=qd: hT[:, c, qd * 512:(qd + 1) * 512]), (lambda c, qd=qd: t_h[c][qd]), qd * 512)
    rms_in(memT, ms, t_ms, "ms", 256, 16, (lambda c: mT[:, c, :]), (lambda c: t_mT[c]), 0)

    if DEBUG:
        dma_sp(dbg_h[:, :, :], hT[:], "dbg0", r=[t for row in t_h for t in row])

    SSB = 1
    FINAL_ENG = "pool"
    FINAL_ENG_D = "pool"

    def norm_sq(psb, ncol):
        b = nxt("sq", 2)
        act(sq[b][:, 0:ncol], ps[psb][:, 0:ncol], AF.Square, r=[t_ps[psb]], w=[t_sq[b]])
        return b

    def norm_rest(b, psb, ncol, ones_ap, gcol, out_ap, in_view, w_tk):
        mm(ps[SSB][:, 0:ncol], ones_ap, sq[b][:, 0:ncol], True, True, r=[t_sq[b], t_const], w=[t_ps[SSB]])
        act(std[:, 0:ncol], ps[SSB][:, 0:ncol], AF.Ln, r=[t_ps[SSB]], w=[t_std], bias=EPS)
        rb_ = nxt("rstd", 2)
        act(rstd[rb_][:, 0:ncol], std[:, 0:ncol], AF.Exp, r=[t_std], w=[t_rstd[rb_]], scale=-0.5)
        if gcol is None:
            tt("dve", out_ap, in_view(ps[psb][:, 0:ncol]), in_view(rstd[rb_][:, 0:ncol]), ALU.mult,
               r=[t_ps[psb], t_rstd[rb_]], w=w_tk)
        else:
            stt("dve", out_ap, in_view(ps[psb][:, 0:ncol]), gcol, in_view(rstd[rb_][:, 0:ncol]), ALU.mult, ALU.mult,
                r=[t_ps[psb], t_rstd[rb_], t_cols], w=w_tk)

    def norm_tail(psb, ncol, ones_ap, gcol, out_ap, in_view, r_extra, w_tk):
        b = norm_sq(psb, ncol)
        norm_rest(b, psb, ncol, ones_ap, gcol, out_ap, in_view, w_tk)

    def proj_qk(wj, dst, t_dst, ones_ap, gcol, Dil, rawbanks, ssb):
        wslot, wtk = ring1.get(wj)
        n = 512 // Dil

        def pv(a):
            return a if Dil == 1 else a.rearrange("p (i r) -> p r i", r=Dil)

        def cv_(a):
            return a if Dil == 1 else a.rearrange("p (r i) -> p r i", r=Dil)

        def MM(qd):
            pb = rawbanks[qd % len(rawbanks)]
            for c in range(8):
                mm(ps[pb][:], wslot[:, c, :], hT[:, c, qd * 512:(qd + 1) * 512], c == 0, c == 7,
                   r=[wtk, t_h[c][qd]], w=[t_ps[pb]])
            if qd == 3:
                ring1.advance()

        def EV(qd):
            pb = rawbanks[qd % len(rawbanks)]
            b = norm_sq(pb, 512)
            rg = nxt("rawg", 2)
            if gcol is None:
                P.op("dve", (lambda e, rg=rg: e.tensor_copy(out=cv_(rawg[rg][:]), in_=pv(ps[pb][:]))),
                     r=[t_ps[pb], t_sq[b]], w=[t_rawg[rg]])
            else:
                tsmul("dve", cv_(rawg[rg][:]), pv(ps[pb][:]), gcol, r=[t_ps[pb], t_cols, t_sq[b]], w=[t_rawg[rg]])
            return b, rg

        def REST(qd, b, rg):
            mm(ps[ssb][:], ones_ap, sq[b][:], True, True, r=[t_sq[b], t_const], w=[t_ps[ssb]])
            act(std[:], ps[ssb][:], AF.Ln, r=[t_ps[ssb]], w=[t_std], bias=EPS)
            rb_ = nxt("rstd", 2)
            act(cv_(rstd[rb_][:]), pv(std[:]), AF.Exp, r=[t_std], w=[t_rstd[rb_]], scale=-0.5)
            if Dil == 1:
                out_ap = dst[:, qd * 512:(qd + 1) * 512]
            else:
                out_ap = dst[:, :].rearrange("p (r i) -> p r i", r=Dil)[:, :, qd * n:(qd + 1) * n]
            tt(FINAL_ENG if Dil == 1 else FINAL_ENG_D, out_ap, cv_(rawg[rg][:]), cv_(rstd[rb_][:]), ALU.mult,
               r=[t_rawg[rg], t_rstd[rb_]], w=[t_dst[qd]])

        if len(rawbanks) == 1:
            for qd in range(4):
                MM(qd)
                yield
                b, rg = EV(qd)
                yield
                REST(qd, b, rg)
            yield
        else:
            MM(0)
            yield
            MM(1)
            yield
            e0 = EV(0)
            yield
            MM(2)
            yield
            e1 = EV(1)
            REST(0, *e0)
            yield
            MM(3)
            yield
            e2 = EV(2)
            REST(1, *e1)
            yield
            e3 = EV(3)
            REST(2, *e2)
            yield
            REST(3, *e3)
            yield

    def proj_v(wj, vbuf, t_vb, tiles, vbanks):
        wslot, wtk = ring1.get(wj)
        ngrp = (len(tiles) + 3) // 4

        def MMG(gi):
            g0 = gi * 4
            grp = tiles[g0:g0 + 4]
            vb = vbanks[gi % len(vbanks)]
            for j, (start, step) in enumerate(grp):
                hi = start + 127 * step
                qs = range(start // 512, hi // 512 + 1)
                for c in range(8):
                    mm(ps[vb][:, j * 128:(j + 1) * 128], hT[:, c, start:hi + 1:step], wslot[:, c, :], c == 0, c == 7,
                       r=[wtk] + [t_h[c][q] for q in qs], w=[t_ps[vb]])
            if gi == ngrp - 1:
                ring1.advance()

        def CP(gi):
            g0 = gi * 4
            n = len(tiles[g0:g0 + 4])
            vb = vbanks[gi % len(vbanks)]
            act(vbuf[:, g0:g0 + n, :], ps[vb][:, 0:n * 128].rearrange("p (a b) -> p a b", b=128), AF.Copy,
                r=[t_ps[vb]], w=[t_vb[gi]])

        if len(vbanks) == 1:
            for gi in range(ngrp):
                MMG(gi)
                yield
                CP(gi)
                yield
        else:
            MMG(0)
            yield
            for gi in range(1, ngrp):
                MMG(gi)
                CP(gi - 1)
                yield
            CP(ngrp - 1)
            yield

    def kv_unit():
        for h in range(4):
            wslot, wtk = ring1.get(h)
            for c in range(8):
                mm(ps[0][:, 0:256], wslot[:, c, :], mT[:, c, :], c == 0, c == 7, r=[wtk, t_mT[c]], w=[t_ps[0]])
            ring1.advance()
            norm_tail(0, 256, ones128[:], None, ckT[:, h, :], (lambda a: a), [], [t_ck[h]])
        for kt in range(2):
            for h in range(4):
                wslot, wtk = ring1.get(4 + h)
                for c in range(8):
                    mm(ps[3][:, h * 128:(h + 1) * 128], mT[:, c, kt * 128:(kt + 1) * 128], wslot[:, c, :], c == 0, c == 7,
                       r=[wtk, t_mT[c]], w=[t_ps[3]])
                if kt == 1:
                    ring1.advance()
            act(cv[:, kt, :], ps[3][:], AF.Copy, r=[t_ps[3]], w=[t_cv[kt]])

    kv_unit()
    P.barrier()

    SLOPE_A = [2.0 ** (-2.0 * (h + 1)) for h in range(4)]
    SLOPE_B = [[2.0 ** (-8.0 * (4 * g + h + 1) / 12.0) for h in range(4)] for g in range(3)]
    DILS = [1, 4, 16]

    def b_geom(g):
        Dil = DILS[g]
        Ls = T // Dil
        nq = Ls // 128
        ntile = nq + 1 if nq > 1 else 1

        def ws(ct):
            if nq == 1 or ct == 0:
                return 0
            if ct == nq:
                return Ls - 128
            return 128 * ct - 64
        return Dil, Ls, nq, ntile, ws

    def gen_proj(u, slot, mode):
        kind, h, g = u
        j0 = ujob[u]
        rawbanks, ssb = ([0], 1) if mode == "A" else ([0, 1], 2)
        vbanks = rawbanks
        if kind == "C":
            yield from proj_qk(j0, uq[slot], t_uq[slot], ones128[:], cols[:, 58:59], 1, rawbanks, ssb)
        elif kind == "A":
            yield from proj_qk(j0, uq[slot], t_uq[slot], onesblk[:], cols[:, 56:57], 1, rawbanks, ssb)
            yield from proj_qk(j0 + 1, uk[slot], t_uk[slot], onesblk[:], None, 1, rawbanks, ssb)
            yield from proj_v(j0 + 2, uv[slot], t_uv[slot], [(kt * 128, 1) for kt in range(16)], vbanks)
        else:
            Dil, Ls, nq, ntile, ws = b_geom(g)
            yield from proj_qk(j0, uq[slot], t_uq[slot], ones128[:], cols[:, 57:58], Dil, rawbanks, ssb)
            yield from proj_qk(j0 + 1, uk[slot], t_uk[slot], ones128[:], None, Dil, rawbanks, ssb)
            tiles = [(Dil * ws(ct) + r, Dil) for r in range(Dil) for ct in range(ntile)]
            yield from proj_v(j0 + 2, uv[slot], t_uv[slot], tiles, vbanks)

    def nproj(u):
        kind, h, g = u
        if kind == "C":
            return 9
        if kind == "A":
            return 9 + 9 + 8
        _, _, _, ntile, _ = b_geom(g)
        return 18 + 1 + ((DILS[g] * ntile + 3) // 4)

    LA = 2
    t_s3 = [Tk(), Tk()]

    def gen_attn_C(h, slot):
        q, tq = uq[slot], t_uq[slot]
        tiles = [(qd, kt) for qd in range(4) for kt in range(2)]
        deferred = []

        def S(i):
            qd, kt = tiles[i]
            sbk = 2 + i % 2
            mm(ps[sbk][:], ckT[:, h, kt * 128:(kt + 1) * 128], q[:, qd * 512:(qd + 1) * 512], True, True,
               r=[t_ck[h], tq[qd]], w=[t_ps[sbk]])

        def fin_c(qd, ob, lb):
            def f():
                recip_act(fin[0][:], ps[lb][:], r=[t_ps[lb]], w=[t_fin[0]])
                tt("dve", outT[:, 8 + h, qd * 512:(qd + 1) * 512], ps[ob][:], fin[0][:], ALU.mult,
                   r=[t_ps[ob], t_fin[0]], w=[t_o[8 + h][qd]])
            return f

        for i in range(LA):
            S(i)
        for j, (qd, kt) in enumerate(tiles):
            sbk = 2 + j % 2
            ob, lb = 4 + qd % 2, 6 + qd % 2
            pj = nxt("Pt", 3)
            act(Pt[pj][:], ps[sbk][:], AF.Exp, r=[t_ps[sbk]], w=[t_Pt[pj]])
            if j + LA < len(tiles):
                S(j + LA)
            mm(ps[ob][:], cv[:, kt, h * 128:(h + 1) * 128], Pt[pj][:], kt == 0, kt == 1, r=[t_cv[kt], t_Pt[pj]], w=[t_ps[ob]])
            mm(ps[lb][:], ones1[:], Pt[pj][:], kt == 0, kt == 1, r=[t_const, t_Pt[pj]], w=[t_ps[lb]])
            if kt == 0 and deferred:
                deferred.pop(0)()
            if kt == 1:
                deferred.append(fin_c(qd, ob, lb))
            yield
        while deferred:
            deferred.pop(0)()
        yield

    def gen_attn_A(h, slot):
        q, tq, k, tk_, v, tv = uq[slot], t_uq[slot], uk[slot], t_uk[slot], uv[slot], t_uv[slot]
        def kts_for(qd):
            out = []
            for kt in range(16):
                gap = max(qd * 512 - (kt * 128 + 127), kt * 128 - (qd * 512 + 511), 0)
                if SLOPE_A[h] * gap < ALIBI_SKIP:
                    out.append(kt)
            return out

        tiles = []
        for qd in range(4):
            for cmp in range(2):
                ks = kts_for(qd)
                for idx, kt in enumerate(ks):
                    tiles.append((qd, cmp, kt, idx, len(ks)))
        deferred = []

        def S(i):
            qd, cmp, kt = tiles[i][:3]
            lo, hi = cmp * 64, cmp * 64 + 64
            sbk = 2 + i % 2
            mm(ps[sbk][:], k[lo:hi, kt * 128:(kt + 1) * 128], q[lo:hi, qd * 512:(qd + 1) * 512], True, True,
               r=[tk_[kt // 4], tq[qd]], w=[t_ps[sbk]])

        def fin_a(qd, cmp, ob, lb):
            st = {}

            def f1():
                act(fin[0][:], ps[lb][:], AF.Ln, r=[t_ps[lb]], w=[t_fin[0]])
                fo = 1 if cmp == 0 else 2
                P.op("dve", (lambda e: e.tensor_copy(out=fin[fo][:], in_=ps[ob][:])), r=[t_ps[ob]], w=[t_fin[fo]])
                act(fin[0][:], fin[0][:], AF.Exp, r=[t_fin[0]], w=[t_fin[0]], scale=-1.0)
                tt("pool", fin[fo][:], fin[fo][:], fin[0][:], ALU.mult, r=[t_fin[0]], w=[t_fin[fo]])

            def f2():
                stt("dve", fin[2][:], fin[2][:], cols[:, 60:61], fin[1][:], ALU.mult, ALU.add,
                    r=[t_fin[1], t_cols], w=[t_fin[2]])
                act(sqA[:], fin[2][:], AF.Square, r=[t_fin[2]], w=[t_fin[1]])

            def f3():
                mm(ps[SSB][:], ones128[:], sqA[:], True, True, r=[t_fin[1], t_const], w=[t_ps[SSB]])
                act(std[:], ps[SSB][:], AF.Ln, r=[t_ps[SSB]], w=[t_std], bias=EPS)
                act(fin[0][:], std[:], AF.Exp, r=[t_std], w=[t_fin[0]], scale=-0.5)

            def f4():
                stt("dve", outT[:, h, qd * 512:(qd + 1) * 512], fin[2][:], cols[:, 59:60], fin[0][:], ALU.mult, ALU.mult,
                    r=[t_fin[2], t_cols, t_fin[0]], w=[t_o[h][qd]])
            return [f1] if cmp == 0 else [f1, f2, f3, f4]

        for i in range(LA):
            S(i)
        for j, (qd, cmp, kt, idx, ng) in enumerate(tiles):
            sbk = 2 + j % 2
            gi = qd * 2 + cmp
            ob, lb = 4 + gi % 2, 6 + gi % 2
            off = qd * 512 - kt * 128 + 1920
            si = nxt("St3", 3)
            stt("dve", St[si][:], ra[:, off:off + 512], -SLOPE_A[h], ps[sbk][:], ALU.mult, ALU.add,
                r=[t_ra, t_ps[sbk]], w=[t_St[si]])
            pj = nxt("Pt", 3)
            act(Pt[pj][:], St[si][:], AF.Exp, r=[t_St[si]], w=[t_Pt[pj]])
            if j + LA < len(tiles):
                S(j + LA)
            mm(ps[ob][:], v[:, kt, :], Pt[pj][:], idx == 0, idx == ng - 1, r=[tv[kt // 4], t_Pt[pj]], w=[t_ps[ob]])
            mm(ps[lb][:], ones1[:], Pt[pj][:], idx == 0, idx == ng - 1, r=[t_const, t_Pt[pj]], w=[t_ps[lb]])
            pops = (2, 6, 9, 12) if ng >= 14 else ((2, 5, 8, 10) if ng >= 12 else (1, 2, 4, 5))
            if idx in pops and deferred:
                deferred.pop(0)()
            if idx == ng - 1:
                assert len(deferred) == 0, "previous finalize not fully emitted"
                deferred.extend(fin_a(qd, cmp, ob, lb))
            yield
        while deferred:
            deferred.pop(0)()
            yield

    def gen_attn_B(h, g, slot):
        sbanks = [3, 6, 7]
        LB = 3
        q, tq, k, tk_, v, tv = uq[slot], t_uq[slot], uk[slot], t_uk[slot], uv[slot], t_uv[slot]
        Dil, Ls, nq, ntile, ws = b_geom(g)
        mult = -SLOPE_B[g][h] * Dil
        qtiles = [(r, m) for r in range(Dil) for m in range(nq)]
        deferred = []

        def geom(i):
            r, m = qtiles[i]
            if nq == 1:
                return r, m, [0], 768
            return r, m, [m, m + 1], (0 if m == 0 else (512 if m == nq - 1 else 256))

        def S(i):
            r, m, subs, boff = geom(i)
            sbk = sbanks[i % LB]
            qap = q[:, r * Ls + m * 128:r * Ls + (m + 1) * 128]
            for si_, ct in enumerate(subs):
                k0 = r * Ls + ws(ct)
                mm(ps[sbk][:, si_ * 128:(si_ + 1) * 128], k[:, k0:k0 + 128], qap, True, True,
                   r=tk_ + tq, w=[t_ps[sbk]])

        def accum(i1, olb):
            r0, m0 = qtiles[i1 - 1]

            def dview(a):
                if nq == 1:
                    return a[:, :].rearrange("p (e r) -> p r e", r=16)[:, r0:r0 + 2, :]
                return a[:, :].rearrange("p (m e r) -> p r m e", r=Dil, e=128)[:, r0, m0:m0 + 2, :]

            def sview(c):
                return ps[olb][:].rearrange("p (t c e) -> p t c e", t=2, c=2)[:, :, c, :]

            def f():
                if g == 0:
                    act(dview(Oacc), sview(0), AF.Copy, r=[t_ps[olb]], w=[t_Oacc])
                    act(dview(Lacc), sview(1), AF.Copy, r=[t_ps[olb]], w=[t_Lacc])
                else:
                    tt("dve", dview(Oacc), sview(0), dview(Oacc), ALU.add, r=[t_ps[olb]], w=[t_Oacc])
                    tt("dve", dview(Lacc), sview(1), dview(Lacc), ALU.add, r=[t_ps[olb]], w=[t_Lacc])
            return f

        for i in range(LB):
            S(i)
        for i in range(16):
            r, m, subs, boff = geom(i)
            n = 128 * len(subs)
            sbk = sbanks[i % LB]
            j = i % 2
            bi = i // 2
            olb = 4 + bi % 2
            si = nxt("St3", 3)
            stt("dve", St[si][:, 0:n], rb[:, boff:boff + n], mult, ps[sbk][:, 0:n], ALU.mult, ALU.add,
                r=[t_rb, t_ps[sbk]], w=[t_St[si]])
            pj = nxt("Pt", 3)
            act(Pt[pj][:, 0:n], St[si][:, 0:n], AF.Exp, r=[t_St[si]], w=[t_Pt[pj]])
            if i + LB < 16:
                S(i + LB)
            for si_, ct in enumerate(subs):
                vt = r * ntile + ct
                mm(ps[olb][:, j * 256:j * 256 + 128], v[:, vt, :], Pt[pj][:, si_ * 128:(si_ + 1) * 128],
                   si_ == 0, si_ == len(subs) - 1, r=[tv[vt // 4], t_Pt[pj]], w=[t_ps[olb]])
            for si_, ct in enumerate(subs):
                mm(ps[olb][:, j * 256 + 128:j * 256 + 256], ones1[:], Pt[pj][:, si_ * 128:(si_ + 1) * 128],
                   si_ == 0, si_ == len(subs) - 1, r=[t_const, t_Pt[pj]], w=[t_ps[olb]])
            if j == 1:
                if deferred:
                    deferred.pop(0)()
                deferred.append(accum(i, olb))
            yield
        while deferred:
            deferred.pop(0)()
        if g == 2:
            for qd in range(4):
                recip_act(Lacc[:, qd * 512:(qd + 1) * 512], Lacc[:, qd * 512:(qd + 1) * 512], r=[t_Lacc], w=[t_Lacc])
            for qd in range(4):
                tt("pool", outT[:, 4 + h, qd * 512:(qd + 1) * 512], Oacc[:, qd * 512:(qd + 1) * 512],
                   Lacc[:, qd * 512:(qd + 1) * 512], ALU.mult, r=[t_Oacc, t_Lacc], w=[t_o[4 + h][qd]])
        yield

    def gen_attn(u, slot):
        kind, h, g = u
        if kind == "C":
            return gen_attn_C(h, slot), 9
        if kind == "A":
            return gen_attn_A(h, slot), 130 if h >= 2 else (110 if h == 1 else 60)
        return gen_attn_B(h, g, slot), 17

    mixT = sb("mixT", [128, 8, T], BF16, 104448)
    W2_OFFS = [92160, 137216, 149504]
    w2g = [sb("w2g%d" % i, [128, 8, 512], BF16, W2_OFFS[i]) for i in range(3)]
    w2b = [sb("w2b%d" % i, [128, 4, 512], BF16, W2_OFFS[i] + 8192) for i in range(3)]
    gacc = sb("gacc", [128, 16, 512], F32, 161792)
    sg = [sb("sg%d" % i, [128, 512], F32, 194560 + 2048 * i) for i in range(2)]
    gtmp = [sb("gtmp%d" % i, [128, 512], F32, 198656 + 2048 * i) for i in range(2)]
    t_sg, t_gtmp = [Tk(), Tk()], [Tk(), Tk()]
    t_gacc = [Tk() for _ in range(16)]
    t_mix = [[Tk() for _ in range(4)] for _ in range(8)]

    jobs2 = []
    for cb in range(2):
        for g in range(3):
            jobs2.append([
                ((lambda s: s[0][:]),
                 w_gate.rearrange("(c p) n -> p c n", p=128)[:, :, g * 1024 + cb * 512:g * 1024 + (cb + 1) * 512]),
                ((lambda s: s[1][:]),
                 w_br[g * 512:(g + 1) * 512, :].rearrange("(c p) n -> p c n", p=128)[:, :, cb * 512:(cb + 1) * 512]),
            ])
    for cb in range(2):
        jobs2.append([((lambda s: s[0][:]),
                       w_out.rearrange("(c p) n -> p c n", p=128)[:, :, cb * 512:(cb + 1) * 512])])
    ring2_box = []
    prev = None
    last_mode = None
    for i, u in enumerate(units + [None]):
        mode = "A" if (prev is not None and prev[0][0] in ("A", "C")) else "B"
        if last_mode is not None and mode != last_mode:
            P.barrier()
        last_mode = mode
        gens = []
        if u is not None:
            gens.append((gen_proj(u, i % 3, mode), nproj(u)))
        if prev is not None:
            gens.append(gen_attn(prev[0], prev[1]))
        interleave(gens)
        if prev is not None and prev[0] == ("A", 3, 0):
            ring2_box.append(Ring(P, list(zip(w2g, w2b)), jobs2, "w2", tks=[t_ra, Tk(), Tk()], init=1))
        prev = (u, i % 3) if u is not None else None

    if DEBUG:
        dma_sp(dbg_o[:, :, :], outT[:], "dbg1", r=[t for row in t_o for t in row])

    ring2 = ring2_box[0]
    P.barrier()
    ring2.advance()
    ring2.advance()

    jn = 0
    for cb in range(2):
        for g in range(3):
            (wg, wb_), wtk = ring2.get(jn)
            jn += 1
            for dci in range(4):
                dc = cb * 4 + dci
                for qd in range(4):
                    qs = slice(qd * 512, (qd + 1) * 512)
                    k = dci * 4 + qd
                    pg = nxt("pb", 2)
                    pbk = 2 + pg
                    for c in range(8):
                        mm(ps[pg][:], wg[:, c, dci * 128:(dci + 1) * 128], hT[:, c, qs], c == 0, c == 7,
                           r=[wtk, t_h[c][qd]], w=[t_ps[pg]])
                    for hh in range(4):
                        mm(ps[pbk][:], wb_[:, hh, dci * 128:(dci + 1) * 128], outT[:, g * 4 + hh, qs], hh == 0, hh == 3,
                           r=[wtk, t_o[g * 4 + hh][qd]], w=[t_ps[pbk]])
                    s_ = nxt("St", 2)
                    act(sg[s_][:], ps[pg][:], AF.Sigmoid, r=[t_ps[pg], t_cols], w=[t_sg[s_]],
                        bias=cols[:, 24 + g * 8 + dc:24 + g * 8 + dc + 1])
                    if g == 0:
                        tt("dve", gacc[:, k, :], ps[pbk][:], sg[s_][:], ALU.mult, r=[t_ps[pbk], t_sg[s_]], w=[t_gacc[k]])
                    else:
                        a = nxt("sq", 2)
                        tt("dve", gtmp[a][:], ps[pbk][:], sg[s_][:], ALU.mult, r=[t_ps[pbk], t_sg[s_]], w=[t_gtmp[a]])
                        if g == 1:
                            tt("dve", gacc[:, k, :], gacc[:, k, :], gtmp[a][:], ALU.add, r=[t_gtmp[a]], w=[t_gacc[k]])
                        else:
                            tt("dve", mixT[:, dc, qs], gacc[:, k, :], gtmp[a][:], ALU.add, r=[t_gtmp[a], t_gacc[k]],
                               w=[t_mix[dc][qd]])
            ring2.advance()

    P.barrier()
    x1T = sb("x1T", [128, 8, T], F32, 10240)
    t_x1 = [[Tk() for _ in range(4)] for _ in range(8)]
    for dc in range(8):
        dma_sp(x1T[:, dc, :], xT[dc * 128:(dc + 1) * 128, :], "x1_%d" % dc, w=t_x1[dc])
    W3_OFF = 153600
    NW3 = 5
    w3 = [sb("w3_%d" % i, [128, 4096], BF16, W3_OFF + 8192 * i) for i in range(NW3)]
    FB = [(0, 4), (4, 4), (8, 4), (12, 4), (16, 4), (20, 2)]
    jobs3 = []
    for hf in range(2):
        for (f0, nf) in FB:
            for wsrc in (w_fg, w_fu):
                jobs3.append([((lambda s, nf=nf: s[:, 0:8 * nf * 128].rearrange("p (c n) -> p c n", c=8)),
                               wsrc.rearrange("(c p) n -> p c n", p=128)[:, :, f0 * 128:(f0 + nf) * 128])])
        for dc in range(8):
            jobs3.append([((lambda s: s[:, 0:22 * 128].rearrange("p (c n) -> p c n", c=22)),
                           w_fd.rearrange("(c p) n -> p c n", p=128)[:, :, dc * 128:(dc + 1) * 128])])
    ring3 = Ring(P, w3, jobs3, "w3", tks=[ring2.tk[2], Tk(), Tk(), Tk(), Tk()])

    for cb in range(2):
        (wg, wb_), wtk = ring2.get(6 + cb)
        for dci in range(4):
            dc = cb * 4 + dci
            for qd in range(4):
                qs = slice(qd * 512, (qd + 1) * 512)
                pb = 4 + nxt("pb", 2)
                for c in range(8):
                    mm(ps[pb][:], wg[:, c, dci * 128:(dci + 1) * 128], mixT[:, c, qs], c == 0, c == 7,
                       r=[wtk, t_mix[c][qd]], w=[t_ps[pb]])
                tt("dve", x1T[:, dc, qs], ps[pb][:], x1T[:, dc, qs], ALU.add, r=[t_ps[pb]], w=[t_x1[dc][qd]])
        ring2.advance()

    if DEBUG:
        dma_sp(dbg_x1[:, :, :], x1T[:], "dbg2", r=[t for row in t_x1 for t in row])
    P.barrier()

    h2T = sb("h2T", [128, 8, T], BF16, 75776)
    actT = sb("actT", [128, 22, 1024], BF16, 108544)
    F_TMP = W3_OFF + 8192 * NW3
    sq3 = [sb("sq3_%d" % i, [128, 512], BF16, F_TMP + 1024 * i) for i in range(2)]
    std3 = sb("std3", [128, 512], F32, F_TMP + 2048)
    rstd3 = [sb("rstd3_%d" % i, [128, 512], F32, F_TMP + 4096 + 2048 * i) for i in range(2)]
    sil = [sb("sil%d" % i, [128, 512], F32, F_TMP + 8192 + 2048 * i) for i in range(2)]
    assert F_TMP + 12288 <= 212864
    t_sq3, t_std3, t_rstd3, t_sil = [Tk(), Tk()], Tk(), [Tk(), Tk()], [Tk(), Tk()]
    t_h2 = [[Tk() for _ in range(4)] for _ in range(8)]
    t_act = [[Tk() for _ in range(2)] for _ in range(22)]

    for qd in range(4):
        qs = slice(qd * 512, (qd + 1) * 512)
        for c in range(8):
            b = nxt("sq", 2)
            act(sq3[b][:], x1T[:, c, qs], AF.Square, r=[t_x1[c][qd]], w=[t_sq3[b]])
            mm(ps[6][:], ones1024[:], sq3[b][:], c == 0, c == 7, r=[t_sq3[b], t_const], w=[t_ps[6]])
        act(std3[:], ps[6][:], AF.Ln, r=[t_ps[6]], w=[t_std3], bias=EPS)
        b = nxt("rstd", 2)
        act(rstd3[b][:], std3[:], AF.Exp, r=[t_std3], w=[t_rstd3[b]], scale=-0.5)
        for c in range(8):
            eng = "dve"
            stt(eng, h2T[:, c, qs], x1T[:, c, qs], cols[:, 8 + c:9 + c], rstd3[b][:], ALU.mult, ALU.mult,
                r=[t_x1[c][qd], t_cols, t_rstd3[b]], w=[t_h2[c][qd]])

    t_out = Tk()
    out_evs = []
    jn = 0
    for hf in range(2):
        for (f0, nf) in FB:
            wgs, wgt = ring3.get(jn)
            wus, wut = ring3.get(jn + 1)
            jn += 2
            wgv = wgs[:, 0:8 * nf * 128].rearrange("p (c n) -> p c n", c=8)
            wuv = wus[:, 0:8 * nf * 128].rearrange("p (c n) -> p c n", c=8)
            for fi in range(nf):
                fc = f0 + fi
                for q2 in range(2):
                    qd = hf * 2 + q2
                    qs = slice(qd * 512, (qd + 1) * 512)
                    pg = nxt("pb", 2)
                    pu = 2 + pg
                    for c in range(8):
                        mm(ps[pg][:], wgv[:, c, fi * 128:(fi + 1) * 128], h2T[:, c, qs], c == 0, c == 7,
                           r=[wgt, t_h2[c][qd]], w=[t_ps[pg]])
                    for c in range(8):
                        mm(ps[pu][:], wuv[:, c, fi * 128:(fi + 1) * 128], h2T[:, c, qs], c == 0, c == 7,
                           r=[wut, t_h2[c][qd]], w=[t_ps[pu]])
                    s_ = nxt("St", 2)
                    act(sil[s_][:], ps[pg][:], AF.Silu, r=[t_ps[pg]], w=[t_sil[s_]])
                    tt("dve", actT[:, fc, q2 * 512:(q2 + 1) * 512], ps[pu][:], sil[s_][:], ALU.mult,
                       r=[t_ps[pu], t_sil[s_]], w=[t_act[fc][q2]])
            ring3.advance()
            ring3.advance()
        for dc in range(8):
            wds, wdt = ring3.get(jn)
            jn += 1
            wdv = wds[:, 0:22 * 128].rearrange("p (c n) -> p c n", c=22)
            for q2 in range(2):
                qd = hf * 2 + q2
                qs = slice(qd * 512, (qd + 1) * 512)
                pb = 4 + nxt("sbk", 2)
                for fc in range(22):
                    mm(ps[pb][:], wdv[:, fc, :], actT[:, fc, q2 * 512:(q2 + 1) * 512], fc == 0, fc == 21,
                       r=[wdt, t_act[fc][q2]], w=[t_ps[pb]])
                tt("dve", x1T[:, dc, qs], ps[pb][:], x1T[:, dc, qs], ALU.add, r=[t_ps[pb]], w=[t_x1[dc][qd]])
            ring3.advance()
            ev = dma_sp(yT[dc * 128:(dc + 1) * 128, hf * 1024:(hf + 1) * 1024], x1T[:, dc, hf * 1024:(hf + 1) * 1024],
                        "y%d" % (dc % 4), r=[t_x1[dc][hf * 2], t_x1[dc][hf * 2 + 1]])
            out_evs.append(ev)
    P.wait_only("sp", out_evs)
    P.emit()
    return nc


def _consts():
    p = np.arange(128, dtype=np.float32)[:, None]
    u = np.arange(3968, dtype=np.float32)[None, :]
    ra = np.abs(u - 1920.0 - p).astype(np.float32)
    j = np.arange(128, dtype=np.float32)[None, :]

    def tile(rel, ok):
        a = np.abs(rel)
        return np.where(ok & (a <= 64), a, BIG).astype(np.float32)

    allp = np.ones((128, 128), bool)
    Ft = tile(j - p, (p < 64) & allp)
    At = tile(j - p + 64, allp)
    Bt = tile(j - p - 64, allp)
    Lt = tile(j - p, (p >= 64) & allp)
    G2 = tile(j - p, allp)
    rb = np.concatenate([Ft, Bt, At, Bt, At, Lt, G2], axis=1)
    return np.ascontiguousarray(ra), np.ascontiguousarray(rb)


_NC_CACHE = {}


def kernel(x, mem, norm_mix, w_in, w_gate, b_gate, a_q_norm, a_k_norm, a_lambda_q1, a_lambda_k1,
           a_lambda_q2, a_lambda_k2, a_subln, b_q_norm, b_k_norm, mem_norm, w_mem_kv, c_q_norm,
           c_k_norm, w_branch, w_out, norm_ffn, w_ffn_gate, w_ffn_up, w_ffn_down):
    f = lambda a: np.ascontiguousarray(np.asarray(a, dtype=np.float32))
    x = np.asarray(x, dtype=np.float32)
    mem = np.asarray(mem, dtype=np.float32)
    ncore = x.shape[0]

    def c8(v):
        return np.asarray(v, np.float32).reshape(8, 128).T

    cols = np.zeros((128, 64), np.float32)
    cols[:, 0:8] = c8(norm_mix[0])
    cols[:, 8:16] = c8(norm_ffn[0])
    cols[:, 16:24] = c8(mem_norm[0])
    cols[:, 24:48] = np.asarray(b_gate[0], np.float32).reshape(24, 128).T
    cols[:, 48] = np.tile(np.asarray(a_q_norm[0], np.float32), 2)
    cols[:, 49] = np.tile(np.asarray(a_k_norm[0], np.float32), 2)
    cols[:, 50] = b_q_norm[0]
    cols[:, 51] = b_k_norm[0]
    cols[:, 52] = c_q_norm[0]
    cols[:, 53] = c_k_norm[0]
    cols[:, 54] = a_subln[0]
    lamv = np.broadcast_to(np.concatenate([np.asarray(v[0], np.float32) for v in
                                           (a_lambda_q1, a_lambda_k1, a_lambda_q2, a_lambda_k2)])[None, :], (128, 256))
    lamv = np.ascontiguousarray(lamv)
    ra, rb = _consts()
    shared = {
        "w_in": f(w_in[0]), "w_gate": f(w_gate[0]), "w_mem": f(w_mem_kv[0]),
        "w_br": f(np.asarray(w_branch[0]).reshape(1536, 1024)), "w_out": f(w_out[0]),
        "w_fg": f(w_ffn_gate[0]), "w_fu": f(w_ffn_up[0]), "w_fd": f(w_ffn_down[0]),
        "cols": cols, "lamv": lamv, "c_ra": ra, "c_rb": rb,
    }
    in_maps = []
    for b in range(ncore):
        m = dict(shared)
        m["xT"] = np.ascontiguousarray(x[b].T)
        m["memT"] = np.ascontiguousarray(mem[b].T)
        in_maps.append(m)
    if "nc" not in _NC_CACHE:
        _NC_CACHE["nc"] = build()
    res = run_bass_kernel_spmd(_NC_CACHE["nc"], in_maps, core_ids=list(range(ncore)))
    if DEBUG:
        _NC_CACHE["res"] = res
    out = np.stack([np.asarray(r["yT"]).T for r in res.results], axis=0)
    return np.ascontiguousarray(out.astype(np.float32))
```
